# Optimizing a Trainium2 kernel written in Bass

```python
import math
import jax, jax.numpy as jnp
from jax import lax
import numpy as np

D_MODEL = 1024
BATCH = 32
SEQ = 2048
DEPTH = 1

N_MEM = 256
MIX_WIDTH = D_MODEL
ATTN_WIDTH = MIX_WIDTH // 2
HEAD_DIM_ATTN = 64
N_ATTN_HEADS = ATTN_WIDTH // HEAD_DIM_ATTN
DILATED_BRANCHES = ((128, 1), (512, 4), (2048, 16))
BAND_BLOCK = 128
MLSTM_WIDTH = MIX_WIDTH - ATTN_WIDTH
N_MLSTM_HEADS = 4
HEAD_DIM_MLSTM = MLSTM_WIDTH // N_MLSTM_HEADS
MLSTM_CHUNK = 64
MLSTM_CONV = 4
IN_COLS = 3 * ATTN_WIDTH + 4 * MLSTM_WIDTH + 2 * N_MLSTM_HEADS
N_XATTN_HEADS = 4
HEAD_DIM_XATTN = D_MODEL // N_XATTN_HEADS
D_FF = ((8 * D_MODEL // 3 + 127) // 128) * 128
FFN_CONV = 3
RMS_EPS = 1e-6
NEG_INF = -1e30

kernel_name = 'hybrid_dilated_attn_mlstm_parallel_block'


def _rmsnorm(x, g):
    xf = x.astype(jnp.float32)
    y = xf * lax.rsqrt(jnp.mean(xf * xf, axis=-1, keepdims=True) + RMS_EPS)
    return (y * g.astype(jnp.float32)).astype(x.dtype)


def _causal_dwconv(x, w, b):
    K = w.shape[0]
    S = x.shape[1]
    xp = jnp.pad(x, ((0, 0), (K - 1, 0), (0, 0)))
    out = b
    for j in range(K):
        out = out + w[j] * xp[:, K - 1 - j:K - 1 - j + S]
    return out


def _alibi_slopes(n_heads):
    h = jnp.arange(1, n_heads + 1, dtype=jnp.float32)
    return jnp.exp2(-8.0 * h / n_heads)


def _dilated_branch(q, k, v, window, dilation, slopes):
    B, S, H, dh = q.shape
    span = window // dilation
    U = S // dilation
    nb = -(-U // BAND_BLOCK)
    Up = nb * BAND_BLOCK

    def to_sub(t):
        t = t.reshape(B, U, dilation, H, dh).transpose(0, 2, 3, 1, 4)
        return jnp.pad(t, ((0, 0), (0, 0), (0, 0), (0, Up - U), (0, 0)))

    def band(t):
        tp = jnp.pad(t, ((0, 0), (0, 0), (0, 0), (BAND_BLOCK, 0), (0, 0)))
        tp = tp.reshape(B, dilation, H, nb + 1, BAND_BLOCK, dh)
        return jnp.concatenate([tp[:, :, :, :-1], tp[:, :, :, 1:]], axis=4)

    qb = to_sub(q).reshape(B, dilation, H, nb, BAND_BLOCK, dh)
    kb = band(to_sub(k))
    vb = band(to_sub(v))
    s = jnp.einsum('brhnqd,brhnkd->brhnqk', qb, kb, preferred_element_type=jnp.float32)

    qi = jnp.arange(BAND_BLOCK)[:, None]
    kj = jnp.arange(2 * BAND_BLOCK)[None, :]
    rel = BAND_BLOCK + qi - kj
    in_band = (rel >= 0) & (rel <= span)
    blk = jnp.arange(nb)[:, None, None]
    valid = in_band[None] & ((blk > 0) | (kj[None] >= BAND_BLOCK))
    dist = (rel * dilation).astype(jnp.float32)
    bias = -slopes[:, None, None, None] * dist[None, None]
    s = jnp.where(valid, s + bias, NEG_INF)

    m = jnp.max(s, axis=-1, keepdims=True)
    p = jnp.exp(s - m)
    den = jnp.sum(p, axis=-1)
    o = jnp.einsum('brhnqk,brhnkd->brhnqd', p, vb.astype(jnp.float32)) / den[..., None]
    lse = m[..., 0] + jnp.log(den)

    o = o.reshape(B, dilation, H, Up, dh)[:, :, :, :U].transpose(0, 3, 1, 2, 4).reshape(B, S, H, dh)
    lse = lse.reshape(B, dilation, H, Up)[:, :, :, :U].transpose(0, 3, 1, 2).reshape(B, S, H)
    return o, lse


def _mlstm_chunkwise(q, k, v, log_i, log_f):
    B, S, H, dh = q.shape
    L = MLSTM_CHUNK
    nc = S // L

    def chunks(t):
        return t.reshape(B, nc, L, H, dh).transpose(0, 3, 1, 2, 4)

    def gchunks(t):
        return t.reshape(B, nc, L, H).transpose(0, 3, 1, 2)

    qc = chunks(q)
    kc = chunks(k) * (dh ** -0.5)
    vc = chunks(v)
    li = gchunks(log_i)
    b = jnp.cumsum(gchunks(log_f), axis=-1)
    g = b[..., -1]

    a = g[..., None] - b + li
    m_loc = jnp.max(a, axis=-1)
    wa = jnp.exp(a - m_loc[..., None])
    kv = jnp.einsum('bhcl,bhcld,bhcle->bhcde', wa, kc, vc)
    ksum = jnp.einsum('bhcl,bhcld->bhcd', wa, kc)

    def step(carry, inp):
        C, n, m = carry
        g_c, mloc_c, kv_c, ks_c = inp
        m_new = jnp.maximum(g_c + m, mloc_c)
        dec = jnp.exp(g_c + m - m_new)
        inj = jnp.exp(mloc_c - m_new)
        C_new = dec[..., None, None] * C + inj[..., None, None] * kv_c
        n_new = dec[..., None] * n + inj[..., None] * ks_c
        return (C_new, n_new, m_new), (C, n, m)

    init = (jnp.zeros((B, H, dh, dh), jnp.float32),
            jnp.zeros((B, H, dh), jnp.float32),
            jnp.zeros((B, H), jnp.float32))
    _, (C_prev, n_prev, m_prev) = lax.scan(
        step, init,
        (g.transpose(2, 0, 1), m_loc.transpose(2, 0, 1),
         kv.transpose(2, 0, 1, 3, 4), ksum.transpose(2, 0, 1, 3)))
    C_prev = C_prev.transpose(1, 2, 0, 3, 4)
    n_prev = n_prev.transpose(1, 2, 0, 3)
    m_prev = m_prev.transpose(1, 2, 0)

    causal = jnp.tril(jnp.ones((L, L), dtype=bool))
    D = jnp.where(causal, b[..., :, None] - b[..., None, :] + li[..., None, :], NEG_INF)
    inter_log = b + m_prev[..., None]
    m_t = jnp.maximum(jnp.max(D, axis=-1), inter_log)
    Sm = jnp.exp(D - m_t[..., None]) * jnp.einsum('bhctd,bhcsd->bhcts', qc, kc)
    inter_w = jnp.exp(inter_log - m_t)
    num = (inter_w[..., None] * jnp.einsum('bhctd,bhcde->bhcte', qc, C_prev)
           + jnp.einsum('bhcts,bhcse->bhcte', Sm, vc))
    den = inter_w * jnp.einsum('bhctd,bhcd->bhct', qc, n_prev) + jnp.sum(Sm, axis=-1)
    h = num / jnp.maximum(jnp.abs(den), jnp.exp(-m_t))[..., None]
    return h.transpose(0, 2, 3, 1, 4).reshape(B, S, H, dh)


def _parallel_mixer(xn, w_in, conv_w, conv_b, gate_b, head_g, w_out):
    B, S, _ = xn.shape
    A, M, Hm = ATTN_WIDTH, MLSTM_WIDTH, N_MLSTM_HEADS
    proj = xn @ w_in
    aq, ak, av, mqk, mv, mo, mgate = jnp.split(
        proj, [A, 2 * A, 3 * A, 3 * A + 2 * M, 3 * A + 3 * M, 3 * A + 4 * M], axis=-1)

    ashp = (B, S, N_ATTN_HEADS, HEAD_DIM_ATTN)
    aq = aq.reshape(ashp) * (HEAD_DIM_ATTN ** -0.5)
    ak = ak.reshape(ashp)
    av = av.reshape(ashp)
    slopes = _alibi_slopes(N_ATTN_HEADS)
    outs, lses = [], []
    for window, dil in DILATED_BRANCHES:
        o, lse = _dilated_branch(aq, ak, av, window, dil, slopes)
        outs.append(o)
        lses.append(lse)
    wts = jax.nn.softmax(jnp.stack(lses, axis=0), axis=0)
    attn = jnp.einsum('ibsh,ibshd->bshd', wts, jnp.stack(outs, axis=0))
    attn = attn.reshape(B, S, A).astype(xn.dtype)

    mqk = jax.nn.silu(_causal_dwconv(mqk, conv_w, conv_b))
    mq, mk = jnp.split(mqk, 2, axis=-1)
    mshp = (B, S, Hm, HEAD_DIM_MLSTM)
    pre = mgate.astype(jnp.float32) + gate_b.astype(jnp.float32)
    log_i = pre[..., :Hm]
    log_f = jax.nn.log_sigmoid(pre[..., Hm:])
    hm = _mlstm_chunkwise(mq.reshape(mshp).astype(jnp.float32),
                          mk.reshape(mshp).astype(jnp.float32),
                          mv.reshape(mshp).astype(jnp.float32), log_i, log_f)
    hm = hm * lax.rsqrt(jnp.mean(hm * hm, axis=-1, keepdims=True) + RMS_EPS)
    hm = hm * head_g.astype(jnp.float32).reshape(Hm, HEAD_DIM_MLSTM)
    hm = (hm.reshape(B, S, M) * jax.nn.sigmoid(mo.astype(jnp.float32))).astype(xn.dtype)

    return jnp.concatenate([attn, hm], axis=-1) @ w_out


def _memory_xattn(hn, memn, w_q, w_kv, w_o):
    B, S, _ = hn.shape
    q = (hn @ w_q).reshape(B, S, N_XATTN_HEADS, HEAD_DIM_XATTN) * (HEAD_DIM_XATTN ** -0.5)
    kv = (memn @ w_kv).reshape(B, memn.shape[1], 2, N_XATTN_HEADS, HEAD_DIM_XATTN)
    s = jnp.einsum('bshd,bmhd->bhsm', q, kv[:, :, 0], preferred_element_type=jnp.float32)
    p = jax.nn.softmax(s, axis=-1)
    o = jnp.einsum('bhsm,bmhd->bshd', p, kv[:, :, 1].astype(jnp.float32))
    return o.reshape(B, S, D_MODEL).astype(hn.dtype) @ w_o


def _conv_ffn(hn, w_up, conv_w, conv_b, w_down):
    gate, up = jnp.split(hn @ w_up, 2, axis=-1)
    gate = _causal_dwconv(gate, conv_w, conv_b)
    return (jax.nn.silu(gate) * up) @ w_down


def setup_inputs(seed: int = 0) -> dict:
    key = jax.random.key(seed)
    ks = jax.random.split(key, 24)
    f32 = jnp.float32
    L = DEPTH

    def nrm(k, shape, fan_in):
        return jax.random.normal(k, shape, f32) * (fan_in ** -0.5)

    def gain(k, shape):
        return 1.0 + 0.02 * jax.random.normal(k, shape, f32)

    def small(k, shape):
        return 0.02 * jax.random.normal(k, shape, f32)

    i_bias = 0.1 * jax.random.normal(ks[5], (L, N_MLSTM_HEADS), f32)
    f_bias = (jnp.linspace(3.0, 6.0, N_MLSTM_HEADS, dtype=f32)[None]
              + 0.1 * jax.random.normal(ks[6], (L, N_MLSTM_HEADS), f32))
    return {
        'x': jax.random.normal(ks[0], (BATCH, SEQ, D_MODEL), f32),
        'mem': jax.random.normal(ks[1], (BATCH, N_MEM, D_MODEL), f32),
        'norm_mix_g': gain(ks[2], (L, D_MODEL)),
        'w_in': nrm(ks[3], (L, D_MODEL, IN_COLS), D_MODEL),
        'mlstm_conv_w': nrm(ks[4], (L, MLSTM_CONV, 2 * MLSTM_WIDTH), MLSTM_CONV),
        'mlstm_conv_b': small(ks[7], (L, 2 * MLSTM_WIDTH)),
        'mlstm_gate_b': jnp.concatenate([i_bias, f_bias], axis=-1),
        'mlstm_head_g': gain(ks[8], (L, MLSTM_WIDTH)),
        'w_mix_out': nrm(ks[9], (L, MIX_WIDTH, D_MODEL), MIX_WIDTH),
        'norm_xattn_g': gain(ks[10], (L, D_MODEL)),
        'norm_mem_g': gain(ks[11], (L, D_MODEL)),
        'w_xq': nrm(ks[12], (L, D_MODEL, D_MODEL), D_MODEL),
        'w_xkv': nrm(ks[13], (L, D_MODEL, 2 * D_MODEL), D_MODEL),
        'w_xo': nrm(ks[14], (L, D_MODEL, D_MODEL), D_MODEL),
        'norm_ffn_g': gain(ks[15], (L, D_MODEL)),
        'w_ffn_up': nrm(ks[16], (L, D_MODEL, 2 * D_FF), D_MODEL),
        'ffn_conv_w': nrm(ks[17], (L, FFN_CONV, D_FF), FFN_CONV),
        'ffn_conv_b': small(ks[18], (L, D_FF)),
        'w_ffn_down': nrm(ks[19], (L, D_FF, D_MODEL), D_FF),
        'norm_final_g': gain(ks[20], (D_MODEL,)),
    }


def reference(x, mem, norm_mix_g, w_in, mlstm_conv_w, mlstm_conv_b, mlstm_gate_b,
              mlstm_head_g, w_mix_out, norm_xattn_g, norm_mem_g, w_xq, w_xkv, w_xo,
              norm_ffn_g, w_ffn_up, ffn_conv_w, ffn_conv_b, w_ffn_down, norm_final_g):
    h = x
    for l in range(DEPTH):
        h = h + _parallel_mixer(_rmsnorm(h, norm_mix_g[l]), w_in[l], mlstm_conv_w[l],
                                mlstm_conv_b[l], mlstm_gate_b[l], mlstm_head_g[l],
                                w_mix_out[l])
        h = h + _memory_xattn(_rmsnorm(h, norm_xattn_g[l]), _rmsnorm(mem, norm_mem_g[l]),
                              w_xq[l], w_xkv[l], w_xo[l])
        h = h + _conv_ffn(_rmsnorm(h, norm_ffn_g[l]), w_ffn_up[l], ffn_conv_w[l],
                          ffn_conv_b[l], w_ffn_down[l])
    return _rmsnorm(h, norm_final_g)
```

```python
import contextlib
import math
import numpy as np
import concourse.bass as bass
import concourse.mybir as mybir
from concourse.bass_utils import run_bass_kernel_spmd

F32 = mybir.dt.float32
BF16 = mybir.dt.bfloat16
AF = mybir.ActivationFunctionType
ALU = mybir.AluOpType

D = 1024
S = 2048
NT = S // 128
NMEM = 256
A_W = 512
M_W = 512
IN_COLS = 3592
DFF = 2816
NFF = DFF // 128
EPS = 1e-6
N_CORES = 8
DIL = (1, 4, 16)


class Buf:
    __slots__ = ("name", "w", "r", "excl")

    def __init__(self, name, excl=False):
        self.name = name
        self.w = None
        self.r = []
        self.excl = excl


class Eng:
    def __init__(self, name, is_pe=False):
        self.name = name
        self.ops = []
        self.sem = None
        self.cnt = 0
        self.seen = {}
        self.is_pe = is_pe
        self.log = []


class Sched:
    def __init__(self, nc, es, nds=12):
        self.nc = nc
        self.pe = Eng("pe", True)
        self.dve = Eng("dve")
        self.act = Eng("act")
        self.pool = Eng("pool")
        self.sp = Eng("sp")
        self.engs = [self.pe, self.dve, self.act, self.pool, self.sp]
        for e in self.engs:
            e.sem = es.enter_context(nc.semaphore("sem_" + e.name))
        self.dsems = {}
        for q in (self.sp, self.pool, self.act):
            self.dsems[q.name] = [[es.enter_context(nc.semaphore("dma_%s%d" % (q.name, i))), 0] for i in range(nds if q is self.sp else 6)]
        self.dnext = {k: 0 for k in self.dsems}

    def all_sems(self):
        out = [e.sem for e in self.engs]
        for lst in self.dsems.values():
            out += [x[0] for x in lst]
        return out

    def _wait(self, eng, ev):
        sem, val = ev
        if eng.is_pe and sem is eng.sem:
            return
        if eng.seen.get(id(sem), 0) >= val:
            return
        eng.seen[id(sem)] = val
        eng.ops.append(lambda e, sem=sem, val=val: e.wait_ge(sem, val))
        eng.log.append(("w", id(sem), val))

    def _deps(self, eng, reads, writes):
        for b in reads:
            if b.w is not None:
                self._wait(eng, b.w)
            if b.excl:
                for ev in b.r:
                    if ev[0] is not eng.sem:
                        self._wait(eng, ev)
        for b in writes:
            if b.w is not None and (b.w[0] is not eng.sem or b in reads):
                self._wait(eng, b.w)
            for ev in b.r:
                if ev[0] is eng.sem:
                    continue
                self._wait(eng, ev)

    def op(self, eng, fn, reads=(), writes=(), inc=True):
        self._deps(eng, reads, writes)
        sem = eng.sem
        if inc:
            eng.cnt += 1
            ev = (sem, eng.cnt)
            eng.ops.append(lambda e, fn=fn, sem=sem: fn(e).then_inc(sem, 1))
            eng.log.append(("i", id(sem), 1))
        else:
            ev = (sem, eng.cnt + 1)
            eng.ops.append(lambda e, fn=fn: fn(e))
        for b in reads:
            b.r.append(ev)
        for b in writes:
            b.w = ev
            b.r = []
        return ev

    def dma(self, q, out_ap, in_ap, reads=(), writes=(), slow=False):
        self._deps(q, reads, writes)
        lst = self.dsems[q.name]
        i = self.dnext[q.name]
        self.dnext[q.name] = (i + 1) % len(lst)
        sem, prev = lst[i]
        if prev > 0:
            self._wait(q, (sem, prev))
        lst[i][1] = prev + 16
        ev = (sem, prev + 16)
        if slow:
            q.ops.append(lambda e, o=out_ap, a=in_ap, sem=sem: e.dma_start(out=o, in_=a, allow_slow_non_contiguous=True).then_inc(sem, 16))
        else:
            q.ops.append(lambda e, o=out_ap, a=in_ap, sem=sem: e.dma_start(out=o, in_=a).then_inc(sem, 16))
        q.log.append(("i", id(sem), 16))
        for b in reads:
            b.r.append(ev)
        for b in writes:
            b.w = ev
            b.r = []
        return ev

    def check_deadlock(self):
        vals = {}
        pos = {e.name: 0 for e in self.engs}
        progress = True
        while progress:
            progress = False
            for e in self.engs:
                while pos[e.name] < len(e.log):
                    kind, sid, v = e.log[pos[e.name]]
                    if kind == "w":
                        if vals.get(sid, 0) < v:
                            break
                    else:
                        vals[sid] = vals.get(sid, 0) + v
                    pos[e.name] += 1
                    progress = True
        stuck = {e.name: (pos[e.name], len(e.log)) for e in self.engs if pos[e.name] < len(e.log)}
        return stuck

    def barrier(self):
        evs = [(e.sem, e.cnt) for e in self.engs if e.cnt > 0]
        for lst in self.dsems.values():
            evs += [(x[0], x[1]) for x in lst if x[1] > 0]
        for e in self.engs:
            for ev in evs:
                self._wait(e, ev)

    def final_wait(self, eng, bufs):
        for b in bufs:
            if b.w is not None:
                self._wait(eng, b.w)

    def emit(self, blk):
        def run(eng):
            def f(e):
                for o in eng.ops:
                    o(e)
            return f
        blk.tensor(run(self.pe))
        blk.vector(run(self.dve))
        blk.scalar(run(self.act))
        blk.gpsimd(run(self.pool))
        blk.sync(run(self.sp))


def build(nseq, stop_after=None, dbg=False):
    nc = bass.Bass("TRN2", target_bir_lowering=False)
    dt_in = {}

    def din(name, shape):
        t = nc.dram_tensor(name, list(shape), F32, kind="ExternalInput").ap()
        dt_in[name] = t
        return t

    x = din("x", (nseq, S, D))
    mem = din("mem", (nseq, NMEM, D))
    g_mix = din("norm_mix_g", (D,))
    w_in = din("w_in", (D, IN_COLS))
    mconv_w = din("mlstm_conv_w", (4, 2 * M_W))
    mconv_b = din("mlstm_conv_b", (2 * M_W,))
    gate_b = din("mlstm_gate_b", (8,))
    head_g = din("mlstm_head_g", (M_W,))
    w_mo = din("w_mix_out", (D, D))
    g_xa = din("norm_xattn_g", (D,))
    g_mem = din("norm_mem_g", (D,))
    w_xq = din("w_xq", (D, D))
    w_xkv = din("w_xkv", (D, 2 * D))
    w_xo = din("w_xo", (D, D))
    g_ffn = din("norm_ffn_g", (D,))
    w_up = din("w_ffn_up", (D, 2 * DFF))
    fconv_w = din("ffn_conv_w", (3, DFF))
    fconv_b = din("ffn_conv_b", (DFF,))
    w_dn = din("w_ffn_down", (DFF, D))
    g_fin = din("norm_final_g", (D,))
    c_ident = din("c_ident", (128, 128))
    c_E = din("c_E", (128, 8 * 3 * 256))
    c_cmask = din("c_cmask", (128, 128))
    c_rmask = din("c_rmask", (4, S))
    c_onehot = din("c_onehot", (4, 4 * 128))
    y = nc.dram_tensor("y", [nseq, S, D], F32, kind="ExternalOutput").ap()
    dbg_t = None
    if dbg:
        dbg_t = nc.dram_tensor("dbg", [D, S], F32, kind="ExternalOutput").ap()

    with contextlib.ExitStack() as es:
        sc = Sched(nc, es)
        PE, DVE, ACT, POOL, SP = sc.pe, sc.dve, sc.act, sc.pool, sc.sp

        def sb(name, shape, dt):
            return es.enter_context(nc.sbuf_tensor(name, list(shape), dt))

        banks = [es.enter_context(nc.psum_tensor("bank%d" % i, [128, 512], F32)) for i in range(8)]
        bank_b = [Buf("bank%d" % i, excl=True) for i in range(8)]
        bank_i = [0]

        def bank():
            i = bank_i[0]
            bank_i[0] = (i + 1) % 7
            return banks[i], bank_b[i]

        def bank_long():
            return banks[7], bank_b[7]

        ident = sb("ident", (128, 128), BF16)
        identf = sb("identf", (128, 128), F32)
        cmask = sb("cmask", (128, 128), F32)
        onehot = sb("onehot", (4, 512), F32)
        onesf = sb("onesf", (128, 128), F32)
        gb_b = Buf("gb1")
        uniq = [0]
        hgbc = sb("hgbc", (128, M_W), F32)
        cwm = sb("cwm", (128, 8, 4), F32)
        cbm = sb("cbm", (128, 8), F32)
        cwf = sb("cwf", (128, NFF, 3), F32)
        cbf = sb("cbf", (128, NFF), F32)
        gbi = sb("gbi", (4, 1), F32)
        gbf = sb("gbf", (4, 1), F32)
        ngbf = sb("ngbf", (4, 1), F32)
        CONST = Buf("const")
        GBFB = Buf("gbf")
        sc.dma(POOL, ident[:], c_ident, writes=[Buf("c")])
        sc.dma(SP, identf[:], c_ident, writes=[Buf("c")])
        sc.dma(SP, cmask[:], c_cmask, writes=[Buf("c")])
        sc.dma(SP, onehot[:], c_onehot, writes=[Buf("c")])
        sc.dma(SP, hgbc[:], head_g.partition_broadcast(128), writes=[Buf("c")])
        for j in range(4):
            sc.dma(SP, cwm[:, :, j], mconv_w[j].rearrange("(b p) -> p b", p=128), writes=[Buf("c")], slow=True)
        sc.dma(SP, cbm[:], mconv_b.rearrange("(b p) -> p b", p=128), writes=[Buf("c")], slow=True)
        for j in range(3):
            sc.dma(SP, cwf[:, :, j], fconv_w[j].rearrange("(b p) -> p b", p=128), writes=[Buf("c")], slow=True)
        sc.dma(SP, cbf[:], fconv_b.rearrange("(b p) -> p b", p=128), writes=[Buf("c")], slow=True)
        sc.dma(SP, gbi[:], gate_b[0:4].rearrange("(p o) -> p o", o=1), writes=[Buf("c")], slow=True)
        sc.dma(SP, gbf[:], gate_b[4:8].rearrange("(p o) -> p o", o=1), writes=[GBFB], slow=True)
        sc.op(DVE, lambda e: e.memset(onesf[:], 1.0), writes=[CONST])
        sc.op(DVE, lambda e: e.tensor_scalar(out=ngbf[:], in0=gbf[:], scalar1=-1.0, scalar2=None, op0=ALU.mult), reads=[GBFB], writes=[CONST])

        xnT = sb("xnT", (128, 8, S), BF16)
        xnT_b = [Buf("xnT%d" % t) for t in range(NT)]
        catT = sb("catT", (128, 8, S), BF16)
        catT_b = [Buf("catT%d" % c) for c in range(8)]

        w_in_v = w_in.rearrange("(kc p) n -> p kc n", p=128)

        def norm_T(name, srcs, gi, dstT, dst_bufs, nt, scr):
            for _ in norm_gen(name, srcs, gi, dstT, dst_bufs, nt, scr):
                pass

        def norm_gen(name, srcs, gi, dstT, dst_bufs, nt, scr):
            _j, ss, sq, rstd, _x, xnb_b = scr
            uniq[0] += 1
            esN = contextlib.ExitStack()
            gb1 = esN.enter_context(nc.sbuf_tensor("gb1_%d" % uniq[0], [128, D], F32))
            junk = esN.enter_context(nc.sbuf_tensor("junk_%d" % uniq[0], [128, D], BF16))
            xnb = [esN.enter_context(nc.sbuf_tensor("xnb%d_%d" % (i, uniq[0]), [128, D], BF16)) for i in range(2)]
            sc.dma(SP, gb1[:], gi.partition_broadcast(128), writes=[gb_b])
            def stage1(t):
                src, sbufs = srcs(t)
                sc.op(ACT, lambda e: e.activation(out=junk[:], in_=src, func=AF.Square, accum_out=ss[:, t:t + 1]),
                      reads=sbufs, writes=[scr_b])
                sc.op(ACT, lambda e: e.activation(out=sq[:, t:t + 1], in_=ss[:, t:t + 1], func=AF.Sqrt, scale=1.0 / D, bias=eps_t[:]),
                      reads=[scr_b, CONST], writes=[scr_b])
                sc.op(DVE, lambda e: e.reciprocal(out=rstd[:, t:t + 1], in_=sq[:, t:t + 1]), reads=[scr_b], writes=[scr_b2])
                k = t % 2
                sc.op(DVE, lambda e: e.scalar_tensor_tensor(out=xnb[k][:], in0=src, scalar=rstd[:, t:t + 1], in1=gb1[:], op0=ALU.mult, op1=ALU.mult),
                      reads=list(sbufs) + [scr_b2, gb_b], writes=[xnb_b[k]])

            def stage2(t):
                k = t % 2
                bk, bb = bank()
                bkv = bk[:].bitcast(BF16)
                for kc in range(8):
                    sc.op(PE, lambda e, kc=kc: e.transpose(out=bkv[:, kc * 128:(kc + 1) * 128], in_=xnb[k][:, kc * 128:(kc + 1) * 128], identity=ident[:]),
                          reads=[xnb_b[k], CONST], writes=[bb], inc=(kc == 7))
                sc.op(ACT, lambda e: e.activation(out=dstT[:, :, t * 128:(t + 1) * 128], in_=bkv[:, 0:1024].rearrange("p (k c) -> p k c", k=8), func=AF.Copy),
                      reads=[bb], writes=[dst_bufs[t]])

            for i in range(nt + 1):
                if i < nt:
                    stage1(i)
                if i >= 1:
                    stage2(i - 1)
                if i < nt:
                    yield i
            esN.close()

        eps_t = sb("eps_t", (128, 1), F32)
        zero_t = sb("zero_t", (128, 1), F32)
        sc.op(DVE, lambda e: e.memset(zero_t[:], 0.0), writes=[CONST])
        sc.op(DVE, lambda e: e.memset(eps_t[:], EPS), writes=[CONST])
        ss_t = sb("ss_t", (128, NT), F32)
        sq_t = sb("sq_t", (128, NT), F32)
        rstd_t = sb("rstd_t", (128, NT), F32)
        xnb_b = [Buf("xnb%d" % i) for i in range(2)]
        scr_b = Buf("scr")
        scr_b2 = Buf("scr2")
        nscr = (None, ss_t, sq_t, rstd_t, None, xnb_b)

        outs_final = []

        for si in range(nseq):
            with contextlib.ExitStack() as esB:
                sc.barrier()
                def sbB(name, shape, dt):
                    return esB.enter_context(nc.sbuf_tensor("%s_s%d" % (name, si), list(shape), dt))
                wA = [sbB("wA%d" % i, (128, 8, 384), BF16) for i in range(2)]
                wA_b = [[Buf("wA%d_%d" % (i, o)) for o in range(3)] for i in range(2)]
                Ehp = [sbB("Ehp%d" % i, (128, 6 * 256), BF16) for i in range(2)]
                Ehp_b = [Buf("Ehp%d" % i) for i in range(2)]
                natT = sbB("natT", (128, 4, S), BF16)
                nat_b = [[Buf("nat%d_%d" % (o, c)) for c in range(4)] for o in range(4)]
                permTs = [sbB("permT%d" % i, (128, 4, S), BF16) for i in range(2)]
                perm_bs = [[[Buf("perm%d_%d_%d" % (i, o, c)) for c in range(4)] for o in range(4)] for i in range(2)]
                sc.op(DVE, lambda e: e.memset(natT[64:128, 0, :], 0.0), writes=nat_b[0])
                sc.op(DVE, lambda e: e.memset(natT[0:64, 1, :], 0.0), writes=nat_b[1])
                for i in range(2):
                    sc.op(DVE, lambda e, i=i: e.memset(permTs[i][64:128, 0, :], 0.0), writes=perm_bs[i][0])
                    sc.op(DVE, lambda e, i=i: e.memset(permTs[i][0:64, 1, :], 0.0), writes=perm_bs[i][1])
                Vtok = [sbB("Vtok%d" % i, (128, 16, 192), BF16) for i in range(2)]
                Vtok_b = [Buf("Vtok%d" % i) for i in range(2)]
                acc2 = sbB("acc2", (128, 2, S), F32)
                acc = [acc2[:, 0, :], acc2[:, 1, :]]
                accB = Buf("acc2")
                acc_b = [accB, accB]
                accP = [Buf("accP%d" % i) for i in range(3)]
                pT = [sbB("pT%d" % i, (128, 512), BF16) for i in range(6)]
                pT_b = [Buf("pT%d" % i) for i in range(6)]
                rec = [sbB("rec%d" % i, (128, 512), F32) for i in range(2)]
                rec_b = [Buf("rec%d" % i) for i in range(2)]
                for i in range(2):
                    sc.op(DVE, lambda e, i=i: e.memset(Vtok[i][:], 0.0), writes=[Vtok_b[i]])
                    sc.op(DVE, lambda e, i=i: e.memset(Vtok[i][:, :, 64:65], 1.0), writes=[Vtok_b[i]])
                pi = 0
                vi = 0
                xin = [sbB("xin%d" % i, (128, D), F32) for i in range(3)]
                xin_b = [Buf("xin%d" % i) for i in range(3)]

                def srcA(t):
                    k = t % 3
                    sc.dma(SP, xin[k][:], x[si, t * 128:(t + 1) * 128, :], writes=[xin_b[k]])
                    return xin[k][:], [xin_b[k]]
                norm_T("mix", srcA, g_mix, xnT, xnT_b, NT, nscr)
                for hp in range(4):
                    wk = hp % 2
                    for o in range(3):
                        sc.dma(POOL, wA[wk][:, :, o * 128:(o + 1) * 128], w_in_v[:, :, o * A_W + hp * 128:o * A_W + (hp + 1) * 128], writes=[wA_b[wk][o]])
                    sc.dma(POOL, Ehp[wk][:], c_E[:, hp * 1536:(hp + 1) * 1536], writes=[Ehp_b[wk]])
                    for o in range(3):
                        for cb in range(4):
                            bk, bb = bank()
                            for kc in range(8):
                                sc.op(PE, lambda e, bk=bk, kc=kc, o=o, cb=cb, wk=wk: e.matmul(bk[:, :], lhsT=wA[wk][:, kc, o * 128:(o + 1) * 128], rhs=xnT[:, kc, cb * 512:(cb + 1) * 512], start=(kc == 0), stop=(kc == 7)),
                                      reads=[wA_b[wk][o]] + xnT_b[cb * 4:cb * 4 + 4], writes=[bb], inc=(kc == 7))
                            cs = slice(cb * 512, (cb + 1) * 512)
                            targets = [(natT, nat_b, None)] + [(permTs[i], perm_bs[i], DIL[i + 1]) for i in range(2)]
                            for ti, (dstT_, dstb_, dd) in enumerate(targets):
                                def views(rows, slot, dstT_=dstT_, dd=dd, bk=bk, cb=cb, cs=cs):
                                    if dd is None:
                                        return dstT_[rows, slot, cs], bk[rows, :]
                                    w = 512 // dd
                                    o_ap = dstT_[rows, slot, :].rearrange("p (r u) -> p r u", r=dd)[:, :, cb * w:(cb + 1) * w]
                                    i_ap = bk[rows, :].rearrange("p (u r) -> p r u", r=dd)
                                    return o_ap, i_ap
                                use_act = ((o * 4 + cb) % 2 == 0)
                                parts = [(slice(0, 64), 0), (slice(64, 128), 1)] if o == 0 else [(slice(0, 128), o + 1)]
                                for rows, slot in parts:
                                    o_ap, i_ap = views(rows, slot)
                                    if use_act:
                                        sc.op(ACT, lambda e, o_ap=o_ap, i_ap=i_ap: e.activation(out=o_ap, in_=i_ap, func=AF.Copy), reads=[bb], writes=[dstb_[slot][cb]])
                                    else:
                                        sc.op(DVE, lambda e, o_ap=o_ap, i_ap=i_ap: e.tensor_copy(out=o_ap, in_=i_ap), reads=[bb], writes=[dstb_[slot][cb]])
                    for di, d in enumerate(DIL):
                        U = S // d
                        nb = U // 128
                        if d == 1:
                            srcT, src_b = natT, nat_b
                        else:
                            permT, perm_b = permTs[di - 1], perm_bs[di - 1]
                            srcT, src_b = permT, perm_b
                        vk = vi % 2
                        vi += 1
                        for half in range(2):
                            bk, bb = bank()
                            bkv = bk[:].bitcast(BF16)
                            for j in range(8):
                                blk_i = half * 8 + j
                                sc.op(PE, lambda e, bkv=bkv, j=j, blk_i=blk_i, srcT=srcT: e.transpose(out=bkv[:, j * 128:(j + 1) * 128], in_=srcT[:, 3, blk_i * 128:(blk_i + 1) * 128], identity=ident[:]),
                                      reads=src_b[3] + [CONST], writes=[bb], inc=(j == 7))
                            sc.op(DVE, lambda e, bkv=bkv, half=half, vk=vk: e.tensor_copy(
                                out=Vtok[vk][:, half * 8:(half + 1) * 8, :].rearrange("p b (s c) -> p b s c", s=3)[:, :, 0:3:2, :],
                                in_=bkv[:, 0:1024].rearrange("p (b s c) -> p b s c", b=8, s=2)),
                                reads=[bb], writes=[Vtok_b[vk]])
                        steps = [(r, n) for r in range(d) for n in range(nb)]
                        st = {}

                        def stageA(step, d=d, U=U, di=di, srcT=srcT, src_b=src_b, wk=wk):
                            nonlocal pi
                            r, n = step
                            Eoff = di * 512
                            c0 = r * U + n * 128
                            lo = 0 if n > 0 else 256
                            bk, bb = bank()
                            if n > 0:
                                sc.op(PE, lambda e: e.matmul(bk[:, 0:256].rearrange("p (s c) -> p s c", s=2), lhsT=srcT[:, 2, c0 - 128:c0], rhs=srcT[:, 0:2, c0:c0 + 128], start=True, stop=True),
                                      reads=src_b[0] + src_b[1] + src_b[2], writes=[bb], inc=False)
                            sc.op(PE, lambda e: e.matmul(bk[:, 256:512].rearrange("p (s c) -> p s c", s=2), lhsT=srcT[:, 2, c0:c0 + 128], rhs=srcT[:, 0:2, c0:c0 + 128], start=True, stop=True),
                                  reads=src_b[0] + src_b[1] + src_b[2], writes=[bb])
                            pk = pi % len(pT)
                            pi += 1
                            sc.op(ACT, lambda e: e.activation(out=pT[pk][:, lo:512], in_=bk[:, lo:512], func=AF.Exp, scale=0.125),
                                  reads=[bb], writes=[pT_b[pk]])
                            sc.op(DVE, lambda e: e.tensor_tensor(out=pT[pk][:, lo:512], in0=pT[pk][:, lo:512], in1=Ehp[wk][:, Eoff + lo:Eoff + 512], op=ALU.mult),
                                  reads=[pT_b[pk], Ehp_b[wk]], writes=[pT_b[pk]])
                            st[step] = pk

                        def stageB(step, d=d, U=U, vk=vk):
                            r, n = step
                            pk = st.pop(step)
                            c0 = r * U + n * 128
                            vb = c0 // 128
                            bk2, bb2 = bank()
                            for h in range(2):
                                vsl = slice(0, 65) if h == 0 else slice(64, 192)
                                mrows = 65 if h == 0 else 128
                                oc = slice(h * 128, (h + 1) * 128)
                                if n > 0:
                                    sc.op(PE, lambda e, vsl=vsl, mrows=mrows, h=h, oc=oc: e.matmul(bk2[0:mrows, oc], lhsT=Vtok[vk][:, vb - 1, vsl], rhs=pT[pk][:, h * 128:(h + 1) * 128], start=True, stop=False),
                                          reads=[Vtok_b[vk], pT_b[pk]], writes=[bb2], inc=False)
                                sc.op(PE, lambda e, vsl=vsl, mrows=mrows, h=h, oc=oc: e.matmul(bk2[0:mrows, oc], lhsT=Vtok[vk][:, vb, vsl], rhs=pT[pk][:, 256 + h * 128:256 + (h + 1) * 128], start=(n == 0), stop=True),
                                      reads=[Vtok_b[vk], pT_b[pk]], writes=[bb2], inc=(h == 1))
                            src2 = bk2[:, 0:256].rearrange("p (h c) -> p h c", h=2)
                            if d == 1:
                                dst = acc2[:, :, n * 128:(n + 1) * 128]
                                sc.op(ACT, lambda e: e.activation(out=dst, in_=src2, func=AF.Copy), reads=[bb2, accB], writes=[accP[0]])
                            else:
                                dst = acc2[:, :, :].rearrange("p h (u r) -> p h r u", r=d)[:, :, r, n * 128:(n + 1) * 128]
                                sc.op(DVE, lambda e, di=di: e.tensor_tensor(out=dst, in0=src2, in1=dst, op=ALU.add), reads=[bb2, accB] + accP[0:di], writes=[accP[di]])

                        SKEW = 3
                        for i in range(len(steps) + SKEW):
                            if i < len(steps):
                                stageA(steps[i])
                            if i >= SKEW:
                                stageB(steps[i - SKEW])
                    sc.op(ACT, lambda e: e.activation(out=acc2[64:65, 0, :], in_=acc2[64:65, 0, :], func=AF.Ln), reads=[accB] + accP, writes=[accB])
                    sc.op(ACT, lambda e: e.activation(out=acc2[0:1, 1, :], in_=acc2[0:1, 1, :], func=AF.Ln), reads=[accB] + accP, writes=[accB])
                    for h in range(2):
                        for cb in range(4):
                            bk, bb = bank()
                            cs = slice(cb * 512, (cb + 1) * 512)
                            if h == 0:
                                sc.op(PE, lambda e, bk=bk, cs=cs: e.matmul(bk[0:64, :], lhsT=onesf[64:65, 0:64], rhs=acc2[64:65, 0, cs], start=True, stop=True), reads=[accB, CONST], writes=[bb])
                                rows = slice(0, 64)
                            else:
                                sc.op(PE, lambda e, bk=bk, cs=cs: e.matmul(bk[:, :], lhsT=onesf[0:1, 0:128], rhs=acc2[0:1, 1, cs], start=True, stop=True), reads=[accB, CONST], writes=[bb])
                                rows = slice(64, 128)
                            rk = (h * 4 + cb) % 2
                            sc.op(ACT, lambda e, bk=bk, rk=rk, rows=rows: e.activation(out=rec[rk][rows, :], in_=bk[rows, :], func=AF.Exp, scale=-1.0), reads=[bb], writes=[rec_b[rk]])
                            sc.op(DVE, lambda e, rk=rk, rows=rows, cs=cs, h=h, hp=hp: e.tensor_tensor(out=catT[rows, hp, cs], in0=acc2[rows, h, cs], in1=rec[rk][rows, :], op=ALU.mult),
                                  reads=[rec_b[rk], accB] + accP, writes=[catT_b[hp]])
                    if stop_after == "B0":
                        stgB = sbB("stgB", (128, S), F32)
                        stgB_b = Buf("stgB")
                        for c, (src, srcb) in enumerate([(acc[0], acc_b[0]), (acc[1], acc_b[1]), (natT[:, 0, :], nat_b[0][0]), (natT[:, 2, :], nat_b[2][0]), (natT[:, 3, :], nat_b[3][0])]):
                            sc.op(DVE, lambda e, src=src: e.tensor_copy(out=stgB[:], in_=src), reads=[srcb], writes=[stgB_b])
                            db = Buf("dbgB%d" % c)
                            sc.dma(SP, dbg_t[c * 128:(c + 1) * 128, :], stgB[:], reads=[stgB_b], writes=[db])
                            outs_final.append(db)
                        break
            if stop_after in ("B", "B0"):
                break

            with contextlib.ExitStack() as esC:
                sc.barrier()

                def sbC(name, shape, dt):
                    return esC.enter_context(nc.sbuf_tensor("%s_s%d" % (name, si), list(shape), dt))
                wg = sbC("wg", (128, 8, 8), BF16)
                wg_b = Buf("wg")
                wv = sbC("wv", (128, 8, 512), BF16)
                wv_b = Buf("wv")
                wqk = [sbC("wqk%d" % i, (128, 8, 384), BF16) for i in range(2)]
                wqk_b = [[Buf("wqk%d_%d" % (i, o)) for o in range(3)] for i in range(2)]
                rmask = sbC("rmask", (4, S), BF16)
                rmask_b = Buf("rmask")
                sc.dma(POOL, rmask[:], c_rmask, writes=[rmask_b])
                X1 = sbC("X1", (4, S), F32)
                X2 = sbC("X2", (4, S), F32)
                X3 = sbC("X3", (4, S), F32)
                X4 = sbC("X4", (4, S), F32)
                GB = Buf("gates")
                gsm = sbC("gsm", (4, 4, 32), F32)
                cpre = [sbC("cpre%d" % i, (128, S + 3), BF16) for i in range(2)]
                cpre_b = [Buf("cpre%d" % i) for i in range(2)]
                cacc = sbC("cacc", (128, S), BF16)
                cacc_b = Buf("cacc")
                dg = [sbC("dg%d" % i, (128, 4, 128), BF16) for i in range(2)]
                dg_b = [Buf("dg%d" % i) for i in range(2)]
                qTe2 = [sbC("qTe%d" % i, (128, S), BF16) for i in range(2)]
                qTo2 = [sbC("qTo%d" % i, (128, S), BF16) for i in range(2)]
                q_b2 = [Buf("qT%d" % i) for i in range(2)]
                ktT2 = [sbC("ktT%d" % i, (128, S), BF16) for i in range(2)]
                ktT_b2 = [Buf("ktT%d" % i) for i in range(2)]
                kttok2 = [sbC("kttok%d" % i, (128, NT, 128), BF16) for i in range(2)]
                kttok_b2 = [Buf("kttok%d" % i) for i in range(2)]
                vaug = sbC("vaug", (128, NT, 4, 129), BF16)
                vaug_b = Buf("vaug")
                Cst = sbC("Cst", (128, 4, 129), F32)
                Cst_r = [Buf("Cst%d" % i) for i in range(4)]
                Cst_b = Buf("Cst")
                Chat = sbC("Chat", (128, 32, 129), BF16)
                Chat_b = Buf("Chat")
                gamB = sbC("gamB", (128, 4, 32), F32)
                gamB_b = Buf("gamB")
                fltok = sbC("fltok", (128, NT, 4), F32)
                fltok_b = Buf("fltok")
                hmb = sbC("hmb", (128, NT, 128), BF16)
                hmb_b = Buf("hmb")
                hmb_t = [Buf("hmb%d" % t) for t in range(NT)]
                ssm = sbC("ssm", (128, NT), F32)
                sqm = sbC("sqm", (128, NT), F32)
                rsm = sbC("rsm", (128, NT), F32)
                ssm_b = Buf("ssm")
                dn = sbC("dn", (128, NT, 2), F32)
                dn_b = [Buf("dn%d" % t) for t in range(NT)]
                Pm = [sbC("Pm%d" % i, (128, 128), BF16) for i in range(3)]
                Pm_b = [Buf("Pm%d" % i) for i in range(3)]
                sg = [sbC("sg%d" % i, (128, 128), BF16) for i in range(3)]
                sg_b = [Buf("sg%d" % i) for i in range(3)]
                t1 = [sbC("t1_%d" % i, (128, 128), BF16) for i in range(3)]
                t1_b = [Buf("t1_%d" % i) for i in range(3)]
                t2 = [sbC("t2_%d" % i, (128, 128), BF16) for i in range(3)]
                t2_b = [Buf("t2_%d" % i) for i in range(3)]
                junkm = sbC("junkm", (128, 128), BF16)

                MQ0 = 3 * A_W
                sc.dma(POOL, wg[:], w_in_v[:, :, 3584:3592], writes=[wg_b])
                sc.dma(POOL, wv[:], w_in_v[:, :, MQ0 + 1024:MQ0 + 1536], writes=[wv_b])

                def load_wqk(j):
                    k = j % 2
                    sc.dma(POOL, wqk[k][:, :, 0:128], w_in_v[:, :, MQ0 + j * 128:MQ0 + (j + 1) * 128], writes=[wqk_b[k][0]])
                    sc.dma(POOL, wqk[k][:, :, 128:256], w_in_v[:, :, MQ0 + 512 + j * 128:MQ0 + 512 + (j + 1) * 128], writes=[wqk_b[k][1]])
                    sc.dma(POOL, wqk[k][:, :, 256:384], w_in_v[:, :, MQ0 + 1536 + j * 128:MQ0 + 1536 + (j + 1) * 128], writes=[wqk_b[k][2]])
                load_wqk(0)
                sc.op(POOL, lambda e: e.memset(vaug[:, :, :, 128:129], 1.0), writes=[vaug_b])
                for i in range(2):
                    sc.op(POOL, lambda e, i=i: e.memset(cpre[i][:, 0:3], 0.0), writes=[cpre_b[i]])
                for i in range(2):
                    sc.op(POOL, lambda e, i=i: e.memset(qTe2[i][:], 0.0), writes=[q_b2[i]])
                    sc.op(POOL, lambda e, i=i: e.memset(qTo2[i][:], 0.0), writes=[q_b2[i]])

                for cb in range(4):
                    cs = slice(cb * 512, (cb + 1) * 512)
                    bk, bb = bank()
                    for kc in range(8):
                        sc.op(PE, lambda e, bk=bk, kc=kc, cs=cs: e.matmul(bk[0:4, :], lhsT=wg[:, kc, 0:4], rhs=xnT[:, kc, cs], start=(kc == 0), stop=(kc == 7)),
                              reads=[wg_b] + xnT_b[cb * 4:cb * 4 + 4], writes=[bb], inc=(kc == 7))
                    sc.op(ACT, lambda e, bk=bk, cs=cs: e.activation(out=X1[:, cs], in_=bk[0:4, :], func=AF.Identity, bias=gbi[:], scale=1.0), reads=[bb, CONST], writes=[GB])
                    bk, bb = bank()
                    for kc in range(8):
                        sc.op(PE, lambda e, bk=bk, kc=kc, cs=cs: e.matmul(bk[0:4, :], lhsT=wg[:, kc, 4:8], rhs=xnT[:, kc, cs], start=(kc == 0), stop=(kc == 7)),
                              reads=[wg_b] + xnT_b[cb * 4:cb * 4 + 4], writes=[bb], inc=(kc == 7))
                    sc.op(ACT, lambda e, bk=bk, cs=cs: e.activation(out=X2[:, cs], in_=bk[0:4, :], func=AF.Exp, bias=ngbf[:], scale=-1.0), reads=[bb, CONST], writes=[GB])
                def v_tile(T):
                    bk, bb = bank()
                    for kc in range(8):
                        sc.op(PE, lambda e, kc=kc: e.matmul(bk[:, :], lhsT=xnT[:, kc, T * 128:(T + 1) * 128], rhs=wv[:, kc, :], start=(kc == 0), stop=(kc == 7)),
                              reads=[wv_b, xnT_b[T]], writes=[bb], inc=(kc == 7))
                    sc.op(ACT, lambda e: e.activation(out=vaug[:, T, :, 0:128], in_=bk[:, :].rearrange("p (j c) -> p j c", j=4), func=AF.Copy), reads=[bb], writes=[vaug_b])
                v_todo = list(range(NT))

                def v_some(n):
                    for _ in range(n):
                        if v_todo:
                            v_tile(v_todo.pop(0))
                sc.op(ACT, lambda e: e.activation(out=X2[:], in_=X2[:], func=AF.Ln, bias=1.0, scale=1.0), reads=[GB], writes=[GB])
                v_some(1)
                sc.op(DVE, lambda e: e.tensor_scalar(out=X2[:], in0=X2[:], scalar1=-1.0, scalar2=None, op0=ALU.mult), reads=[GB], writes=[GB])
                v_some(1)
                sc.op(DVE, lambda e: e.tensor_tensor_scan(out=X3[:], data0=rmask[:], data1=X2[:], initial=0.0, op0=ALU.mult, op1=ALU.add), reads=[GB, rmask_b], writes=[GB])
                v_some(1)
                sc.op(DVE, lambda e: e.tensor_tensor(out=X1[:], in0=X1[:], in1=X3[:], op=ALU.subtract), reads=[GB], writes=[GB])
                v_some(1)
                sc.op(DVE, lambda e: e.memset(X2[:], 0.0), reads=[GB], writes=[GB])
                v_some(1)
                X3v = X3[:].rearrange("p (c l) -> p c l", l=64)
                X2v = X2[:].rearrange("p (c l) -> p c l", l=64)
                X1v = X1[:].rearrange("p (c l) -> p c l", l=64)
                X4v = X4[:].rearrange("p (c l) -> p c l", l=64)
                sc.op(DVE, lambda e: e.tensor_copy(out=X2v[:, 1:32, 0:1], in_=X3v[:, 0:31, 63:64]), reads=[GB], writes=[GB])
                v_some(1)
                sc.op(DVE, lambda e: e.tensor_tensor_scan(out=X4[:], data0=X2[:], data1=X1[:], initial=0.0, op0=ALU.add, op1=ALU.max), reads=[GB], writes=[GB])
                v_some(1)
                sc.op(DVE, lambda e: e.tensor_copy(out=gsm[:, 0, :], in_=X3v[:, :, 63]), reads=[GB], writes=[GB])
                v_some(1)
                sc.op(DVE, lambda e: e.tensor_copy(out=gsm[:, 1, :], in_=X4v[:, :, 63]), reads=[GB], writes=[GB])
                v_some(1)
                sc.op(DVE, lambda e: e.memset(gsm[:, 2, :], 0.0), reads=[GB], writes=[GB])
                v_some(1)
                sc.op(DVE, lambda e: e.tensor_tensor(out=gsm[:, 3, 1:32], in0=gsm[:, 0, 0:31], in1=gsm[:, 1, 0:31], op=ALU.add), reads=[GB], writes=[GB])
                v_some(1)
                sc.op(DVE, lambda e: e.tensor_tensor(out=gsm[:, 3, 1:32], in0=gsm[:, 3, 1:32], in1=gsm[:, 1, 1:32], op=ALU.subtract), reads=[GB], writes=[GB])
                v_some(1)
                sc.op(ACT, lambda e: e.activation(out=gsm[:, 2, 1:32], in_=gsm[:, 3, 1:32], func=AF.Exp), reads=[GB], writes=[GB])
                v_some(1)
                Mend_bc = gsm[:, 1, :].unsqueeze(2).to_broadcast([4, 32, 64])
                sc.op(DVE, lambda e: e.tensor_tensor(out=X1v, in0=X1v, in1=Mend_bc, op=ALU.subtract), reads=[GB], writes=[GB])
                v_some(1)
                sc.op(DVE, lambda e: e.tensor_scalar(out=X1[:], in0=X1[:], scalar1=-0.5 * math.log(128.0), scalar2=None, op0=ALU.add), reads=[GB], writes=[GB])
                v_some(1)
                sc.op(ACT, lambda e: e.activation(out=X1[:], in_=X1[:], func=AF.Exp), reads=[GB], writes=[GB])
                v_some(1)
                sc.op(DVE, lambda e: e.tensor_tensor(out=X3v, in0=X3v, in1=Mend_bc, op=ALU.add), reads=[GB], writes=[GB])
                v_some(1)
                sc.op(ACT, lambda e: e.activation(out=X3[:], in_=X3[:], func=AF.Exp, scale=-1.0), reads=[GB], writes=[GB])
                v_some(1)
                if stop_after == "C1":
                    for c, (src, n) in enumerate([(X1[:], S), (X3[:], S), (X4[:], S), (gsm[:].rearrange("p a b -> p (a b)"), 128)]):
                        db = Buf("dbgC%d" % c)
                        sc.dma(SP, dbg_t[c * 4:(c + 1) * 4, 0:n], src, reads=[GB], writes=[db])
                        outs_final.append(db)
                    break
                bk, bb = bank()
                for j in range(4):
                    sc.op(PE, lambda e, bk=bk, j=j: e.matmul(bk[:, j * 32:(j + 1) * 32], lhsT=onehot[:, j * 128:(j + 1) * 128], rhs=gsm[:, 2, :], start=True, stop=True), reads=[GB, CONST], writes=[bb], inc=(j == 3))
                sc.op(DVE, lambda e, bk=bk: e.tensor_copy(out=gamB[:].rearrange("p j c -> p (j c)"), in_=bk[:, 0:128]), reads=[bb], writes=[gamB_b])
                bk, bb = bank()
                for T in range(NT):
                    sc.op(PE, lambda e, bk=bk, T=T: e.transpose(out=bk[:, T * 4:(T + 1) * 4], in_=X3[:, T * 128:(T + 1) * 128], identity=identf[0:4, 0:4]), reads=[GB, CONST], writes=[bb], inc=(T == NT - 1))
                sc.op(DVE, lambda e, bk=bk: e.tensor_copy(out=fltok[:].rearrange("p t j -> p (t j)"), in_=bk[:, 0:64]), reads=[bb], writes=[fltok_b])
                v_some(NT)
                def head_gen(j):
                    wk = j % 2
                    qTe, qTo, ktT, kttok = qTe2[j % 2], qTo2[j % 2], ktT2[j % 2], kttok2[j % 2]
                    q_b, ktT_b, kttok_b = q_b2[j % 2], ktT_b2[j % 2], kttok_b2[j % 2]
                    if j > 0:
                        load_wqk(j)
                    for qk in range(2):
                        chb = qk * 4 + j
                        cp = cpre[qk]
                        cpb = cpre_b[qk]
                        for tap in range(4):
                            sc.op(DVE, lambda e, tap=tap, chb=chb, qk=qk: e.tensor_scalar(out=dg[qk][:, tap, :], in0=ident[:], scalar1=cwm[:, chb, tap:tap + 1], scalar2=None, op0=ALU.mult), reads=[CONST], writes=[dg_b[qk]])
                        for cb in range(4):
                            cs = slice(cb * 512, (cb + 1) * 512)
                            bk, bb = bank()
                            for kc in range(8):
                                sc.op(PE, lambda e, bk=bk, kc=kc, cs=cs, wk=wk, qk=qk: e.matmul(bk[:, :], lhsT=wqk[wk][:, kc, qk * 128:(qk + 1) * 128], rhs=xnT[:, kc, cs], start=(kc == 0), stop=(kc == 7)),
                                      reads=[wqk_b[wk][qk]] + xnT_b[cb * 4:cb * 4 + 4], writes=[bb], inc=(kc == 7))
                            if cb % 2 == 0:
                                sc.op(ACT, lambda e, bk=bk, cb=cb, cp=cp: e.activation(out=cp[:, 3 + cb * 512:3 + (cb + 1) * 512], in_=bk[:, :], func=AF.Copy), reads=[bb], writes=[cpb])
                            else:
                                sc.op(DVE, lambda e, bk=bk, cb=cb, cp=cp: e.tensor_copy(out=cp[:, 3 + cb * 512:3 + (cb + 1) * 512], in_=bk[:, :]), reads=[bb], writes=[cpb])
                            yield None
                    for qk in range(2):
                        chb = qk * 4 + j
                        cp = cpre[qk]
                        cpb = cpre_b[qk]
                        for cb in range(4):
                            cs = slice(cb * 512, (cb + 1) * 512)
                            bk, bb = bank()
                            for tap in range(4):
                                sc.op(PE, lambda e, bk=bk, tap=tap, cb=cb, cp=cp, qk=qk: e.matmul(bk[:, :], lhsT=dg[qk][:, tap, :], rhs=cp[:, 3 + cb * 512 - tap:3 + (cb + 1) * 512 - tap], start=(tap == 0), stop=(tap == 3)),
                                      reads=[dg_b[qk], cpb], writes=[bb], inc=(tap == 3))
                            if qk == 0:
                                bv = bk[:, :].rearrange("p (t two l) -> p t two l", two=2, l=64)
                                sc.op(ACT, lambda e, bv=bv, cs=cs, chb=chb: e.activation(out=qTe[:, cs].rearrange("p (t two l) -> p t two l", two=2, l=64)[:, :, 0, :], in_=bv[:, :, 0, :], func=AF.Silu, bias=cbm[:, chb:chb + 1], scale=1.0), reads=[bb, CONST], writes=[q_b])
                                sc.op(ACT, lambda e, bv=bv, cs=cs, chb=chb: e.activation(out=qTo[:, cs].rearrange("p (t two l) -> p t two l", two=2, l=64)[:, :, 1, :], in_=bv[:, :, 1, :], func=AF.Silu, bias=cbm[:, chb:chb + 1], scale=1.0), reads=[bb, CONST], writes=[q_b])
                            else:
                                sc.op(ACT, lambda e, bk=bk, cs=cs, chb=chb: e.activation(out=cacc[:, cs], in_=bk[:, :], func=AF.Silu, bias=cbm[:, chb:chb + 1], scale=1.0), reads=[bb, CONST], writes=[cacc_b])
                                bk2, bb2 = bank()
                                sc.op(PE, lambda e, bk2=bk2, cs=cs, j=j: e.matmul(bk2[:, :], lhsT=onehot[:, j * 128:(j + 1) * 128], rhs=X1[:, cs], start=True, stop=True), reads=[GB, CONST], writes=[bb2])
                                sc.op(DVE, lambda e, bk2=bk2, cs=cs: e.tensor_tensor(out=ktT[:, cs], in0=bk2[:, :], in1=cacc[:, cs], op=ALU.mult), reads=[bb2, cacc_b], writes=[ktT_b])
                            yield None
                    for half in range(2):
                        bk, bb = bank()
                        bkv = bk[:].bitcast(BF16)
                        for i in range(8):
                            T = half * 8 + i
                            sc.op(PE, lambda e, bkv=bkv, i=i, T=T: e.transpose(out=bkv[:, i * 128:(i + 1) * 128], in_=ktT[:, T * 128:(T + 1) * 128], identity=ident[:]), reads=[ktT_b, CONST], writes=[bb], inc=(i == 7))
                        sc.op(ACT, lambda e, bkv=bkv, half=half: e.activation(out=kttok[:, half * 8:(half + 1) * 8, :], in_=bkv[:, 0:1024].rearrange("p (t c) -> p t c", t=8), func=AF.Copy), reads=[bb], writes=[kttok_b])
                    yield "EARLY_DONE"
                    sc.op(POOL, lambda e: e.memset(Cst[:, 0, :], 0.0), writes=[Cst_r[0]])
                    sc.op(POOL, lambda e: e.memset(Chat[:, 0, :], 0.0), reads=[], writes=[Chat_b])
                    for c in range(31):
                        T, par = c // 2, c % 2
                        pr = slice(par * 64, (par + 1) * 64)
                        bk, bb = bank()
                        sc.op(PE, lambda e, bk=bk, T=T, pr=pr, j=j: e.matmul(bk[:, 0:129], lhsT=kttok[pr, T, :], rhs=vaug[pr, T, j, :], start=True, stop=True), reads=[kttok_b, vaug_b], writes=[bb])
                        sc.op(DVE, lambda e, bk=bk, c=c, j=j: e.scalar_tensor_tensor(out=Cst[:, (c + 1) % 4, :], in0=Cst[:, c % 4, :], scalar=gamB[:, j, c:c + 1], in1=bk[:, 0:129], op0=ALU.mult, op1=ALU.add), reads=[bb, Cst_r[c % 4], gamB_b], writes=[Cst_r[(c + 1) % 4]])
                        sc.op(ACT, lambda e, c=c, j=j: e.activation(out=Chat[:, c + 1, :], in_=Cst[:, (c + 1) % 4, :], func=AF.Copy, scale=gamB[:, j, c + 1:c + 2]), reads=[Cst_r[(c + 1) % 4], gamB_b], writes=[Chat_b])
                        yield None
                    def tile1(T):
                        ts_ = slice(T * 128, (T + 1) * 128)
                        bk, bb = bank()
                        sc.op(PE, lambda e: e.matmul(bk[:, 0:128], lhsT=ktT[:, ts_], rhs=qTe[:, ts_], start=True, stop=False), reads=[ktT_b, q_b], writes=[bb], inc=False)
                        sc.op(PE, lambda e: e.matmul(bk[:, 0:128], lhsT=ktT[:, ts_], rhs=qTo[:, ts_], start=False, stop=True), reads=[ktT_b, q_b], writes=[bb])
                        pk = T % 3
                        sc.op(DVE, lambda e: e.tensor_tensor(out=Pm[pk][:], in0=bk[:, 0:128], in1=cmask[:], op=ALU.mult), reads=[bb, CONST], writes=[Pm_b[pk]])

                    def tile2(T, j=j):
                        ts_ = slice(T * 128, (T + 1) * 128)
                        pk = T % 3
                        bk2, bb2 = bank()
                        sc.op(PE, lambda e: e.matmul(bk2[:, 0:129], lhsT=qTe[:, ts_], rhs=Chat[:, 2 * T, :], start=True, stop=False), reads=[q_b, Chat_b], writes=[bb2], inc=False)
                        sc.op(PE, lambda e: e.matmul(bk2[:, 0:129], lhsT=qTo[:, ts_], rhs=Chat[:, 2 * T + 1, :], start=False, stop=False), reads=[q_b, Chat_b], writes=[bb2], inc=False)
                        sc.op(PE, lambda e: e.matmul(bk2[:, 0:129], lhsT=Pm[pk][:], rhs=vaug[:, T, j, :], start=False, stop=True), reads=[Pm_b[pk], vaug_b], writes=[bb2])
                        sc.op(ACT, lambda e: e.activation(out=dn[:, T, 0:1], in_=bk2[:, 128:129], func=AF.Abs), reads=[bb2], writes=[dn_b[T]])
                        sc.op(DVE, lambda e: e.tensor_tensor(out=dn[:, T, 0:1], in0=dn[:, T, 0:1], in1=fltok[:, T, j:j + 1], op=ALU.max), reads=[dn_b[T], fltok_b], writes=[dn_b[T]])
                        sc.op(DVE, lambda e: e.reciprocal(out=dn[:, T, 1:2], in_=dn[:, T, 0:1]), reads=[dn_b[T]], writes=[dn_b[T]])
                        sc.op(DVE, lambda e: e.tensor_scalar(out=hmb[:, T, :], in0=bk2[:, 0:128], scalar1=dn[:, T, 1:2], scalar2=None, op0=ALU.mult), reads=[bb2, dn_b[T]], writes=[hmb_t[T]])
                        sc.op(ACT, lambda e: e.activation(out=junkm[:], in_=hmb[:, T, :], func=AF.Square, accum_out=ssm[:, T:T + 1]), reads=[hmb_t[T]], writes=[ssm_b])
                    for i in range(NT + 2):
                        if i < NT:
                            tile1(i)
                        if i >= 2:
                            tile2(i - 2)
                        yield None
                    sc.op(ACT, lambda e: e.activation(out=sqm[:], in_=ssm[:], func=AF.Sqrt, scale=1.0 / 128.0, bias=eps_t[:]), reads=[ssm_b, CONST], writes=[ssm_b])
                    sc.op(DVE, lambda e: e.reciprocal(out=rsm[:], in_=sqm[:]), reads=[ssm_b], writes=[ssm_b])
                    bkT, bbT = bank_long()
                    bkTv = bkT[:].bitcast(BF16)

                    def og1(T, j=j, wk=wk):
                        k2 = T % 3
                        bk, bb = bank()
                        for kc in range(8):
                            sc.op(PE, lambda e, kc=kc: e.matmul(bk[:, 0:128], lhsT=xnT[:, kc, T * 128:(T + 1) * 128], rhs=wqk[wk][:, kc, 256:384], start=(kc == 0), stop=(kc == 7)),
                                  reads=[wqk_b[wk][2], xnT_b[T]], writes=[bb], inc=(kc == 7))
                        sc.op(ACT, lambda e: e.activation(out=sg[k2][:], in_=bk[:, 0:128], func=AF.Sigmoid), reads=[bb], writes=[sg_b[k2]])
                        sc.op(DVE, lambda e: e.scalar_tensor_tensor(out=t1[k2][:], in0=hmb[:, T, :], scalar=rsm[:, T:T + 1], in1=hgbc[:, j * 128:(j + 1) * 128], op0=ALU.mult, op1=ALU.mult), reads=[hmb_t[T], ssm_b, CONST], writes=[t1_b[k2]])
                        sc.op(DVE, lambda e: e.tensor_tensor(out=t2[k2][:], in0=t1[k2][:], in1=sg[k2][:], op=ALU.mult), reads=[t1_b[k2], sg_b[k2]], writes=[t2_b[k2]])

                    def og2(T, j=j):
                        k2 = T % 3
                        i = T % 8
                        half = T // 8
                        sc.op(PE, lambda e: e.transpose(out=bkTv[:, i * 128:(i + 1) * 128], in_=t2[k2][:], identity=ident[:]), reads=[t2_b[k2], CONST], writes=[bbT])
                        if i == 7:
                            sc.op(ACT, lambda e: e.activation(out=catT[:, 4 + j, half * 1024:(half + 1) * 1024], in_=bkTv[:, 0:1024], func=AF.Copy), reads=[bbT], writes=[catT_b[4 + j]])
                    for i in range(NT + 2):
                        if i < NT:
                            og1(i)
                        if i >= 2:
                            og2(i - 2)
                        yield None


                gens = [head_gen(j) for j in range(4)]

                def run_early(g):
                    for v in g:
                        if v == "EARLY_DONE":
                            return
                run_early(gens[0])
                for j in range(4):
                    nxt = gens[j + 1] if j + 1 < 4 else None
                    nxt_done = nxt is None
                    cur_done = False
                    while not (cur_done and nxt_done):
                        if not cur_done:
                            try:
                                next(gens[j])
                            except StopIteration:
                                cur_done = True
                        if not nxt_done:
                            try:
                                v = next(nxt)
                                if v == "EARLY_DONE":
                                    nxt_done = True
                            except StopIteration:
                                nxt_done = True

            if stop_after in ("C", "C1", "C2", "C3"):
                break

            with contextlib.ExitStack() as esD:
                sc.barrier()

                def sbD(name, shape, dt):
                    return esD.enter_context(nc.sbuf_tensor("%s_s%d" % (name, si), list(shape), dt))
                h_sb = sbD("h_sb", (128, NT, D), F32)
                h_b = [Buf("h%d" % t) for t in range(NT)]
                wbig = sbD("wbig", (128, 8, 1024), BF16)
                wbig_h = [Buf("wbig_h0"), Buf("wbig_h1")]
                wbig_b = wbig_h

                def load_wbig(src_v):
                    n = src_v.shape[1]
                    for hf in range(2):
                        sc.dma(POOL, wbig[:, 0:n, hf * 512:(hf + 1) * 512], src_v[:, :, hf * 512:(hf + 1) * 512], writes=[wbig_h[hf]])

                def out_proj(nk, in_bufs, fused=None):
                    for T in range(NT):
                        for nb in range(2):
                            bk, bb = bank()
                            for kc in range(nk):
                                sc.op(PE, lambda e, bk=bk, kc=kc, T=T, nb=nb: e.matmul(bk[:, :], lhsT=catT[:, kc, T * 128:(T + 1) * 128], rhs=wbig[:, kc, nb * 512:(nb + 1) * 512], start=(kc == 0), stop=(kc == nk - 1)),
                                      reads=[wbig_h[nb]] + in_bufs, writes=[bb], inc=(kc == nk - 1))
                            sc.op(DVE, lambda e, bk=bk, T=T, nb=nb: e.tensor_tensor(out=h_sb[:, T, nb * 512:(nb + 1) * 512], in0=bk[:, :], in1=h_sb[:, T, nb * 512:(nb + 1) * 512], op=ALU.add),
                                  reads=[bb, h_b[T]], writes=[h_b[T]])
                        if fused is not None:
                            next(fused, None)
                    if fused is not None:
                        for _ in fused:
                            pass

                def srcH(t):
                    return h_sb[:, t, :], [h_b[t]]

                load_wbig(w_mo.rearrange("(kc p) n -> p kc n", p=128))
                for T in range(NT):
                    sc.dma(SP, h_sb[:, T, :], x[si, T * 128:(T + 1) * 128, :], writes=[h_b[T]])
                if stop_after == "D":
                    out_proj(8, catT_b)
                else:
                    out_proj(8, catT_b, fused=norm_gen("xa", srcH, g_xa, xnT, xnT_b, NT, nscr))

                stopped = False
                if stop_after != "D":
                    with contextlib.ExitStack() as esE:
                        def sbE(name, shape, dt):
                            return esE.enter_context(nc.sbuf_tensor("%s_s%d" % (name, si), list(shape), dt))
                        xm = [sbE("xm%d" % i, (128, D), F32) for i in range(2)]
                        xm_b = [Buf("xm%d" % i) for i in range(2)]
                        memnT = sbE("memnT", (128, 8, NMEM), BF16)
                        memn_b = [Buf("memn%d" % i) for i in range(2)]
                        kTx = sbE("kTx", (128, 8, NMEM), BF16)
                        kTx_b = Buf("kTx")
                        vtokx = sbE("vtokx", (128, 2, D), BF16)
                        vtokx_b = Buf("vtokx")
                        wq = [sbE("wq%d" % i, (128, 8, 256), BF16) for i in range(2)]
                        wq_b = [Buf("wq%d" % i) for i in range(2)]
                        qTx = sbE("qTx", (128, 2, S), BF16)
                        qTx_b = Buf("qTx")
                        qTx_c = [[Buf("qTx%d_%d" % (a_, b_)) for b_ in range(4)] for a_ in range(2)]
                        pTx = [sbE("pTx%d" % i, (128, 512), BF16) for i in range(4)]
                        pTx_b = [Buf("pTx%d" % i) for i in range(4)]
                        recx = [sbE("recx%d" % i, (128, 512), F32) for i in range(2)]
                        recx_b = [Buf("recx%d" % i) for i in range(2)]
                        ones_bf = sbE("ones_bf", (128, 128), BF16)
                        ones_bf_b = Buf("ones_bf")
                        sc.op(POOL, lambda e: e.memset(ones_bf[:], 1.0), writes=[ones_bf_b])

                        def srcM(t):
                            sc.dma(SP, xm[t][:], mem[si, t * 128:(t + 1) * 128, :], writes=[xm_b[t]])
                            return xm[t][:], [xm_b[t]]
                        norm_T("mem", srcM, g_mem, memnT, memn_b, 2, nscr)
                        w_xkv_v = w_xkv.rearrange("(kc p) n -> p kc n", p=128)
                        load_wbig(w_xkv_v[:, :, 0:1024])
                        for oc in range(8):
                            bk, bb = bank()
                            for kc in range(8):
                                sc.op(PE, lambda e, bk=bk, kc=kc, oc=oc: e.matmul(bk[:, 0:NMEM], lhsT=wbig[:, kc, oc * 128:(oc + 1) * 128], rhs=memnT[:, kc, :], start=(kc == 0), stop=(kc == 7)),
                                      reads=[wbig_h[oc // 4]] + memn_b, writes=[bb], inc=(kc == 7))
                            sc.op(ACT, lambda e, bk=bk, oc=oc: e.activation(out=kTx[:, oc, :], in_=bk[:, 0:NMEM], func=AF.Copy), reads=[bb], writes=[kTx_b])
                        load_wbig(w_xkv_v[:, :, 1024:2048])
                        for mt in range(2):
                            for nb in range(2):
                                bk, bb = bank()
                                for kc in range(8):
                                    sc.op(PE, lambda e, bk=bk, kc=kc, mt=mt, nb=nb: e.matmul(bk[:, :], lhsT=memnT[:, kc, mt * 128:(mt + 1) * 128], rhs=wbig[:, kc, nb * 512:(nb + 1) * 512], start=(kc == 0), stop=(kc == 7)),
                                          reads=[wbig_h[nb], memn_b[mt]], writes=[bb], inc=(kc == 7))
                                sc.op(DVE, lambda e, bk=bk, mt=mt, nb=nb: e.tensor_copy(out=vtokx[:, mt, nb * 512:(nb + 1) * 512], in_=bk[:, :]), reads=[bb], writes=[vtokx_b])
                        w_xq_v = w_xq.rearrange("(kc p) n -> p kc n", p=128)
                        pxi = 0
                        for hh in range(4):
                            wk = hh % 2
                            sc.dma(POOL, wq[wk][:], w_xq_v[:, :, hh * 256:(hh + 1) * 256], writes=[wq_b[wk]])
                            for c2 in range(2):
                                for cb in range(4):
                                    cs = slice(cb * 512, (cb + 1) * 512)
                                    bk, bb = bank()
                                    for kc in range(8):
                                        sc.op(PE, lambda e, bk=bk, kc=kc, c2=c2, cs=cs, wk=wk: e.matmul(bk[:, :], lhsT=wq[wk][:, kc, c2 * 128:(c2 + 1) * 128], rhs=xnT[:, kc, cs], start=(kc == 0), stop=(kc == 7)),
                                              reads=[wq_b[wk]] + xnT_b[cb * 4:cb * 4 + 4], writes=[bb], inc=(kc == 7))
                                    if cb % 2 == 0:
                                        sc.op(ACT, lambda e, bk=bk, c2=c2, cs=cs: e.activation(out=qTx[:, c2, cs], in_=bk[:, :], func=AF.Copy), reads=[bb], writes=[qTx_c[c2][cb]])
                                    else:
                                        sc.op(DVE, lambda e, bk=bk, c2=c2, cs=cs: e.tensor_copy(out=qTx[:, c2, cs], in_=bk[:, :]), reads=[bb], writes=[qTx_c[c2][cb]])
                            for cb in range(4):
                                cs = slice(cb * 512, (cb + 1) * 512)
                                pks = []
                                for mt in range(2):
                                    bk, bb = bank()
                                    for c2 in range(2):
                                        sc.op(PE, lambda e, bk=bk, c2=c2, mt=mt, cs=cs, hh=hh: e.matmul(bk[:, :], lhsT=kTx[:, hh * 2 + c2, mt * 128:(mt + 1) * 128], rhs=qTx[:, c2, cs], start=(c2 == 0), stop=(c2 == 1)),
                                              reads=[kTx_b, qTx_c[c2][cb]], writes=[bb], inc=(c2 == 1))
                                    pk = pxi % 4
                                    pxi += 1
                                    pks.append(pk)
                                    sc.op(ACT, lambda e, bk=bk, pk=pk: e.activation(out=pTx[pk][:], in_=bk[:, :], func=AF.Exp, scale=1.0 / 16.0), reads=[bb], writes=[pTx_b[pk]])
                                bk, bb = bank()
                                for mt in range(2):
                                    sc.op(PE, lambda e, bk=bk, mt=mt, pk=pks[mt]: e.matmul(bk[:, :], lhsT=ones_bf[:], rhs=pTx[pk][:], start=(mt == 0), stop=(mt == 1)),
                                          reads=[ones_bf_b, pTx_b[pks[mt]]], writes=[bb], inc=(mt == 1))
                                rk = cb % 2
                                sc.op(ACT, lambda e, bk=bk, rk=rk: e.activation(out=recx[rk][:], in_=bk[:, :], func=AF.Ln), reads=[bb], writes=[recx_b[rk]])
                                sc.op(ACT, lambda e, rk=rk: e.activation(out=recx[rk][:], in_=recx[rk][:], func=AF.Exp, scale=-1.0), reads=[recx_b[rk]], writes=[recx_b[rk]])
                                for c2 in range(2):
                                    bk, bb = bank()
                                    for mt in range(2):
                                        sc.op(PE, lambda e, bk=bk, mt=mt, c2=c2, hh=hh, pk=pks[mt]: e.matmul(bk[:, :], lhsT=vtokx[:, mt, hh * 256 + c2 * 128:hh * 256 + (c2 + 1) * 128], rhs=pTx[pk][:], start=(mt == 0), stop=(mt == 1)),
                                              reads=[vtokx_b, pTx_b[pks[mt]]], writes=[bb], inc=(mt == 1))
                                    sc.op(DVE, lambda e, bk=bk, rk=rk, c2=c2, hh=hh, cs=cs: e.tensor_tensor(out=catT[:, hh * 2 + c2, cs], in0=bk[:, :], in1=recx[rk][:], op=ALU.mult),
                                          reads=[bb, recx_b[rk]], writes=[catT_b[hh * 2 + c2]])
                        load_wbig(w_xo.rearrange("(kc p) n -> p kc n", p=128))
                        if stop_after == "E":
                            out_proj(8, catT_b)
                        else:
                            out_proj(8, catT_b, fused=norm_gen("ffn", srcH, g_ffn, xnT, xnT_b, NT, nscr))

                if stop_after not in ("D", "E"):
                    with contextlib.ExitStack() as esF:
                        sc.barrier()
                        def sbF(name, shape, dt):
                            return esF.enter_context(nc.sbuf_tensor("%s_s%d" % (name, si), list(shape), dt))
                        wup = [sbF("wup%d" % i, (128, 8, 256), BF16) for i in range(3)]
                        wupg_b = [Buf("wupg%d" % i) for i in range(3)]
                        wupu_b = [Buf("wupu%d" % i) for i in range(3)]
                        gpre = [sbF("gpre%d" % i, (128, S + 2), BF16) for i in range(2)]
                        gpre_b = [Buf("gpre%d" % i) for i in range(2)]
                        gact = [sbF("gact%d" % i, (128, S), BF16) for i in range(2)]
                        gact_b = [Buf("gact%d" % i) for i in range(2)]
                        dgf = [sbF("dgf%d" % i, (128, 3, 128), BF16) for i in range(2)]
                        dgf_b = [Buf("dgf%d" % i) for i in range(2)]
                        for i in range(2):
                            sc.op(POOL, lambda e, i=i: e.memset(gpre[i][:, 0:2], 0.0), writes=[gpre_b[i]])
                        otF = [sbF("otF%d" % i, (128, D), F32) for i in range(2)]
                        otF_b = [Buf("otF%d" % i) for i in range(2)]
                        gbF = sbF("gbF", (128, D), F32)
                        junkF = sbF("junkF", (128, D), BF16)

                        def final_gen():
                            sc.dma(SP, gbF[:], g_fin.partition_broadcast(128), writes=[gb_b])
                            for T in range(NT):
                                k = T % 2
                                yb = Buf("y%d_%d" % (si, T))
                                sc.op(ACT, lambda e, T=T: e.activation(out=junkF[:], in_=h_sb[:, T, :], func=AF.Square, accum_out=ss_t[:, T:T + 1]), reads=[h_b[T]], writes=[scr_b])
                                sc.op(ACT, lambda e, T=T: e.activation(out=sq_t[:, T:T + 1], in_=ss_t[:, T:T + 1], func=AF.Sqrt, scale=1.0 / D, bias=eps_t[:]), reads=[scr_b, CONST], writes=[scr_b])
                                sc.op(DVE, lambda e, T=T: e.reciprocal(out=rstd_t[:, T:T + 1], in_=sq_t[:, T:T + 1]), reads=[scr_b], writes=[scr_b2])
                                sc.op(DVE, lambda e, T=T, k=k: e.scalar_tensor_tensor(out=otF[k][:], in0=h_sb[:, T, :], scalar=rstd_t[:, T:T + 1], in1=gbF[:], op0=ALU.mult, op1=ALU.mult), reads=[h_b[T], scr_b2, gb_b], writes=[otF_b[k]])
                                sc.dma(SP, y[si, T * 128:(T + 1) * 128, :], otF[k][:], reads=[otF_b[k]], writes=[yb])
                                outs_final.append(yb)
                                yield T
                        w_up_v = w_up.rearrange("(kc p) n -> p kc n", p=128)

                        def f_gate(fc):
                            k = fc % 2
                            w3 = fc % 3
                            sc.dma(POOL, wup[w3][:, :, 0:128], w_up_v[:, :, fc * 128:(fc + 1) * 128], writes=[wupg_b[w3]])
                            sc.dma(POOL, wup[w3][:, :, 128:256], w_up_v[:, :, DFF + fc * 128:DFF + (fc + 1) * 128], writes=[wupu_b[w3]])
                            for tap in range(3):
                                sc.op(DVE, lambda e, tap=tap: e.tensor_scalar(out=dgf[k][:, tap, :], in0=ident[:], scalar1=cwf[:, fc, tap:tap + 1], scalar2=None, op0=ALU.mult), reads=[CONST], writes=[dgf_b[k]])
                            for cb in range(4):
                                cs = slice(cb * 512, (cb + 1) * 512)
                                bk, bb = bank()
                                for kc in range(8):
                                    sc.op(PE, lambda e, bk=bk, kc=kc, cs=cs: e.matmul(bk[:, :], lhsT=wup[w3][:, kc, 0:128], rhs=xnT[:, kc, cs], start=(kc == 0), stop=(kc == 7)),
                                          reads=[wupg_b[w3]] + xnT_b[cb * 4:cb * 4 + 4], writes=[bb], inc=(kc == 7))
                                sc.op(ACT, lambda e, bk=bk, cb=cb: e.activation(out=gpre[k][:, 2 + cb * 512:2 + (cb + 1) * 512], in_=bk[:, :], func=AF.Copy), reads=[bb], writes=[gpre_b[k]])

                        def f_rest(fc, f0):
                            k = fc % 2
                            w3 = fc % 3
                            for cb in range(4):
                                cs = slice(cb * 512, (cb + 1) * 512)
                                bk, bb = bank()
                                for tap in range(3):
                                    sc.op(PE, lambda e, bk=bk, tap=tap, cb=cb: e.matmul(bk[:, :], lhsT=dgf[k][:, tap, :], rhs=gpre[k][:, 2 + cb * 512 - tap:2 + (cb + 1) * 512 - tap], start=(tap == 0), stop=(tap == 2)),
                                          reads=[dgf_b[k], gpre_b[k]], writes=[bb], inc=(tap == 2))
                                sc.op(ACT, lambda e, bk=bk, cs=cs: e.activation(out=gact[k][:, cs], in_=bk[:, :], func=AF.Silu, bias=cbf[:, fc:fc + 1], scale=1.0), reads=[bb, CONST], writes=[gact_b[k]])
                            for cb in range(4):
                                cs = slice(cb * 512, (cb + 1) * 512)
                                bk, bb = bank()
                                for kc in range(8):
                                    sc.op(PE, lambda e, bk=bk, kc=kc, cs=cs: e.matmul(bk[:, :], lhsT=wup[w3][:, kc, 128:256], rhs=xnT[:, kc, cs], start=(kc == 0), stop=(kc == 7)),
                                          reads=[wupu_b[w3]] + xnT_b[cb * 4:cb * 4 + 4], writes=[bb], inc=(kc == 7))
                                sc.op(DVE, lambda e, bk=bk, cs=cs: e.tensor_tensor(out=catT[:, fc - f0, cs], in0=bk[:, :], in1=gact[k][:, cs], op=ALU.mult), reads=[bb, gact_b[k]], writes=[catT_b[fc - f0]])

                        f_gate(0)
                        for (f0, f1) in ((0, 8), (8, 16), (16, 22)):
                            ng = f1 - f0
                            load_wbig(w_dn[f0 * 128:f1 * 128, :].rearrange("(fc p) n -> p fc n", p=128))
                            for fc in range(f0, f1):
                                if fc + 1 < NFF:
                                    f_gate(fc + 1)
                                f_rest(fc, f0)
                            if f1 == NFF and stop_after != "F":
                                out_proj(ng, catT_b[0:ng], fused=final_gen())
                            else:
                                out_proj(ng, catT_b[0:ng])

                if stop_after in ("D", "E", "F"):
                  with contextlib.ExitStack() as esG:
                    sc.barrier()
                    ot = [esG.enter_context(nc.sbuf_tensor("ot%d_s%d" % (i, si), [128, D], F32)) for i in range(2)]
                    ot_b = [Buf("ot%d" % i) for i in range(2)]
                    raw = stop_after in ("D", "E", "F")
                    gb1 = esG.enter_context(nc.sbuf_tensor("gb1G_s%d" % si, [128, D], F32))
                    junk = esG.enter_context(nc.sbuf_tensor("junkG_s%d" % si, [128, D], BF16))
                    if not raw:
                        sc.dma(SP, gb1[:], g_fin.partition_broadcast(128), writes=[gb_b])
                    for T in range(NT):
                        yb = Buf("y%d_%d" % (si, T))
                        if raw:
                            sc.dma(SP, y[si, T * 128:(T + 1) * 128, :], h_sb[:, T, :], reads=[h_b[T]], writes=[yb])
                        else:
                            k = T % 2
                            sc.op(ACT, lambda e, T=T: e.activation(out=junk[:], in_=h_sb[:, T, :], func=AF.Square, accum_out=ss_t[:, T:T + 1]), reads=[h_b[T]], writes=[scr_b])
                            sc.op(ACT, lambda e, T=T: e.activation(out=sq_t[:, T:T + 1], in_=ss_t[:, T:T + 1], func=AF.Sqrt, scale=1.0 / D, bias=eps_t[:]), reads=[scr_b, CONST], writes=[scr_b])
                            sc.op(DVE, lambda e, T=T: e.reciprocal(out=rstd_t[:, T:T + 1], in_=sq_t[:, T:T + 1]), reads=[scr_b], writes=[scr_b2])
                            sc.op(DVE, lambda e, T=T, k=k: e.scalar_tensor_tensor(out=ot[k][:], in0=h_sb[:, T, :], scalar=rstd_t[:, T:T + 1], in1=gb1[:], op0=ALU.mult, op1=ALU.mult), reads=[h_b[T], scr_b2, gb_b], writes=[ot_b[k]])
                            sc.dma(SP, y[si, T * 128:(T + 1) * 128, :], ot[k][:], reads=[ot_b[k]], writes=[yb])
                        outs_final.append(yb)
            if stop_after in ("D", "E", "F"):
                break

        fin = []
        if dbg and stop_after in ("B", "C"):
            stg = sb("stg", (128, S), F32)
            stg_b = Buf("stg")
            for c in range(8):
                sc.op(DVE, lambda e, c=c: e.tensor_copy(out=stg[:], in_=catT[:, c, :]), reads=[catT_b[c]] + xnT_b, writes=[stg_b])
                db = Buf("dbgout%d" % c)
                sc.dma(SP, dbg_t[c * 128:(c + 1) * 128, :], stg[:], reads=[stg_b], writes=[db])
                fin.append(db)
        sc.final_wait(SP, fin + outs_final)
        stuck = sc.check_deadlock()
        if stuck:
            raise RuntimeError("semaphore deadlock detected at build time: %r" % (stuck,))
        for sem in sc.all_sems():
            nc.sync.sem_clear(sem)
        nc.all_engine_barrier()
        blk = es.enter_context(nc.Block())
        sc.emit(blk)
    return nc


def _consts():
    ident = np.eye(128, dtype=np.float32)
    slopes = np.exp2(-8.0 * np.arange(1, 9, dtype=np.float64) / 8.0)
    kj = np.arange(128)[:, None]
    qi = np.arange(128)[None, :]
    E = np.zeros((128, 4, 3, 512), np.float32)
    for h in range(8):
        for di, d in enumerate(DIL):
            relp = qi - kj + 128
            prev = np.where(relp <= 128, np.exp(np.minimum(-slopes[h] * d * relp, 0.0)), 0.0)
            relc = qi - kj
            cur = np.where(relc >= 0, np.exp(np.minimum(-slopes[h] * d * relc, 0.0)), 0.0)
            hp, hh = h // 2, h % 2
            E[:, hp, di, hh * 128:(hh + 1) * 128] = prev
            E[:, hp, di, 256 + hh * 128:256 + (hh + 1) * 128] = cur
    E = E.reshape(128, 8 * 3 * 256)
    s_ = np.arange(128)[:, None]
    t_ = np.arange(128)[None, :]
    cm = ((s_ // 64 == t_ // 64) & (s_ <= t_)).astype(np.float32)
    rm = np.ones((4, S), np.float32)
    rm[:, 0::64] = 0.0
    oh = np.zeros((4, 4, 128), np.float32)
    for j in range(4):
        oh[j, j, :] = 1.0
    return dict(c_ident=ident, c_E=E, c_cmask=cm, c_rmask=rm, c_onehot=oh.reshape(4, 512))


_W_NAMES = ["norm_mix_g", "w_in", "mlstm_conv_w", "mlstm_conv_b", "mlstm_gate_b", "mlstm_head_g", "w_mix_out",
            "norm_xattn_g", "norm_mem_g", "w_xq", "w_xkv", "w_xo", "norm_ffn_g", "w_ffn_up", "ffn_conv_w",
            "ffn_conv_b", "w_ffn_down"]


def make_in_maps(inputs, n_cores, nseq):
    consts = _consts()
    shared = {}
    for k in _W_NAMES:
        shared[k] = np.ascontiguousarray(np.asarray(inputs[k], dtype=np.float32)[0])
    shared["norm_final_g"] = np.ascontiguousarray(np.asarray(inputs["norm_final_g"], dtype=np.float32))
    shared.update(consts)
    xs = np.asarray(inputs["x"], dtype=np.float32)
    ms = np.asarray(inputs["mem"], dtype=np.float32)
    maps = []
    for c in range(n_cores):
        m = dict(shared)
        m["x"] = np.ascontiguousarray(xs[c * nseq:(c + 1) * nseq])
        m["mem"] = np.ascontiguousarray(ms[c * nseq:(c + 1) * nseq])
        maps.append(m)
    return maps


def kernel(**inputs):
    nseq = inputs["x"].shape[0] // N_CORES
    nc = build(nseq)
    maps = make_in_maps(inputs, N_CORES, nseq)
    res = run_bass_kernel_spmd(nc, maps, core_ids=list(range(N_CORES)))
    return np.concatenate([r["y"] for r in res.results], axis=0)
```

```python
import contextlib
import math
import numpy as np
import concourse.bass as bass
import concourse.mybir as mybir
from concourse.bass_utils import run_bass_kernel_spmd

F32 = mybir.dt.float32
BF16 = mybir.dt.bfloat16
AF = mybir.ActivationFunctionType
ALU = mybir.AluOpType

D = 1024
S = 2048
NT = S // 128
NMEM = 256
A_W = 512
M_W = 512
IN_COLS = 3592
DFF = 2816
NFF = DFF // 128
EPS = 1e-6
N_CORES = 8
DIL = (1, 4, 16)


class Buf:
    __slots__ = ("name", "w", "r", "excl")

    def __init__(self, name, excl=False):
        self.name = name
        self.w = None
        self.r = []
        self.excl = excl


class Eng:
    def __init__(self, name, is_pe=False):
        self.name = name
        self.ops = []
        self.sem = None
        self.cnt = 0
        self.seen = {}
        self.is_pe = is_pe
        self.log = []


class Sched:
    def __init__(self, nc, es, nds=12):
        self.nc = nc
        self.pe = Eng("pe", True)
        self.dve = Eng("dve")
        self.act = Eng("act")
        self.pool = Eng("pool")
        self.sp = Eng("sp")
        self.engs = [self.pe, self.dve, self.act, self.pool, self.sp]
        for e in self.engs:
            e.sem = es.enter_context(nc.semaphore("sem_" + e.name))
        self.dsems = {}
        for q in (self.sp, self.pool, self.act):
            self.dsems[q.name] = [[es.enter_context(nc.semaphore("dma_%s%d" % (q.name, i))), 0] for i in range(nds if q is self.sp else 6)]
        self.dnext = {k: 0 for k in self.dsems}

    def all_sems(self):
        out = [e.sem for e in self.engs]
        for lst in self.dsems.values():
            out += [x[0] for x in lst]
        return out

    def _wait(self, eng, ev):
        sem, val = ev
        if eng.is_pe and sem is eng.sem:
            return
        if eng.seen.get(id(sem), 0) >= val:
            return
        eng.seen[id(sem)] = val
        eng.ops.append(lambda e, sem=sem, val=val: e.wait_ge(sem, val))
        eng.log.append(("w", id(sem), val))

    def _deps(self, eng, reads, writes):
        for b in reads:
            if b.w is not None:
                self._wait(eng, b.w)
            if b.excl:
                for ev in b.r:
                    if ev[0] is not eng.sem:
                        self._wait(eng, ev)
        for b in writes:
            if b.w is not None and (b.w[0] is not eng.sem or b in reads):
                self._wait(eng, b.w)
            for ev in b.r:
                if ev[0] is eng.sem:
                    continue
                self._wait(eng, ev)

    def op(self, eng, fn, reads=(), writes=(), inc=True):
        self._deps(eng, reads, writes)
        sem = eng.sem
        if inc:
            eng.cnt += 1
            ev = (sem, eng.cnt)
            eng.ops.append(lambda e, fn=fn, sem=sem: fn(e).then_inc(sem, 1))
            eng.log.append(("i", id(sem), 1))
        else:
            ev = (sem, eng.cnt + 1)
            eng.ops.append(lambda e, fn=fn: fn(e))
        for b in reads:
            b.r.append(ev)
        for b in writes:
            b.w = ev
            b.r = []
        return ev

    def dma(self, q, out_ap, in_ap, reads=(), writes=(), slow=False):
        self._deps(q, reads, writes)
        lst = self.dsems[q.name]
        i = self.dnext[q.name]
        self.dnext[q.name] = (i + 1) % len(lst)
        sem, prev = lst[i]
        if prev > 0:
            self._wait(q, (sem, prev))
        lst[i][1] = prev + 16
        ev = (sem, prev + 16)
        if slow:
            q.ops.append(lambda e, o=out_ap, a=in_ap, sem=sem: e.dma_start(out=o, in_=a, allow_slow_non_contiguous=True).then_inc(sem, 16))
        else:
            q.ops.append(lambda e, o=out_ap, a=in_ap, sem=sem: e.dma_start(out=o, in_=a).then_inc(sem, 16))
        q.log.append(("i", id(sem), 16))
        for b in reads:
            b.r.append(ev)
        for b in writes:
            b.w = ev
            b.r = []
        return ev

    def check_deadlock(self):
        vals = {}
        pos = {e.name: 0 for e in self.engs}
        progress = True
        while progress:
            progress = False
            for e in self.engs:
                while pos[e.name] < len(e.log):
                    kind, sid, v = e.log[pos[e.name]]
                    if kind == "w":
                        if vals.get(sid, 0) < v:
                            break
                    else:
                        vals[sid] = vals.get(sid, 0) + v
                    pos[e.name] += 1
                    progress = True
        stuck = {e.name: (pos[e.name], len(e.log)) for e in self.engs if pos[e.name] < len(e.log)}
        return stuck

    def barrier(self):
        evs = [(e.sem, e.cnt) for e in self.engs if e.cnt > 0]
        for lst in self.dsems.values():
            evs += [(x[0], x[1]) for x in lst if x[1] > 0]
        for e in self.engs:
            for ev in evs:
                self._wait(e, ev)

    def final_wait(self, eng, bufs):
        for b in bufs:
            if b.w is not None:
                self._wait(eng, b.w)

    def emit(self, blk):
        def run(eng):
            def f(e):
                for o in eng.ops:
                    o(e)
            return f
        blk.tensor(run(self.pe))
        blk.vector(run(self.dve))
        blk.scalar(run(self.act))
        blk.gpsimd(run(self.pool))
        blk.sync(run(self.sp))


def build(nseq, stop_after=None, dbg=False):
    nc = bass.Bass("TRN2", target_bir_lowering=False)
    dt_in = {}

    def din(name, shape):
        t = nc.dram_tensor(name, list(shape), F32, kind="ExternalInput").ap()
        dt_in[name] = t
        return t

    x = din("x", (nseq, S, D))
    mem = din("mem", (nseq, NMEM, D))
    g_mix = din("norm_mix_g", (D,))
    w_in = din("w_in", (D, IN_COLS))
    mconv_w = din("mlstm_conv_w", (4, 2 * M_W))
    mconv_b = din("mlstm_conv_b", (2 * M_W,))
    gate_b = din("mlstm_gate_b", (8,))
    head_g = din("mlstm_head_g", (M_W,))
    w_mo = din("w_mix_out", (D, D))
    g_xa = din("norm_xattn_g", (D,))
    g_mem = din("norm_mem_g", (D,))
    w_xq = din("w_xq", (D, D))
    w_xkv = din("w_xkv", (D, 2 * D))
    w_xo = din("w_xo", (D, D))
    g_ffn = din("norm_ffn_g", (D,))
    w_up = din("w_ffn_up", (D, 2 * DFF))
    fconv_w = din("ffn_conv_w", (3, DFF))
    fconv_b = din("ffn_conv_b", (DFF,))
    w_dn = din("w_ffn_down", (DFF, D))
    g_fin = din("norm_final_g", (D,))
    c_ident = din("c_ident", (128, 128))
    c_E = din("c_E", (128, 8 * 3 * 256))
    c_cmask = din("c_cmask", (128, 128))
    c_rmask = din("c_rmask", (4, S))
    c_onehot = din("c_onehot", (4, 4 * 128))
    y = nc.dram_tensor("y", [nseq, S, D], F32, kind="ExternalOutput").ap()
    dbg_t = None
    if dbg:
        dbg_t = nc.dram_tensor("dbg", [D, S], F32, kind="ExternalOutput").ap()

    with contextlib.ExitStack() as es:
        sc = Sched(nc, es)
        PE, DVE, ACT, POOL, SP = sc.pe, sc.dve, sc.act, sc.pool, sc.sp

        def sb(name, shape, dt):
            return es.enter_context(nc.sbuf_tensor(name, list(shape), dt))

        banks = [es.enter_context(nc.psum_tensor("bank%d" % i, [128, 512], F32)) for i in range(8)]
        bank_b = [Buf("bank%d" % i, excl=True) for i in range(8)]
        bank_i = [0]

        def bank():
            i = bank_i[0]
            bank_i[0] = (i + 1) % 7
            return banks[i], bank_b[i]

        def bank_long():
            return banks[7], bank_b[7]

        ident = sb("ident", (128, 128), BF16)
        identf = sb("identf", (128, 128), F32)
        cmask = sb("cmask", (128, 128), F32)
        onehot = sb("onehot", (4, 512), F32)
        onesf = sb("onesf", (128, 128), F32)
        gb_b = Buf("gb1")
        uniq = [0]
        hgbc = sb("hgbc", (128, M_W), F32)
        cwm = sb("cwm", (128, 8, 4), F32)
        cbm = sb("cbm", (128, 8), F32)
        cwf = sb("cwf", (128, NFF, 3), F32)
        cbf = sb("cbf", (128, NFF), F32)
        gbi = sb("gbi", (4, 1), F32)
        gbf = sb("gbf", (4, 1), F32)
        ngbf = sb("ngbf", (4, 1), F32)
        CONST = Buf("const")
        GBFB = Buf("gbf")
        sc.dma(POOL, ident[:], c_ident, writes=[Buf("c")])
        sc.dma(SP, identf[:], c_ident, writes=[Buf("c")])
        sc.dma(SP, cmask[:], c_cmask, writes=[Buf("c")])
        sc.dma(SP, onehot[:], c_onehot, writes=[Buf("c")])
        sc.dma(SP, hgbc[:], head_g.partition_broadcast(128), writes=[Buf("c")])
        for j in range(4):
            sc.dma(SP, cwm[:, :, j], mconv_w[j].rearrange("(b p) -> p b", p=128), writes=[Buf("c")], slow=True)
        sc.dma(SP, cbm[:], mconv_b.rearrange("(b p) -> p b", p=128), writes=[Buf("c")], slow=True)
        for j in range(3):
            sc.dma(SP, cwf[:, :, j], fconv_w[j].rearrange("(b p) -> p b", p=128), writes=[Buf("c")], slow=True)
        sc.dma(SP, cbf[:], fconv_b.rearrange("(b p) -> p b", p=128), writes=[Buf("c")], slow=True)
        sc.dma(SP, gbi[:], gate_b[0:4].rearrange("(p o) -> p o", o=1), writes=[Buf("c")], slow=True)
        sc.dma(SP, gbf[:], gate_b[4:8].rearrange("(p o) -> p o", o=1), writes=[GBFB], slow=True)
        sc.op(DVE, lambda e: e.memset(onesf[:], 1.0), writes=[CONST])
        sc.op(DVE, lambda e: e.tensor_scalar(out=ngbf[:], in0=gbf[:], scalar1=-1.0, scalar2=None, op0=ALU.mult), reads=[GBFB], writes=[CONST])

        xnT = sb("xnT", (128, 8, S), BF16)
        xnT_b = [Buf("xnT%d" % t) for t in range(NT)]
        catT = sb("catT", (128, 8, S), BF16)
        catT_b = [Buf("catT%d" % c) for c in range(8)]

        w_in_v = w_in.rearrange("(kc p) n -> p kc n", p=128)

        def norm_T(name, srcs, gi, dstT, dst_bufs, nt, scr):
            for _ in norm_gen(name, srcs, gi, dstT, dst_bufs, nt, scr):
                pass

        def norm_gen(name, srcs, gi, dstT, dst_bufs, nt, scr):
            _j, ss, sq, rstd, _x, xnb_b = scr
            uniq[0] += 1
            esN = contextlib.ExitStack()
            gb1 = esN.enter_context(nc.sbuf_tensor("gb1_%d" % uniq[0], [128, D], F32))
            junk = esN.enter_context(nc.sbuf_tensor("junk_%d" % uniq[0], [128, D], BF16))
            xnb = [esN.enter_context(nc.sbuf_tensor("xnb%d_%d" % (i, uniq[0]), [128, D], BF16)) for i in range(2)]
            sc.dma(SP, gb1[:], gi.partition_broadcast(128), writes=[gb_b])
            def stage1(t):
                src, sbufs = srcs(t)
                sc.op(ACT, lambda e: e.activation(out=junk[:], in_=src, func=AF.Square, accum_out=ss[:, t:t + 1]),
                      reads=sbufs, writes=[scr_b])
                sc.op(ACT, lambda e: e.activation(out=sq[:, t:t + 1], in_=ss[:, t:t + 1], func=AF.Sqrt, scale=1.0 / D, bias=eps_t[:]),
                      reads=[scr_b, CONST], writes=[scr_b])
                sc.op(DVE, lambda e: e.reciprocal(out=rstd[:, t:t + 1], in_=sq[:, t:t + 1]), reads=[scr_b], writes=[scr_b2])
                k = t % 2
                sc.op(DVE, lambda e: e.scalar_tensor_tensor(out=xnb[k][:], in0=src, scalar=rstd[:, t:t + 1], in1=gb1[:], op0=ALU.mult, op1=ALU.mult),
                      reads=list(sbufs) + [scr_b2, gb_b], writes=[xnb_b[k]])

            def stage2(t):
                k = t % 2
                bk, bb = bank()
                bkv = bk[:].bitcast(BF16)
                for kc in range(8):
                    sc.op(PE, lambda e, kc=kc: e.transpose(out=bkv[:, kc * 128:(kc + 1) * 128], in_=xnb[k][:, kc * 128:(kc + 1) * 128], identity=ident[:]),
                          reads=[xnb_b[k], CONST], writes=[bb], inc=(kc == 7))
                sc.op(ACT, lambda e: e.activation(out=dstT[:, :, t * 128:(t + 1) * 128], in_=bkv[:, 0:1024].rearrange("p (k c) -> p k c", k=8), func=AF.Copy),
                      reads=[bb], writes=[dst_bufs[t]])

            for i in range(nt + 1):
                if i < nt:
                    stage1(i)
                if i >= 1:
                    stage2(i - 1)
                if i < nt:
                    yield i
            esN.close()

        eps_t = sb("eps_t", (128, 1), F32)
        zero_t = sb("zero_t", (128, 1), F32)
        sc.op(DVE, lambda e: e.memset(zero_t[:], 0.0), writes=[CONST])
        sc.op(DVE, lambda e: e.memset(eps_t[:], EPS), writes=[CONST])
        ss_t = sb("ss_t", (128, NT), F32)
        sq_t = sb("sq_t", (128, NT), F32)
        rstd_t = sb("rstd_t", (128, NT), F32)
        xnb_b = [Buf("xnb%d" % i) for i in range(2)]
        scr_b = Buf("scr")
        scr_b2 = Buf("scr2")
        nscr = (None, ss_t, sq_t, rstd_t, None, xnb_b)

        outs_final = []

        for si in range(nseq):
            with contextlib.ExitStack() as esB:
                sc.barrier()
                def sbB(name, shape, dt):
                    return esB.enter_context(nc.sbuf_tensor("%s_s%d" % (name, si), list(shape), dt))
                wA = [sbB("wA%d" % i, (128, 8, 384), BF16) for i in range(2)]
                wA_b = [[Buf("wA%d_%d" % (i, o)) for o in range(3)] for i in range(2)]
                Ehp = [sbB("Ehp%d" % i, (128, 6 * 256), BF16) for i in range(2)]
                Ehp_b = [Buf("Ehp%d" % i) for i in range(2)]
                natT = sbB("natT", (128, 4, S), BF16)
                nat_b = [Buf("nat%d" % o) for o in range(4)]
                permTs = [sbB("permT%d" % i, (128, 4, S), BF16) for i in range(2)]
                perm_bs = [[Buf("perm%d_%d" % (i, o)) for o in range(4)] for i in range(2)]
                sc.op(DVE, lambda e: e.memset(natT[64:128, 0, :], 0.0), writes=[nat_b[0]])
                sc.op(DVE, lambda e: e.memset(natT[0:64, 1, :], 0.0), writes=[nat_b[1]])
                for i in range(2):
                    sc.op(DVE, lambda e, i=i: e.memset(permTs[i][64:128, 0, :], 0.0), writes=[perm_bs[i][0]])
                    sc.op(DVE, lambda e, i=i: e.memset(permTs[i][0:64, 1, :], 0.0), writes=[perm_bs[i][1]])
                Vtok = [sbB("Vtok%d" % i, (128, 16, 192), BF16) for i in range(2)]
                Vtok_b = [Buf("Vtok%d" % i) for i in range(2)]
                acc2 = sbB("acc2", (128, 2, S), F32)
                acc = [acc2[:, 0, :], acc2[:, 1, :]]
                accB = Buf("acc2")
                acc_b = [accB, accB]
                accP = [Buf("accP%d" % i) for i in range(3)]
                pT = [sbB("pT%d" % i, (128, 512), BF16) for i in range(6)]
                pT_b = [Buf("pT%d" % i) for i in range(6)]
                rec = [sbB("rec%d" % i, (128, 512), F32) for i in range(2)]
                rec_b = [Buf("rec%d" % i) for i in range(2)]
                for i in range(2):
                    sc.op(DVE, lambda e, i=i: e.memset(Vtok[i][:], 0.0), writes=[Vtok_b[i]])
                    sc.op(DVE, lambda e, i=i: e.memset(Vtok[i][:, :, 64:65], 1.0), writes=[Vtok_b[i]])
                pi = 0
                vi = 0
                xin = [sbB("xin%d" % i, (128, D), F32) for i in range(3)]
                xin_b = [Buf("xin%d" % i) for i in range(3)]

                def srcA(t):
                    k = t % 3
                    sc.dma(SP, xin[k][:], x[si, t * 128:(t + 1) * 128, :], writes=[xin_b[k]])
                    return xin[k][:], [xin_b[k]]
                norm_T("mix", srcA, g_mix, xnT, xnT_b, NT, nscr)
                for hp in range(4):
                    wk = hp % 2
                    for o in range(3):
                        sc.dma(POOL, wA[wk][:, :, o * 128:(o + 1) * 128], w_in_v[:, :, o * A_W + hp * 128:o * A_W + (hp + 1) * 128], writes=[wA_b[wk][o]])
                    sc.dma(POOL, Ehp[wk][:], c_E[:, hp * 1536:(hp + 1) * 1536], writes=[Ehp_b[wk]])
                    for o in range(3):
                        for cb in range(4):
                            bk, bb = bank()
                            for kc in range(8):
                                sc.op(PE, lambda e, bk=bk, kc=kc, o=o, cb=cb, wk=wk: e.matmul(bk[:, :], lhsT=wA[wk][:, kc, o * 128:(o + 1) * 128], rhs=xnT[:, kc, cb * 512:(cb + 1) * 512], start=(kc == 0), stop=(kc == 7)),
                                      reads=[wA_b[wk][o]] + xnT_b[cb * 4:cb * 4 + 4], writes=[bb], inc=(kc == 7))
                            cs = slice(cb * 512, (cb + 1) * 512)
                            targets = [(natT, nat_b, None)] + [(permTs[i], perm_bs[i], DIL[i + 1]) for i in range(2)]
                            for ti, (dstT_, dstb_, dd) in enumerate(targets):
                                def views(rows, slot, dstT_=dstT_, dd=dd, bk=bk, cb=cb, cs=cs):
                                    if dd is None:
                                        return dstT_[rows, slot, cs], bk[rows, :]
                                    w = 512 // dd
                                    o_ap = dstT_[rows, slot, :].rearrange("p (r u) -> p r u", r=dd)[:, :, cb * w:(cb + 1) * w]
                                    i_ap = bk[rows, :].rearrange("p (u r) -> p r u", r=dd)
                                    return o_ap, i_ap
                                use_act = ((o * 4 + cb) % 2 == 0)
                                parts = [(slice(0, 64), 0), (slice(64, 128), 1)] if o == 0 else [(slice(0, 128), o + 1)]
                                for rows, slot in parts:
                                    o_ap, i_ap = views(rows, slot)
                                    if use_act:
                                        sc.op(ACT, lambda e, o_ap=o_ap, i_ap=i_ap: e.activation(out=o_ap, in_=i_ap, func=AF.Copy), reads=[bb], writes=[dstb_[slot]])
                                    else:
                                        sc.op(DVE, lambda e, o_ap=o_ap, i_ap=i_ap: e.tensor_copy(out=o_ap, in_=i_ap), reads=[bb], writes=[dstb_[slot]])
                    for di, d in enumerate(DIL):
                        U = S // d
                        nb = U // 128
                        if d == 1:
                            srcT, src_b = natT, nat_b
                        else:
                            permT, perm_b = permTs[di - 1], perm_bs[di - 1]
                            srcT, src_b = permT, perm_b
                        vk = vi % 2
                        vi += 1
                        for half in range(2):
                            bk, bb = bank()
                            bkv = bk[:].bitcast(BF16)
                            for j in range(8):
                                blk_i = half * 8 + j
                                sc.op(PE, lambda e, bkv=bkv, j=j, blk_i=blk_i, srcT=srcT: e.transpose(out=bkv[:, j * 128:(j + 1) * 128], in_=srcT[:, 3, blk_i * 128:(blk_i + 1) * 128], identity=ident[:]),
                                      reads=[src_b[3], CONST], writes=[bb], inc=(j == 7))
                            sc.op(DVE, lambda e, bkv=bkv, half=half, vk=vk: e.tensor_copy(
                                out=Vtok[vk][:, half * 8:(half + 1) * 8, :].rearrange("p b (s c) -> p b s c", s=3)[:, :, 0:3:2, :],
                                in_=bkv[:, 0:1024].rearrange("p (b s c) -> p b s c", b=8, s=2)),
                                reads=[bb], writes=[Vtok_b[vk]])
                        steps = [(r, n) for r in range(d) for n in range(nb)]
                        st = {}

                        def stageA(step, d=d, U=U, di=di, srcT=srcT, src_b=src_b, wk=wk):
                            nonlocal pi
                            r, n = step
                            Eoff = di * 512
                            c0 = r * U + n * 128
                            lo = 0 if n > 0 else 256
                            bk, bb = bank()
                            if n > 0:
                                sc.op(PE, lambda e: e.matmul(bk[:, 0:256].rearrange("p (s c) -> p s c", s=2), lhsT=srcT[:, 2, c0 - 128:c0], rhs=srcT[:, 0:2, c0:c0 + 128], start=True, stop=True),
                                      reads=[src_b[0], src_b[1], src_b[2]], writes=[bb], inc=False)
                            sc.op(PE, lambda e: e.matmul(bk[:, 256:512].rearrange("p (s c) -> p s c", s=2), lhsT=srcT[:, 2, c0:c0 + 128], rhs=srcT[:, 0:2, c0:c0 + 128], start=True, stop=True),
                                  reads=[src_b[0], src_b[1], src_b[2]], writes=[bb])
                            pk = pi % len(pT)
                            pi += 1
                            sc.op(ACT, lambda e: e.activation(out=pT[pk][:, lo:512], in_=bk[:, lo:512], func=AF.Exp, scale=0.125),
                                  reads=[bb], writes=[pT_b[pk]])
                            sc.op(DVE, lambda e: e.tensor_tensor(out=pT[pk][:, lo:512], in0=pT[pk][:, lo:512], in1=Ehp[wk][:, Eoff + lo:Eoff + 512], op=ALU.mult),
                                  reads=[pT_b[pk], Ehp_b[wk]], writes=[pT_b[pk]])
                            st[step] = pk

                        def stageB(step, d=d, U=U, vk=vk):
                            r, n = step
                            pk = st.pop(step)
                            c0 = r * U + n * 128
                            vb = c0 // 128
                            bk2, bb2 = bank()
                            for h in range(2):
                                vsl = slice(0, 65) if h == 0 else slice(64, 192)
                                mrows = 65 if h == 0 else 128
                                oc = slice(h * 128, (h + 1) * 128)
                                if n > 0:
                                    sc.op(PE, lambda e, vsl=vsl, mrows=mrows, h=h, oc=oc: e.matmul(bk2[0:mrows, oc], lhsT=Vtok[vk][:, vb - 1, vsl], rhs=pT[pk][:, h * 128:(h + 1) * 128], start=True, stop=False),
                                          reads=[Vtok_b[vk], pT_b[pk]], writes=[bb2], inc=False)
                                sc.op(PE, lambda e, vsl=vsl, mrows=mrows, h=h, oc=oc: e.matmul(bk2[0:mrows, oc], lhsT=Vtok[vk][:, vb, vsl], rhs=pT[pk][:, 256 + h * 128:256 + (h + 1) * 128], start=(n == 0), stop=True),
                                      reads=[Vtok_b[vk], pT_b[pk]], writes=[bb2], inc=(h == 1))
                            src2 = bk2[:, 0:256].rearrange("p (h c) -> p h c", h=2)
                            if d == 1:
                                dst = acc2[:, :, n * 128:(n + 1) * 128]
                                sc.op(ACT, lambda e: e.activation(out=dst, in_=src2, func=AF.Copy), reads=[bb2, accB], writes=[accP[0]])
                            else:
                                dst = acc2[:, :, :].rearrange("p h (u r) -> p h r u", r=d)[:, :, r, n * 128:(n + 1) * 128]
                                sc.op(DVE, lambda e, di=di: e.tensor_tensor(out=dst, in0=src2, in1=dst, op=ALU.add), reads=[bb2, accB] + accP[0:di], writes=[accP[di]])

                        SKEW = 3
                        for i in range(len(steps) + SKEW):
                            if i < len(steps):
                                stageA(steps[i])
                            if i >= SKEW:
                                stageB(steps[i - SKEW])
                    sc.op(ACT, lambda e: e.activation(out=acc2[64:65, 0, :], in_=acc2[64:65, 0, :], func=AF.Ln), reads=[accB] + accP, writes=[accB])
                    sc.op(ACT, lambda e: e.activation(out=acc2[0:1, 1, :], in_=acc2[0:1, 1, :], func=AF.Ln), reads=[accB] + accP, writes=[accB])
                    for h in range(2):
                        for cb in range(4):
                            bk, bb = bank()
                            cs = slice(cb * 512, (cb + 1) * 512)
                            if h == 0:
                                sc.op(PE, lambda e, bk=bk, cs=cs: e.matmul(bk[0:64, :], lhsT=onesf[64:65, 0:64], rhs=acc2[64:65, 0, cs], start=True, stop=True), reads=[accB, CONST], writes=[bb])
                                rows = slice(0, 64)
                            else:
                                sc.op(PE, lambda e, bk=bk, cs=cs: e.matmul(bk[:, :], lhsT=onesf[0:1, 0:128], rhs=acc2[0:1, 1, cs], start=True, stop=True), reads=[accB, CONST], writes=[bb])
                                rows = slice(64, 128)
                            rk = (h * 4 + cb) % 2
                            sc.op(ACT, lambda e, bk=bk, rk=rk, rows=rows: e.activation(out=rec[rk][rows, :], in_=bk[rows, :], func=AF.Exp, scale=-1.0), reads=[bb], writes=[rec_b[rk]])
                            sc.op(DVE, lambda e, rk=rk, rows=rows, cs=cs, h=h, hp=hp: e.tensor_tensor(out=catT[rows, hp, cs], in0=acc2[rows, h, cs], in1=rec[rk][rows, :], op=ALU.mult),
                                  reads=[rec_b[rk], accB] + accP, writes=[catT_b[hp]])
                    if stop_after == "B0":
                        stgB = sbB("stgB", (128, S), F32)
                        stgB_b = Buf("stgB")
                        for c, (src, srcb) in enumerate([(acc[0], acc_b[0]), (acc[1], acc_b[1]), (natT[:, 0, :], nat_b[0]), (natT[:, 2, :], nat_b[2]), (natT[:, 3, :], nat_b[3])]):
                            sc.op(DVE, lambda e, src=src: e.tensor_copy(out=stgB[:], in_=src), reads=[srcb], writes=[stgB_b])
                            db = Buf("dbgB%d" % c)
                            sc.dma(SP, dbg_t[c * 128:(c + 1) * 128, :], stgB[:], reads=[stgB_b], writes=[db])
                            outs_final.append(db)
                        break
            if stop_after in ("B", "B0"):
                break

            with contextlib.ExitStack() as esC:
                sc.barrier()

                def sbC(name, shape, dt):
                    return esC.enter_context(nc.sbuf_tensor("%s_s%d" % (name, si), list(shape), dt))
                wg = sbC("wg", (128, 8, 8), BF16)
                wg_b = Buf("wg")
                wv = sbC("wv", (128, 8, 512), BF16)
                wv_b = Buf("wv")
                wqk = [sbC("wqk%d" % i, (128, 8, 384), BF16) for i in range(2)]
                wqk_b = [[Buf("wqk%d_%d" % (i, o)) for o in range(3)] for i in range(2)]
                rmask = sbC("rmask", (4, S), BF16)
                rmask_b = Buf("rmask")
                sc.dma(POOL, rmask[:], c_rmask, writes=[rmask_b])
                X1 = sbC("X1", (4, S), F32)
                X2 = sbC("X2", (4, S), F32)
                X3 = sbC("X3", (4, S), F32)
                X4 = sbC("X4", (4, S), F32)
                GB = Buf("gates")
                gsm = sbC("gsm", (4, 4, 32), F32)
                cpre = [sbC("cpre%d" % i, (128, S + 3), BF16) for i in range(2)]
                cpre_b = [Buf("cpre%d" % i) for i in range(2)]
                cacc = sbC("cacc", (128, S), BF16)
                cacc_b = Buf("cacc")
                dg = [sbC("dg%d" % i, (128, 4, 128), BF16) for i in range(2)]
                dg_b = [Buf("dg%d" % i) for i in range(2)]
                qTe2 = [sbC("qTe%d" % i, (128, S), BF16) for i in range(2)]
                qTo2 = [sbC("qTo%d" % i, (128, S), BF16) for i in range(2)]
                q_b2 = [Buf("qT%d" % i) for i in range(2)]
                ktT2 = [sbC("ktT%d" % i, (128, S), BF16) for i in range(2)]
                ktT_b2 = [Buf("ktT%d" % i) for i in range(2)]
                kttok2 = [sbC("kttok%d" % i, (128, NT, 128), BF16) for i in range(2)]
                kttok_b2 = [Buf("kttok%d" % i) for i in range(2)]
                vaug = sbC("vaug", (128, NT, 4, 129), BF16)
                vaug_b = Buf("vaug")
                Cst = sbC("Cst", (128, 4, 129), F32)
                Cst_r = [Buf("Cst%d" % i) for i in range(4)]
                Cst_b = Buf("Cst")
                Chat = sbC("Chat", (128, 32, 129), BF16)
                Chat_b = Buf("Chat")
                gamB = sbC("gamB", (128, 4, 32), F32)
                gamB_b = Buf("gamB")
                fltok = sbC("fltok", (128, NT, 4), F32)
                fltok_b = Buf("fltok")
                hmb = sbC("hmb", (128, NT, 128), BF16)
                hmb_b = Buf("hmb")
                hmb_t = [Buf("hmb%d" % t) for t in range(NT)]
                ssm = sbC("ssm", (128, NT), F32)
                sqm = sbC("sqm", (128, NT), F32)
                rsm = sbC("rsm", (128, NT), F32)
                ssm_b = Buf("ssm")
                dn = sbC("dn", (128, NT, 2), F32)
                dn_b = [Buf("dn%d" % t) for t in range(NT)]
                Pm = [sbC("Pm%d" % i, (128, 128), BF16) for i in range(3)]
                Pm_b = [Buf("Pm%d" % i) for i in range(3)]
                sg = [sbC("sg%d" % i, (128, 128), BF16) for i in range(3)]
                sg_b = [Buf("sg%d" % i) for i in range(3)]
                t1 = [sbC("t1_%d" % i, (128, 128), BF16) for i in range(3)]
                t1_b = [Buf("t1_%d" % i) for i in range(3)]
                t2 = [sbC("t2_%d" % i, (128, 128), BF16) for i in range(3)]
                t2_b = [Buf("t2_%d" % i) for i in range(3)]
                junkm = sbC("junkm", (128, 128), BF16)

                MQ0 = 3 * A_W
                sc.dma(POOL, wg[:], w_in_v[:, :, 3584:3592], writes=[wg_b])
                sc.dma(POOL, wv[:], w_in_v[:, :, MQ0 + 1024:MQ0 + 1536], writes=[wv_b])

                def load_wqk(j):
                    k = j % 2
                    sc.dma(POOL, wqk[k][:, :, 0:128], w_in_v[:, :, MQ0 + j * 128:MQ0 + (j + 1) * 128], writes=[wqk_b[k][0]])
                    sc.dma(POOL, wqk[k][:, :, 128:256], w_in_v[:, :, MQ0 + 512 + j * 128:MQ0 + 512 + (j + 1) * 128], writes=[wqk_b[k][1]])
                    sc.dma(POOL, wqk[k][:, :, 256:384], w_in_v[:, :, MQ0 + 1536 + j * 128:MQ0 + 1536 + (j + 1) * 128], writes=[wqk_b[k][2]])
                load_wqk(0)
                sc.op(POOL, lambda e: e.memset(vaug[:, :, :, 128:129], 1.0), writes=[vaug_b])
                for i in range(2):
                    sc.op(POOL, lambda e, i=i: e.memset(cpre[i][:, 0:3], 0.0), writes=[cpre_b[i]])
                for i in range(2):
                    sc.op(POOL, lambda e, i=i: e.memset(qTe2[i][:], 0.0), writes=[q_b2[i]])
                    sc.op(POOL, lambda e, i=i: e.memset(qTo2[i][:], 0.0), writes=[q_b2[i]])

                for cb in range(4):
                    cs = slice(cb * 512, (cb + 1) * 512)
                    bk, bb = bank()
                    for kc in range(8):
                        sc.op(PE, lambda e, bk=bk, kc=kc, cs=cs: e.matmul(bk[0:4, :], lhsT=wg[:, kc, 0:4], rhs=xnT[:, kc, cs], start=(kc == 0), stop=(kc == 7)),
                              reads=[wg_b] + xnT_b[cb * 4:cb * 4 + 4], writes=[bb], inc=(kc == 7))
                    sc.op(ACT, lambda e, bk=bk, cs=cs: e.activation(out=X1[:, cs], in_=bk[0:4, :], func=AF.Identity, bias=gbi[:], scale=1.0), reads=[bb, CONST], writes=[GB])
                    bk, bb = bank()
                    for kc in range(8):
                        sc.op(PE, lambda e, bk=bk, kc=kc, cs=cs: e.matmul(bk[0:4, :], lhsT=wg[:, kc, 4:8], rhs=xnT[:, kc, cs], start=(kc == 0), stop=(kc == 7)),
                              reads=[wg_b] + xnT_b[cb * 4:cb * 4 + 4], writes=[bb], inc=(kc == 7))
                    sc.op(ACT, lambda e, bk=bk, cs=cs: e.activation(out=X2[:, cs], in_=bk[0:4, :], func=AF.Exp, bias=ngbf[:], scale=-1.0), reads=[bb, CONST], writes=[GB])
                def v_tile(T):
                    bk, bb = bank()
                    for kc in range(8):
                        sc.op(PE, lambda e, kc=kc: e.matmul(bk[:, :], lhsT=xnT[:, kc, T * 128:(T + 1) * 128], rhs=wv[:, kc, :], start=(kc == 0), stop=(kc == 7)),
                              reads=[wv_b, xnT_b[T]], writes=[bb], inc=(kc == 7))
                    sc.op(ACT, lambda e: e.activation(out=vaug[:, T, :, 0:128], in_=bk[:, :].rearrange("p (j c) -> p j c", j=4), func=AF.Copy), reads=[bb], writes=[vaug_b])
                v_todo = list(range(NT))

                def v_some(n):
                    for _ in range(n):
                        if v_todo:
                            v_tile(v_todo.pop(0))
                sc.op(ACT, lambda e: e.activation(out=X2[:], in_=X2[:], func=AF.Ln, bias=1.0, scale=1.0), reads=[GB], writes=[GB])
                v_some(1)
                sc.op(DVE, lambda e: e.tensor_scalar(out=X2[:], in0=X2[:], scalar1=-1.0, scalar2=None, op0=ALU.mult), reads=[GB], writes=[GB])
                v_some(1)
                sc.op(DVE, lambda e: e.tensor_tensor_scan(out=X3[:], data0=rmask[:], data1=X2[:], initial=0.0, op0=ALU.mult, op1=ALU.add), reads=[GB, rmask_b], writes=[GB])
                v_some(1)
                sc.op(DVE, lambda e: e.tensor_tensor(out=X1[:], in0=X1[:], in1=X3[:], op=ALU.subtract), reads=[GB], writes=[GB])
                v_some(1)
                sc.op(DVE, lambda e: e.memset(X2[:], 0.0), reads=[GB], writes=[GB])
                v_some(1)
                X3v = X3[:].rearrange("p (c l) -> p c l", l=64)
                X2v = X2[:].rearrange("p (c l) -> p c l", l=64)
                X1v = X1[:].rearrange("p (c l) -> p c l", l=64)
                X4v = X4[:].rearrange("p (c l) -> p c l", l=64)
                sc.op(DVE, lambda e: e.tensor_copy(out=X2v[:, 1:32, 0:1], in_=X3v[:, 0:31, 63:64]), reads=[GB], writes=[GB])
                v_some(1)
                sc.op(DVE, lambda e: e.tensor_tensor_scan(out=X4[:], data0=X2[:], data1=X1[:], initial=0.0, op0=ALU.add, op1=ALU.max), reads=[GB], writes=[GB])
                v_some(1)
                sc.op(DVE, lambda e: e.tensor_copy(out=gsm[:, 0, :], in_=X3v[:, :, 63]), reads=[GB], writes=[GB])
                v_some(1)
                sc.op(DVE, lambda e: e.tensor_copy(out=gsm[:, 1, :], in_=X4v[:, :, 63]), reads=[GB], writes=[GB])
                v_some(1)
                sc.op(DVE, lambda e: e.memset(gsm[:, 2, :], 0.0), reads=[GB], writes=[GB])
                v_some(1)
                sc.op(DVE, lambda e: e.tensor_tensor(out=gsm[:, 3, 1:32], in0=gsm[:, 0, 0:31], in1=gsm[:, 1, 0:31], op=ALU.add), reads=[GB], writes=[GB])
                v_some(1)
                sc.op(DVE, lambda e: e.tensor_tensor(out=gsm[:, 3, 1:32], in0=gsm[:, 3, 1:32], in1=gsm[:, 1, 1:32], op=ALU.subtract), reads=[GB], writes=[GB])
                v_some(1)
                sc.op(ACT, lambda e: e.activation(out=gsm[:, 2, 1:32], in_=gsm[:, 3, 1:32], func=AF.Exp), reads=[GB], writes=[GB])
                v_some(1)
                Mend_bc = gsm[:, 1, :].unsqueeze(2).to_broadcast([4, 32, 64])
                sc.op(DVE, lambda e: e.tensor_tensor(out=X1v, in0=X1v, in1=Mend_bc, op=ALU.subtract), reads=[GB], writes=[GB])
                v_some(1)
                sc.op(DVE, lambda e: e.tensor_scalar(out=X1[:], in0=X1[:], scalar1=-0.5 * math.log(128.0), scalar2=None, op0=ALU.add), reads=[GB], writes=[GB])
                v_some(1)
                sc.op(ACT, lambda e: e.activation(out=X1[:], in_=X1[:], func=AF.Exp), reads=[GB], writes=[GB])
                v_some(1)
                sc.op(DVE, lambda e: e.tensor_tensor(out=X3v, in0=X3v, in1=Mend_bc, op=ALU.add), reads=[GB], writes=[GB])
                v_some(1)
                sc.op(ACT, lambda e: e.activation(out=X3[:], in_=X3[:], func=AF.Exp, scale=-1.0), reads=[GB], writes=[GB])
                v_some(1)
                if stop_after == "C1":
                    for c, (src, n) in enumerate([(X1[:], S), (X3[:], S), (X4[:], S), (gsm[:].rearrange("p a b -> p (a b)"), 128)]):
                        db = Buf("dbgC%d" % c)
                        sc.dma(SP, dbg_t[c * 4:(c + 1) * 4, 0:n], src, reads=[GB], writes=[db])
                        outs_final.append(db)
                    break
                bk, bb = bank()
                for j in range(4):
                    sc.op(PE, lambda e, bk=bk, j=j: e.matmul(bk[:, j * 32:(j + 1) * 32], lhsT=onehot[:, j * 128:(j + 1) * 128], rhs=gsm[:, 2, :], start=True, stop=True), reads=[GB, CONST], writes=[bb], inc=(j == 3))
                sc.op(DVE, lambda e, bk=bk: e.tensor_copy(out=gamB[:].rearrange("p j c -> p (j c)"), in_=bk[:, 0:128]), reads=[bb], writes=[gamB_b])
                bk, bb = bank()
                for T in range(NT):
                    sc.op(PE, lambda e, bk=bk, T=T: e.transpose(out=bk[:, T * 4:(T + 1) * 4], in_=X3[:, T * 128:(T + 1) * 128], identity=identf[0:4, 0:4]), reads=[GB, CONST], writes=[bb], inc=(T == NT - 1))
                sc.op(DVE, lambda e, bk=bk: e.tensor_copy(out=fltok[:].rearrange("p t j -> p (t j)"), in_=bk[:, 0:64]), reads=[bb], writes=[fltok_b])
                v_some(NT)
                def head_gen(j):
                    wk = j % 2
                    qTe, qTo, ktT, kttok = qTe2[j % 2], qTo2[j % 2], ktT2[j % 2], kttok2[j % 2]
                    q_b, ktT_b, kttok_b = q_b2[j % 2], ktT_b2[j % 2], kttok_b2[j % 2]
                    if j > 0:
                        load_wqk(j)
                    for qk in range(2):
                        chb = qk * 4 + j
                        cp = cpre[qk]
                        cpb = cpre_b[qk]
                        for tap in range(4):
                            sc.op(DVE, lambda e, tap=tap, chb=chb, qk=qk: e.tensor_scalar(out=dg[qk][:, tap, :], in0=ident[:], scalar1=cwm[:, chb, tap:tap + 1], scalar2=None, op0=ALU.mult), reads=[CONST], writes=[dg_b[qk]])
                        for cb in range(4):
                            cs = slice(cb * 512, (cb + 1) * 512)
                            bk, bb = bank()
                            for kc in range(8):
                                sc.op(PE, lambda e, bk=bk, kc=kc, cs=cs, wk=wk, qk=qk: e.matmul(bk[:, :], lhsT=wqk[wk][:, kc, qk * 128:(qk + 1) * 128], rhs=xnT[:, kc, cs], start=(kc == 0), stop=(kc == 7)),
                                      reads=[wqk_b[wk][qk]] + xnT_b[cb * 4:cb * 4 + 4], writes=[bb], inc=(kc == 7))
                            if cb % 2 == 0:
                                sc.op(ACT, lambda e, bk=bk, cb=cb, cp=cp: e.activation(out=cp[:, 3 + cb * 512:3 + (cb + 1) * 512], in_=bk[:, :], func=AF.Copy), reads=[bb], writes=[cpb])
                            else:
                                sc.op(DVE, lambda e, bk=bk, cb=cb, cp=cp: e.tensor_copy(out=cp[:, 3 + cb * 512:3 + (cb + 1) * 512], in_=bk[:, :]), reads=[bb], writes=[cpb])
                            yield None
                    for qk in range(2):
                        chb = qk * 4 + j
                        cp = cpre[qk]
                        cpb = cpre_b[qk]
                        for cb in range(4):
                            cs = slice(cb * 512, (cb + 1) * 512)
                            bk, bb = bank()
                            for tap in range(4):
                                sc.op(PE, lambda e, bk=bk, tap=tap, cb=cb, cp=cp, qk=qk: e.matmul(bk[:, :], lhsT=dg[qk][:, tap, :], rhs=cp[:, 3 + cb * 512 - tap:3 + (cb + 1) * 512 - tap], start=(tap == 0), stop=(tap == 3)),
                                      reads=[dg_b[qk], cpb], writes=[bb], inc=(tap == 3))
                            if qk == 0:
                                bv = bk[:, :].rearrange("p (t two l) -> p t two l", two=2, l=64)
                                sc.op(ACT, lambda e, bv=bv, cs=cs, chb=chb: e.activation(out=qTe[:, cs].rearrange("p (t two l) -> p t two l", two=2, l=64)[:, :, 0, :], in_=bv[:, :, 0, :], func=AF.Silu, bias=cbm[:, chb:chb + 1], scale=1.0), reads=[bb, CONST], writes=[q_b])
                                sc.op(ACT, lambda e, bv=bv, cs=cs, chb=chb: e.activation(out=qTo[:, cs].rearrange("p (t two l) -> p t two l", two=2, l=64)[:, :, 1, :], in_=bv[:, :, 1, :], func=AF.Silu, bias=cbm[:, chb:chb + 1], scale=1.0), reads=[bb, CONST], writes=[q_b])
                            else:
                                sc.op(ACT, lambda e, bk=bk, cs=cs, chb=chb: e.activation(out=cacc[:, cs], in_=bk[:, :], func=AF.Silu, bias=cbm[:, chb:chb + 1], scale=1.0), reads=[bb, CONST], writes=[cacc_b])
                                bk2, bb2 = bank()
                                sc.op(PE, lambda e, bk2=bk2, cs=cs, j=j: e.matmul(bk2[:, :], lhsT=onehot[:, j * 128:(j + 1) * 128], rhs=X1[:, cs], start=True, stop=True), reads=[GB, CONST], writes=[bb2])
                                sc.op(DVE, lambda e, bk2=bk2, cs=cs: e.tensor_tensor(out=ktT[:, cs], in0=bk2[:, :], in1=cacc[:, cs], op=ALU.mult), reads=[bb2, cacc_b], writes=[ktT_b])
                            yield None
                    for half in range(2):
                        bk, bb = bank()
                        bkv = bk[:].bitcast(BF16)
                        for i in range(8):
                            T = half * 8 + i
                            sc.op(PE, lambda e, bkv=bkv, i=i, T=T: e.transpose(out=bkv[:, i * 128:(i + 1) * 128], in_=ktT[:, T * 128:(T + 1) * 128], identity=ident[:]), reads=[ktT_b, CONST], writes=[bb], inc=(i == 7))
                        sc.op(ACT, lambda e, bkv=bkv, half=half: e.activation(out=kttok[:, half * 8:(half + 1) * 8, :], in_=bkv[:, 0:1024].rearrange("p (t c) -> p t c", t=8), func=AF.Copy), reads=[bb], writes=[kttok_b])
                    yield "EARLY_DONE"
                    sc.op(POOL, lambda e: e.memset(Cst[:, 0, :], 0.0), writes=[Cst_r[0]])
                    sc.op(POOL, lambda e: e.memset(Chat[:, 0, :], 0.0), reads=[], writes=[Chat_b])
                    for c in range(31):
                        T, par = c // 2, c % 2
                        pr = slice(par * 64, (par + 1) * 64)
                        bk, bb = bank()
                        sc.op(PE, lambda e, bk=bk, T=T, pr=pr, j=j: e.matmul(bk[:, 0:129], lhsT=kttok[pr, T, :], rhs=vaug[pr, T, j, :], start=True, stop=True), reads=[kttok_b, vaug_b], writes=[bb])
                        sc.op(DVE, lambda e, bk=bk, c=c, j=j: e.scalar_tensor_tensor(out=Cst[:, (c + 1) % 4, :], in0=Cst[:, c % 4, :], scalar=gamB[:, j, c:c + 1], in1=bk[:, 0:129], op0=ALU.mult, op1=ALU.add), reads=[bb, Cst_r[c % 4], gamB_b], writes=[Cst_r[(c + 1) % 4]])
                        sc.op(ACT, lambda e, c=c, j=j: e.activation(out=Chat[:, c + 1, :], in_=Cst[:, (c + 1) % 4, :], func=AF.Copy, scale=gamB[:, j, c + 1:c + 2]), reads=[Cst_r[(c + 1) % 4], gamB_b], writes=[Chat_b])
                        yield None
                    def tile1(T):
                        ts_ = slice(T * 128, (T + 1) * 128)
                        bk, bb = bank()
                        sc.op(PE, lambda e: e.matmul(bk[:, 0:128], lhsT=ktT[:, ts_], rhs=qTe[:, ts_], start=True, stop=False), reads=[ktT_b, q_b], writes=[bb], inc=False)
                        sc.op(PE, lambda e: e.matmul(bk[:, 0:128], lhsT=ktT[:, ts_], rhs=qTo[:, ts_], start=False, stop=True), reads=[ktT_b, q_b], writes=[bb])
                        pk = T % 3
                        sc.op(DVE, lambda e: e.tensor_tensor(out=Pm[pk][:], in0=bk[:, 0:128], in1=cmask[:], op=ALU.mult), reads=[bb, CONST], writes=[Pm_b[pk]])

                    def tile2(T, j=j):
                        ts_ = slice(T * 128, (T + 1) * 128)
                        pk = T % 3
                        bk2, bb2 = bank()
                        sc.op(PE, lambda e: e.matmul(bk2[:, 0:129], lhsT=qTe[:, ts_], rhs=Chat[:, 2 * T, :], start=True, stop=False), reads=[q_b, Chat_b], writes=[bb2], inc=False)
                        sc.op(PE, lambda e: e.matmul(bk2[:, 0:129], lhsT=qTo[:, ts_], rhs=Chat[:, 2 * T + 1, :], start=False, stop=False), reads=[q_b, Chat_b], writes=[bb2], inc=False)
                        sc.op(PE, lambda e: e.matmul(bk2[:, 0:129], lhsT=Pm[pk][:], rhs=vaug[:, T, j, :], start=False, stop=True), reads=[Pm_b[pk], vaug_b], writes=[bb2])
                        sc.op(ACT, lambda e: e.activation(out=dn[:, T, 0:1], in_=bk2[:, 128:129], func=AF.Abs), reads=[bb2], writes=[dn_b[T]])
                        sc.op(DVE, lambda e: e.tensor_tensor(out=dn[:, T, 0:1], in0=dn[:, T, 0:1], in1=fltok[:, T, j:j + 1], op=ALU.max), reads=[dn_b[T], fltok_b], writes=[dn_b[T]])
                        sc.op(DVE, lambda e: e.reciprocal(out=dn[:, T, 1:2], in_=dn[:, T, 0:1]), reads=[dn_b[T]], writes=[dn_b[T]])
                        sc.op(DVE, lambda e: e.tensor_scalar(out=hmb[:, T, :], in0=bk2[:, 0:128], scalar1=dn[:, T, 1:2], scalar2=None, op0=ALU.mult), reads=[bb2, dn_b[T]], writes=[hmb_t[T]])
                        sc.op(ACT, lambda e: e.activation(out=junkm[:], in_=hmb[:, T, :], func=AF.Square, accum_out=ssm[:, T:T + 1]), reads=[hmb_t[T]], writes=[ssm_b])
                    for i in range(NT + 2):
                        if i < NT:
                            tile1(i)
                        if i >= 2:
                            tile2(i - 2)
                        yield None
                    sc.op(ACT, lambda e: e.activation(out=sqm[:], in_=ssm[:], func=AF.Sqrt, scale=1.0 / 128.0, bias=eps_t[:]), reads=[ssm_b, CONST], writes=[ssm_b])
                    sc.op(DVE, lambda e: e.reciprocal(out=rsm[:], in_=sqm[:]), reads=[ssm_b], writes=[ssm_b])
                    bkT, bbT = bank_long()
                    bkTv = bkT[:].bitcast(BF16)

                    def og1(T, j=j, wk=wk):
                        k2 = T % 3
                        bk, bb = bank()
                        for kc in range(8):
                            sc.op(PE, lambda e, kc=kc: e.matmul(bk[:, 0:128], lhsT=xnT[:, kc, T * 128:(T + 1) * 128], rhs=wqk[wk][:, kc, 256:384], start=(kc == 0), stop=(kc == 7)),
                                  reads=[wqk_b[wk][2], xnT_b[T]], writes=[bb], inc=(kc == 7))
                        sc.op(ACT, lambda e: e.activation(out=sg[k2][:], in_=bk[:, 0:128], func=AF.Sigmoid), reads=[bb], writes=[sg_b[k2]])
                        sc.op(DVE, lambda e: e.scalar_tensor_tensor(out=t1[k2][:], in0=hmb[:, T, :], scalar=rsm[:, T:T + 1], in1=hgbc[:, j * 128:(j + 1) * 128], op0=ALU.mult, op1=ALU.mult), reads=[hmb_t[T], ssm_b, CONST], writes=[t1_b[k2]])
                        sc.op(DVE, lambda e: e.tensor_tensor(out=t2[k2][:], in0=t1[k2][:], in1=sg[k2][:], op=ALU.mult), reads=[t1_b[k2], sg_b[k2]], writes=[t2_b[k2]])

                    def og2(T, j=j):
                        k2 = T % 3
                        i = T % 8
                        half = T // 8
                        sc.op(PE, lambda e: e.transpose(out=bkTv[:, i * 128:(i + 1) * 128], in_=t2[k2][:], identity=ident[:]), reads=[t2_b[k2], CONST], writes=[bbT])
                        if i == 7:
                            sc.op(ACT, lambda e: e.activation(out=catT[:, 4 + j, half * 1024:(half + 1) * 1024], in_=bkTv[:, 0:1024], func=AF.Copy), reads=[bbT], writes=[catT_b[4 + j]])
                    for i in range(NT + 2):
                        if i < NT:
                            og1(i)
                        if i >= 2:
                            og2(i - 2)
                        yield None


                gens = [head_gen(j) for j in range(4)]

                def run_early(g):
                    for v in g:
                        if v == "EARLY_DONE":
                            return
                run_early(gens[0])
                for j in range(4):
                    nxt = gens[j + 1] if j + 1 < 4 else None
                    nxt_done = nxt is None
                    cur_done = False
                    while not (cur_done and nxt_done):
                        if not cur_done:
                            try:
                                next(gens[j])
                            except StopIteration:
                                cur_done = True
                        if not nxt_done:
                            try:
                                v = next(nxt)
                                if v == "EARLY_DONE":
                                    nxt_done = True
                            except StopIteration:
                                nxt_done = True

            if stop_after in ("C", "C1", "C2", "C3"):
                break

            with contextlib.ExitStack() as esD:
                sc.barrier()

                def sbD(name, shape, dt):
                    return esD.enter_context(nc.sbuf_tensor("%s_s%d" % (name, si), list(shape), dt))
                h_sb = sbD("h_sb", (128, NT, D), F32)
                h_b = [Buf("h%d" % t) for t in range(NT)]
                wbig = sbD("wbig", (128, 8, 1024), BF16)
                wbig_h = [Buf("wbig_h0"), Buf("wbig_h1")]
                wbig_b = wbig_h

                def load_wbig(src_v):
                    n = src_v.shape[1]
                    for hf in range(2):
                        sc.dma(POOL, wbig[:, 0:n, hf * 512:(hf + 1) * 512], src_v[:, :, hf * 512:(hf + 1) * 512], writes=[wbig_h[hf]])

                def out_proj(nk, in_bufs, fused=None):
                    for T in range(NT):
                        for nb in range(2):
                            bk, bb = bank()
                            for kc in range(nk):
                                sc.op(PE, lambda e, bk=bk, kc=kc, T=T, nb=nb: e.matmul(bk[:, :], lhsT=catT[:, kc, T * 128:(T + 1) * 128], rhs=wbig[:, kc, nb * 512:(nb + 1) * 512], start=(kc == 0), stop=(kc == nk - 1)),
                                      reads=[wbig_h[nb]] + in_bufs, writes=[bb], inc=(kc == nk - 1))
                            sc.op(DVE, lambda e, bk=bk, T=T, nb=nb: e.tensor_tensor(out=h_sb[:, T, nb * 512:(nb + 1) * 512], in0=bk[:, :], in1=h_sb[:, T, nb * 512:(nb + 1) * 512], op=ALU.add),
                                  reads=[bb, h_b[T]], writes=[h_b[T]])
                        if fused is not None:
                            next(fused, None)
                    if fused is not None:
                        for _ in fused:
                            pass

                def srcH(t):
                    return h_sb[:, t, :], [h_b[t]]

                load_wbig(w_mo.rearrange("(kc p) n -> p kc n", p=128))
                for T in range(NT):
                    sc.dma(SP, h_sb[:, T, :], x[si, T * 128:(T + 1) * 128, :], writes=[h_b[T]])
                if stop_after == "D":
                    out_proj(8, catT_b)
                else:
                    out_proj(8, catT_b, fused=norm_gen("xa", srcH, g_xa, xnT, xnT_b, NT, nscr))

                stopped = False
                if stop_after != "D":
                    with contextlib.ExitStack() as esE:
                        def sbE(name, shape, dt):
                            return esE.enter_context(nc.sbuf_tensor("%s_s%d" % (name, si), list(shape), dt))
                        xm = [sbE("xm%d" % i, (128, D), F32) for i in range(2)]
                        xm_b = [Buf("xm%d" % i) for i in range(2)]
                        memnT = sbE("memnT", (128, 8, NMEM), BF16)
                        memn_b = [Buf("memn%d" % i) for i in range(2)]
                        kTx = sbE("kTx", (128, 8, NMEM), BF16)
                        kTx_b = Buf("kTx")
                        vtokx = sbE("vtokx", (128, 2, D), BF16)
                        vtokx_b = Buf("vtokx")
                        wq = [sbE("wq%d" % i, (128, 8, 256), BF16) for i in range(2)]
                        wq_b = [Buf("wq%d" % i) for i in range(2)]
                        qTx = sbE("qTx", (128, 2, S), BF16)
                        qTx_b = Buf("qTx")
                        qTx_c = [[Buf("qTx%d_%d" % (a_, b_)) for b_ in range(4)] for a_ in range(2)]
                        pTx = [sbE("pTx%d" % i, (128, 512), BF16) for i in range(4)]
                        pTx_b = [Buf("pTx%d" % i) for i in range(4)]
                        recx = [sbE("recx%d" % i, (128, 512), F32) for i in range(2)]
                        recx_b = [Buf("recx%d" % i) for i in range(2)]
                        ones_bf = sbE("ones_bf", (128, 128), BF16)
                        ones_bf_b = Buf("ones_bf")
                        sc.op(POOL, lambda e: e.memset(ones_bf[:], 1.0), writes=[ones_bf_b])

                        def srcM(t):
                            sc.dma(SP, xm[t][:], mem[si, t * 128:(t + 1) * 128, :], writes=[xm_b[t]])
                            return xm[t][:], [xm_b[t]]
                        norm_T("mem", srcM, g_mem, memnT, memn_b, 2, nscr)
                        w_xkv_v = w_xkv.rearrange("(kc p) n -> p kc n", p=128)
                        load_wbig(w_xkv_v[:, :, 0:1024])
                        for oc in range(8):
                            bk, bb = bank()
                            for kc in range(8):
                                sc.op(PE, lambda e, bk=bk, kc=kc, oc=oc: e.matmul(bk[:, 0:NMEM], lhsT=wbig[:, kc, oc * 128:(oc + 1) * 128], rhs=memnT[:, kc, :], start=(kc == 0), stop=(kc == 7)),
                                      reads=[wbig_h[oc // 4]] + memn_b, writes=[bb], inc=(kc == 7))
                            sc.op(ACT, lambda e, bk=bk, oc=oc: e.activation(out=kTx[:, oc, :], in_=bk[:, 0:NMEM], func=AF.Copy), reads=[bb], writes=[kTx_b])
                        load_wbig(w_xkv_v[:, :, 1024:2048])
                        for mt in range(2):
                            for nb in range(2):
                                bk, bb = bank()
                                for kc in range(8):
                                    sc.op(PE, lambda e, bk=bk, kc=kc, mt=mt, nb=nb: e.matmul(bk[:, :], lhsT=memnT[:, kc, mt * 128:(mt + 1) * 128], rhs=wbig[:, kc, nb * 512:(nb + 1) * 512], start=(kc == 0), stop=(kc == 7)),
                                          reads=[wbig_h[nb], memn_b[mt]], writes=[bb], inc=(kc == 7))
                                sc.op(DVE, lambda e, bk=bk, mt=mt, nb=nb: e.tensor_copy(out=vtokx[:, mt, nb * 512:(nb + 1) * 512], in_=bk[:, :]), reads=[bb], writes=[vtokx_b])
                        w_xq_v = w_xq.rearrange("(kc p) n -> p kc n", p=128)
                        pxi = 0
                        for hh in range(4):
                            wk = hh % 2
                            sc.dma(POOL, wq[wk][:], w_xq_v[:, :, hh * 256:(hh + 1) * 256], writes=[wq_b[wk]])
                            for c2 in range(2):
                                for cb in range(4):
                                    cs = slice(cb * 512, (cb + 1) * 512)
                                    bk, bb = bank()
                                    for kc in range(8):
                                        sc.op(PE, lambda e, bk=bk, kc=kc, c2=c2, cs=cs, wk=wk: e.matmul(bk[:, :], lhsT=wq[wk][:, kc, c2 * 128:(c2 + 1) * 128], rhs=xnT[:, kc, cs], start=(kc == 0), stop=(kc == 7)),
                                              reads=[wq_b[wk]] + xnT_b[cb * 4:cb * 4 + 4], writes=[bb], inc=(kc == 7))
                                    if cb % 2 == 0:
                                        sc.op(ACT, lambda e, bk=bk, c2=c2, cs=cs: e.activation(out=qTx[:, c2, cs], in_=bk[:, :], func=AF.Copy), reads=[bb], writes=[qTx_c[c2][cb]])
                                    else:
                                        sc.op(DVE, lambda e, bk=bk, c2=c2, cs=cs: e.tensor_copy(out=qTx[:, c2, cs], in_=bk[:, :]), reads=[bb], writes=[qTx_c[c2][cb]])
                            for cb in range(4):
                                cs = slice(cb * 512, (cb + 1) * 512)
                                pks = []
                                for mt in range(2):
                                    bk, bb = bank()
                                    for c2 in range(2):
                                        sc.op(PE, lambda e, bk=bk, c2=c2, mt=mt, cs=cs, hh=hh: e.matmul(bk[:, :], lhsT=kTx[:, hh * 2 + c2, mt * 128:(mt + 1) * 128], rhs=qTx[:, c2, cs], start=(c2 == 0), stop=(c2 == 1)),
                                              reads=[kTx_b, qTx_c[c2][cb]], writes=[bb], inc=(c2 == 1))
                                    pk = pxi % 4
                                    pxi += 1
                                    pks.append(pk)
                                    sc.op(ACT, lambda e, bk=bk, pk=pk: e.activation(out=pTx[pk][:], in_=bk[:, :], func=AF.Exp, scale=1.0 / 16.0), reads=[bb], writes=[pTx_b[pk]])
                                bk, bb = bank()
                                for mt in range(2):
                                    sc.op(PE, lambda e, bk=bk, mt=mt, pk=pks[mt]: e.matmul(bk[:, :], lhsT=ones_bf[:], rhs=pTx[pk][:], start=(mt == 0), stop=(mt == 1)),
                                          reads=[ones_bf_b, pTx_b[pks[mt]]], writes=[bb], inc=(mt == 1))
                                rk = cb % 2
                                sc.op(ACT, lambda e, bk=bk, rk=rk: e.activation(out=recx[rk][:], in_=bk[:, :], func=AF.Ln), reads=[bb], writes=[recx_b[rk]])
                                sc.op(ACT, lambda e, rk=rk: e.activation(out=recx[rk][:], in_=recx[rk][:], func=AF.Exp, scale=-1.0), reads=[recx_b[rk]], writes=[recx_b[rk]])
                                for c2 in range(2):
                                    bk, bb = bank()
                                    for mt in range(2):
                                        sc.op(PE, lambda e, bk=bk, mt=mt, c2=c2, hh=hh, pk=pks[mt]: e.matmul(bk[:, :], lhsT=vtokx[:, mt, hh * 256 + c2 * 128:hh * 256 + (c2 + 1) * 128], rhs=pTx[pk][:], start=(mt == 0), stop=(mt == 1)),
                                              reads=[vtokx_b, pTx_b[pks[mt]]], writes=[bb], inc=(mt == 1))
                                    sc.op(DVE, lambda e, bk=bk, rk=rk, c2=c2, hh=hh, cs=cs: e.tensor_tensor(out=catT[:, hh * 2 + c2, cs], in0=bk[:, :], in1=recx[rk][:], op=ALU.mult),
                                          reads=[bb, recx_b[rk]], writes=[catT_b[hh * 2 + c2]])
                        load_wbig(w_xo.rearrange("(kc p) n -> p kc n", p=128))
                        if stop_after == "E":
                            out_proj(8, catT_b)
                        else:
                            out_proj(8, catT_b, fused=norm_gen("ffn", srcH, g_ffn, xnT, xnT_b, NT, nscr))

                if stop_after not in ("D", "E"):
                    with contextlib.ExitStack() as esF:
                        sc.barrier()
                        def sbF(name, shape, dt):
                            return esF.enter_context(nc.sbuf_tensor("%s_s%d" % (name, si), list(shape), dt))
                        wup = [sbF("wup%d" % i, (128, 8, 256), BF16) for i in range(3)]
                        wupg_b = [Buf("wupg%d" % i) for i in range(3)]
                        wupu_b = [Buf("wupu%d" % i) for i in range(3)]
                        gpre = [sbF("gpre%d" % i, (128, S + 2), BF16) for i in range(2)]
                        gpre_b = [Buf("gpre%d" % i) for i in range(2)]
                        gact = [sbF("gact%d" % i, (128, S), BF16) for i in range(2)]
                        gact_b = [Buf("gact%d" % i) for i in range(2)]
                        dgf = [sbF("dgf%d" % i, (128, 3, 128), BF16) for i in range(2)]
                        dgf_b = [Buf("dgf%d" % i) for i in range(2)]
                        for i in range(2):
                            sc.op(POOL, lambda e, i=i: e.memset(gpre[i][:, 0:2], 0.0), writes=[gpre_b[i]])
                        otF = [sbF("otF%d" % i, (128, D), F32) for i in range(2)]
                        otF_b = [Buf("otF%d" % i) for i in range(2)]
                        gbF = sbF("gbF", (128, D), F32)
                        junkF = sbF("junkF", (128, D), BF16)

                        def final_gen():
                            sc.dma(SP, gbF[:], g_fin.partition_broadcast(128), writes=[gb_b])
                            for T in range(NT):
                                k = T % 2
                                yb = Buf("y%d_%d" % (si, T))
                                sc.op(ACT, lambda e, T=T: e.activation(out=junkF[:], in_=h_sb[:, T, :], func=AF.Square, accum_out=ss_t[:, T:T + 1]), reads=[h_b[T]], writes=[scr_b])
                                sc.op(ACT, lambda e, T=T: e.activation(out=sq_t[:, T:T + 1], in_=ss_t[:, T:T + 1], func=AF.Sqrt, scale=1.0 / D, bias=eps_t[:]), reads=[scr_b, CONST], writes=[scr_b])
                                sc.op(DVE, lambda e, T=T: e.reciprocal(out=rstd_t[:, T:T + 1], in_=sq_t[:, T:T + 1]), reads=[scr_b], writes=[scr_b2])
                                sc.op(DVE, lambda e, T=T, k=k: e.scalar_tensor_tensor(out=otF[k][:], in0=h_sb[:, T, :], scalar=rstd_t[:, T:T + 1], in1=gbF[:], op0=ALU.mult, op1=ALU.mult), reads=[h_b[T], scr_b2, gb_b], writes=[otF_b[k]])
                                sc.dma(SP, y[si, T * 128:(T + 1) * 128, :], otF[k][:], reads=[otF_b[k]], writes=[yb])
                                outs_final.append(yb)
                                yield T
                        w_up_v = w_up.rearrange("(kc p) n -> p kc n", p=128)

                        def f_gate(fc):
                            k = fc % 2
                            w3 = fc % 3
                            sc.dma(POOL, wup[w3][:, :, 0:128], w_up_v[:, :, fc * 128:(fc + 1) * 128], writes=[wupg_b[w3]])
                            sc.dma(POOL, wup[w3][:, :, 128:256], w_up_v[:, :, DFF + fc * 128:DFF + (fc + 1) * 128], writes=[wupu_b[w3]])
                            for tap in range(3):
                                sc.op(DVE, lambda e, tap=tap: e.tensor_scalar(out=dgf[k][:, tap, :], in0=ident[:], scalar1=cwf[:, fc, tap:tap + 1], scalar2=None, op0=ALU.mult), reads=[CONST], writes=[dgf_b[k]])
                            for cb in range(4):
                                cs = slice(cb * 512, (cb + 1) * 512)
                                bk, bb = bank()
                                for kc in range(8):
                                    sc.op(PE, lambda e, bk=bk, kc=kc, cs=cs: e.matmul(bk[:, :], lhsT=wup[w3][:, kc, 0:128], rhs=xnT[:, kc, cs], start=(kc == 0), stop=(kc == 7)),
                                          reads=[wupg_b[w3]] + xnT_b[cb * 4:cb * 4 + 4], writes=[bb], inc=(kc == 7))
                                sc.op(ACT, lambda e, bk=bk, cb=cb: e.activation(out=gpre[k][:, 2 + cb * 512:2 + (cb + 1) * 512], in_=bk[:, :], func=AF.Copy), reads=[bb], writes=[gpre_b[k]])

                        def f_rest(fc, f0):
                            k = fc % 2
                            w3 = fc % 3
                            for cb in range(4):
                                cs = slice(cb * 512, (cb + 1) * 512)
                                bk, bb = bank()
                                for tap in range(3):
                                    sc.op(PE, lambda e, bk=bk, tap=tap, cb=cb: e.matmul(bk[:, :], lhsT=dgf[k][:, tap, :], rhs=gpre[k][:, 2 + cb * 512 - tap:2 + (cb + 1) * 512 - tap], start=(tap == 0), stop=(tap == 2)),
                                          reads=[dgf_b[k], gpre_b[k]], writes=[bb], inc=(tap == 2))
                                sc.op(ACT, lambda e, bk=bk, cs=cs: e.activation(out=gact[k][:, cs], in_=bk[:, :], func=AF.Silu, bias=cbf[:, fc:fc + 1], scale=1.0), reads=[bb, CONST], writes=[gact_b[k]])
                            for cb in range(4):
                                cs = slice(cb * 512, (cb + 1) * 512)
                                bk, bb = bank()
                                for kc in range(8):
                                    sc.op(PE, lambda e, bk=bk, kc=kc, cs=cs: e.matmul(bk[:, :], lhsT=wup[w3][:, kc, 128:256], rhs=xnT[:, kc, cs], start=(kc == 0), stop=(kc == 7)),
                                          reads=[wupu_b[w3]] + xnT_b[cb * 4:cb * 4 + 4], writes=[bb], inc=(kc == 7))
                                sc.op(DVE, lambda e, bk=bk, cs=cs: e.tensor_tensor(out=catT[:, fc - f0, cs], in0=bk[:, :], in1=gact[k][:, cs], op=ALU.mult), reads=[bb, gact_b[k]], writes=[catT_b[fc - f0]])

                        f_gate(0)
                        for (f0, f1) in ((0, 8), (8, 16), (16, 22)):
                            ng = f1 - f0
                            load_wbig(w_dn[f0 * 128:f1 * 128, :].rearrange("(fc p) n -> p fc n", p=128))
                            for fc in range(f0, f1):
                                if fc + 1 < NFF:
                                    f_gate(fc + 1)
                                f_rest(fc, f0)
                            if f1 == NFF and stop_after != "F":
                                out_proj(ng, catT_b[0:ng], fused=final_gen())
                            else:
                                out_proj(ng, catT_b[0:ng])

                if stop_after in ("D", "E", "F"):
                  with contextlib.ExitStack() as esG:
                    sc.barrier()
                    ot = [esG.enter_context(nc.sbuf_tensor("ot%d_s%d" % (i, si), [128, D], F32)) for i in range(2)]
                    ot_b = [Buf("ot%d" % i) for i in range(2)]
                    raw = stop_after in ("D", "E", "F")
                    gb1 = esG.enter_context(nc.sbuf_tensor("gb1G_s%d" % si, [128, D], F32))
                    junk = esG.enter_context(nc.sbuf_tensor("junkG_s%d" % si, [128, D], BF16))
                    if not raw:
                        sc.dma(SP, gb1[:], g_fin.partition_broadcast(128), writes=[gb_b])
                    for T in range(NT):
                        yb = Buf("y%d_%d" % (si, T))
                        if raw:
                            sc.dma(SP, y[si, T * 128:(T + 1) * 128, :], h_sb[:, T, :], reads=[h_b[T]], writes=[yb])
                        else:
                            k = T % 2
                            sc.op(ACT, lambda e, T=T: e.activation(out=junk[:], in_=h_sb[:, T, :], func=AF.Square, accum_out=ss_t[:, T:T + 1]), reads=[h_b[T]], writes=[scr_b])
                            sc.op(ACT, lambda e, T=T: e.activation(out=sq_t[:, T:T + 1], in_=ss_t[:, T:T + 1], func=AF.Sqrt, scale=1.0 / D, bias=eps_t[:]), reads=[scr_b, CONST], writes=[scr_b])
                            sc.op(DVE, lambda e, T=T: e.reciprocal(out=rstd_t[:, T:T + 1], in_=sq_t[:, T:T + 1]), reads=[scr_b], writes=[scr_b2])
                            sc.op(DVE, lambda e, T=T, k=k: e.scalar_tensor_tensor(out=ot[k][:], in0=h_sb[:, T, :], scalar=rstd_t[:, T:T + 1], in1=gb1[:], op0=ALU.mult, op1=ALU.mult), reads=[h_b[T], scr_b2, gb_b], writes=[ot_b[k]])
                            sc.dma(SP, y[si, T * 128:(T + 1) * 128, :], ot[k][:], reads=[ot_b[k]], writes=[yb])
                        outs_final.append(yb)
            if stop_after in ("D", "E", "F"):
                break

        fin = []
        if dbg and stop_after in ("B", "C"):
            stg = sb("stg", (128, S), F32)
            stg_b = Buf("stg")
            for c in range(8):
                sc.op(DVE, lambda e, c=c: e.tensor_copy(out=stg[:], in_=catT[:, c, :]), reads=[catT_b[c]] + xnT_b, writes=[stg_b])
                db = Buf("dbgout%d" % c)
                sc.dma(SP, dbg_t[c * 128:(c + 1) * 128, :], stg[:], reads=[stg_b], writes=[db])
                fin.append(db)
        sc.final_wait(SP, fin + outs_final)
        stuck = sc.check_deadlock()
        if stuck:
            raise RuntimeError("semaphore deadlock detected at build time: %r" % (stuck,))
        for sem in sc.all_sems():
            nc.sync.sem_clear(sem)
        nc.all_engine_barrier()
        blk = es.enter_context(nc.Block())
        sc.emit(blk)
    return nc


def _consts():
    ident = np.eye(128, dtype=np.float32)
    slopes = np.exp2(-8.0 * np.arange(1, 9, dtype=np.float64) / 8.0)
    kj = np.arange(128)[:, None]
    qi = np.arange(128)[None, :]
    E = np.zeros((128, 4, 3, 512), np.float32)
    for h in range(8):
        for di, d in enumerate(DIL):
            relp = qi - kj + 128
            prev = np.where(relp <= 128, np.exp(np.minimum(-slopes[h] * d * relp, 0.0)), 0.0)
            relc = qi - kj
            cur = np.where(relc >= 0, np.exp(np.minimum(-slopes[h] * d * relc, 0.0)), 0.0)
            hp, hh = h // 2, h % 2
            E[:, hp, di, hh * 128:(hh + 1) * 128] = prev
            E[:, hp, di, 256 + hh * 128:256 + (hh + 1) * 128] = cur
    E = E.reshape(128, 8 * 3 * 256)
    s_ = np.arange(128)[:, None]
    t_ = np.arange(128)[None, :]
    cm = ((s_ // 64 == t_ // 64) & (s_ <= t_)).astype(np.float32)
    rm = np.ones((4, S), np.float32)
    rm[:, 0::64] = 0.0
    oh = np.zeros((4, 4, 128), np.float32)
    for j in range(4):
        oh[j, j, :] = 1.0
    return dict(c_ident=ident, c_E=E, c_cmask=cm, c_rmask=rm, c_onehot=oh.reshape(4, 512))


_W_NAMES = ["norm_mix_g", "w_in", "mlstm_conv_w", "mlstm_conv_b", "mlstm_gate_b", "mlstm_head_g", "w_mix_out",
            "norm_xattn_g", "norm_mem_g", "w_xq", "w_xkv", "w_xo", "norm_ffn_g", "w_ffn_up", "ffn_conv_w",
            "ffn_conv_b", "w_ffn_down"]


def make_in_maps(inputs, n_cores, nseq):
    consts = _consts()
    shared = {}
    for k in _W_NAMES:
        shared[k] = np.ascontiguousarray(np.asarray(inputs[k], dtype=np.float32)[0])
    shared["norm_final_g"] = np.ascontiguousarray(np.asarray(inputs["norm_final_g"], dtype=np.float32))
    shared.update(consts)
    xs = np.asarray(inputs["x"], dtype=np.float32)
    ms = np.asarray(inputs["mem"], dtype=np.float32)
    maps = []
    for c in range(n_cores):
        m = dict(shared)
        m["x"] = np.ascontiguousarray(xs[c * nseq:(c + 1) * nseq])
        m["mem"] = np.ascontiguousarray(ms[c * nseq:(c + 1) * nseq])
        maps.append(m)
    return maps


def kernel(**inputs):
    nseq = inputs["x"].shape[0] // N_CORES
    nc = build(nseq)
    maps = make_in_maps(inputs, N_CORES, nseq)
    res = run_bass_kernel_spmd(nc, maps, core_ids=list(range(N_CORES)))
    return np.concatenate([r["y"] for r in res.results], axis=0)
```

```python
import contextlib
import math
import numpy as np
import concourse.bass as bass
import concourse.mybir as mybir
from concourse.bass_utils import run_bass_kernel_spmd

F32 = mybir.dt.float32
BF16 = mybir.dt.bfloat16
AF = mybir.ActivationFunctionType
ALU = mybir.AluOpType

D = 1024
S = 2048
NT = S // 128
NMEM = 256
A_W = 512
M_W = 512
IN_COLS = 3592
DFF = 2816
NFF = DFF // 128
EPS = 1e-6
N_CORES = 8
DIL = (1, 4, 16)


class Buf:
    __slots__ = ("name", "w", "r", "excl")

    def __init__(self, name, excl=False):
        self.name = name
        self.w = None
        self.r = []
        self.excl = excl


class Eng:
    def __init__(self, name, is_pe=False):
        self.name = name
        self.ops = []
        self.sem = None
        self.cnt = 0
        self.seen = {}
        self.is_pe = is_pe
        self.log = []


class Sched:
    def __init__(self, nc, es, nds=12):
        self.nc = nc
        self.pe = Eng("pe", True)
        self.dve = Eng("dve")
        self.act = Eng("act")
        self.pool = Eng("pool")
        self.sp = Eng("sp")
        self.engs = [self.pe, self.dve, self.act, self.pool, self.sp]
        for e in self.engs:
            e.sem = es.enter_context(nc.semaphore("sem_" + e.name))
        self.dsems = {}
        for q in (self.sp, self.pool, self.act):
            self.dsems[q.name] = [[es.enter_context(nc.semaphore("dma_%s%d" % (q.name, i))), 0] for i in range(nds if q is self.sp else 6)]
        self.dnext = {k: 0 for k in self.dsems}

    def all_sems(self):
        out = [e.sem for e in self.engs]
        for lst in self.dsems.values():
            out += [x[0] for x in lst]
        return out

    def _wait(self, eng, ev):
        sem, val = ev
        if eng.is_pe and sem is eng.sem:
            return
        if eng.seen.get(id(sem), 0) >= val:
            return
        eng.seen[id(sem)] = val
        eng.ops.append(lambda e, sem=sem, val=val: e.wait_ge(sem, val))
        eng.log.append(("w", id(sem), val))

    def _deps(self, eng, reads, writes):
        for b in reads:
            if b.w is not None:
                self._wait(eng, b.w)
            if b.excl:
                for ev in b.r:
                    if ev[0] is not eng.sem:
                        self._wait(eng, ev)
        for b in writes:
            if b.w is not None and (b.w[0] is not eng.sem or b in reads):
                self._wait(eng, b.w)
            for ev in b.r:
                if ev[0] is eng.sem:
                    continue
                self._wait(eng, ev)

    def op(self, eng, fn, reads=(), writes=(), inc=True):
        self._deps(eng, reads, writes)
        sem = eng.sem
        if inc:
            eng.cnt += 1
            ev = (sem, eng.cnt)
            eng.ops.append(lambda e, fn=fn, sem=sem: fn(e).then_inc(sem, 1))
            eng.log.append(("i", id(sem), 1))
        else:
            ev = (sem, eng.cnt + 1)
            eng.ops.append(lambda e, fn=fn: fn(e))
        for b in reads:
            b.r.append(ev)
        for b in writes:
            b.w = ev
            b.r = []
        return ev

    def dma(self, q, out_ap, in_ap, reads=(), writes=(), slow=False):
        self._deps(q, reads, writes)
        lst = self.dsems[q.name]
        i = self.dnext[q.name]
        self.dnext[q.name] = (i + 1) % len(lst)
        sem, prev = lst[i]
        if prev > 0:
            self._wait(q, (sem, prev))
        lst[i][1] = prev + 16
        ev = (sem, prev + 16)
        if slow:
            q.ops.append(lambda e, o=out_ap, a=in_ap, sem=sem: e.dma_start(out=o, in_=a, allow_slow_non_contiguous=True).then_inc(sem, 16))
        else:
            q.ops.append(lambda e, o=out_ap, a=in_ap, sem=sem: e.dma_start(out=o, in_=a).then_inc(sem, 16))
        q.log.append(("i", id(sem), 16))
        for b in reads:
            b.r.append(ev)
        for b in writes:
            b.w = ev
            b.r = []
        return ev

    def check_deadlock(self):
        vals = {}
        pos = {e.name: 0 for e in self.engs}
        progress = True
        while progress:
            progress = False
            for e in self.engs:
                while pos[e.name] < len(e.log):
                    kind, sid, v = e.log[pos[e.name]]
                    if kind == "w":
                        if vals.get(sid, 0) < v:
                            break
                    else:
                        vals[sid] = vals.get(sid, 0) + v
                    pos[e.name] += 1
                    progress = True
        stuck = {e.name: (pos[e.name], len(e.log)) for e in self.engs if pos[e.name] < len(e.log)}
        return stuck

    def barrier(self):
        evs = [(e.sem, e.cnt) for e in self.engs if e.cnt > 0]
        for lst in self.dsems.values():
            evs += [(x[0], x[1]) for x in lst if x[1] > 0]
        for e in self.engs:
            for ev in evs:
                self._wait(e, ev)

    def final_wait(self, eng, bufs):
        for b in bufs:
            if b.w is not None:
                self._wait(eng, b.w)

    def emit(self, blk):
        def run(eng):
            def f(e):
                for o in eng.ops:
                    o(e)
            return f
        blk.tensor(run(self.pe))
        blk.vector(run(self.dve))
        blk.scalar(run(self.act))
        blk.gpsimd(run(self.pool))
        blk.sync(run(self.sp))


def build(nseq, stop_after=None, dbg=False):
    nc = bass.Bass("TRN2", target_bir_lowering=False)
    dt_in = {}

    def din(name, shape):
        t = nc.dram_tensor(name, list(shape), F32, kind="ExternalInput").ap()
        dt_in[name] = t
        return t

    x = din("x", (nseq, S, D))
    mem = din("mem", (nseq, NMEM, D))
    g_mix = din("norm_mix_g", (D,))
    w_in = din("w_in", (D, IN_COLS))
    mconv_w = din("mlstm_conv_w", (4, 2 * M_W))
    mconv_b = din("mlstm_conv_b", (2 * M_W,))
    gate_b = din("mlstm_gate_b", (8,))
    head_g = din("mlstm_head_g", (M_W,))
    w_mo = din("w_mix_out", (D, D))
    g_xa = din("norm_xattn_g", (D,))
    g_mem = din("norm_mem_g", (D,))
    w_xq = din("w_xq", (D, D))
    w_xkv = din("w_xkv", (D, 2 * D))
    w_xo = din("w_xo", (D, D))
    g_ffn = din("norm_ffn_g", (D,))
    w_up = din("w_ffn_up", (D, 2 * DFF))
    fconv_w = din("ffn_conv_w", (3, DFF))
    fconv_b = din("ffn_conv_b", (DFF,))
    w_dn = din("w_ffn_down", (DFF, D))
    g_fin = din("norm_final_g", (D,))
    c_ident = din("c_ident", (128, 128))
    c_E = din("c_E", (128, 8 * 3 * 256))
    c_cmask = din("c_cmask", (128, 128))
    c_rmask = din("c_rmask", (4, S))
    c_onehot = din("c_onehot", (4, 4 * 128))
    y = nc.dram_tensor("y", [nseq, S, D], F32, kind="ExternalOutput").ap()
    dbg_t = None
    if dbg:
        dbg_t = nc.dram_tensor("dbg", [D, S], F32, kind="ExternalOutput").ap()

    with contextlib.ExitStack() as es:
        sc = Sched(nc, es)
        PE, DVE, ACT, POOL, SP = sc.pe, sc.dve, sc.act, sc.pool, sc.sp

        def sb(name, shape, dt):
            return es.enter_context(nc.sbuf_tensor(name, list(shape), dt))

        banks = [es.enter_context(nc.psum_tensor("bank%d" % i, [128, 512], F32)) for i in range(8)]
        bank_b = [Buf("bank%d" % i, excl=True) for i in range(8)]
        bank_i = [0]

        def bank():
            i = bank_i[0]
            bank_i[0] = (i + 1) % 7
            return banks[i], bank_b[i]

        def bank_long():
            return banks[7], bank_b[7]

        ident = sb("ident", (128, 128), BF16)
        identf = sb("identf", (128, 128), F32)
        cmask = sb("cmask", (128, 128), F32)
        onehot = sb("onehot", (4, 512), F32)
        onesf = sb("onesf", (128, 128), F32)
        gb_b = Buf("gb1")
        uniq = [0]
        hgbc = sb("hgbc", (128, M_W), F32)
        cwm = sb("cwm", (128, 8, 4), F32)
        cbm = sb("cbm", (128, 8), F32)
        cwf = sb("cwf", (128, NFF, 3), F32)
        cbf = sb("cbf", (128, NFF), F32)
        gbi = sb("gbi", (4, 1), F32)
        gbf = sb("gbf", (4, 1), F32)
        ngbf = sb("ngbf", (4, 1), F32)
        CONST = Buf("const")
        GBFB = Buf("gbf")
        sc.dma(POOL, ident[:], c_ident, writes=[Buf("c")])
        sc.dma(SP, identf[:], c_ident, writes=[Buf("c")])
        sc.dma(SP, cmask[:], c_cmask, writes=[Buf("c")])
        sc.dma(SP, onehot[:], c_onehot, writes=[Buf("c")])
        sc.dma(SP, hgbc[:], head_g.partition_broadcast(128), writes=[Buf("c")])
        for j in range(4):
            sc.dma(SP, cwm[:, :, j], mconv_w[j].rearrange("(b p) -> p b", p=128), writes=[Buf("c")], slow=True)
        sc.dma(SP, cbm[:], mconv_b.rearrange("(b p) -> p b", p=128), writes=[Buf("c")], slow=True)
        for j in range(3):
            sc.dma(SP, cwf[:, :, j], fconv_w[j].rearrange("(b p) -> p b", p=128), writes=[Buf("c")], slow=True)
        sc.dma(SP, cbf[:], fconv_b.rearrange("(b p) -> p b", p=128), writes=[Buf("c")], slow=True)
        sc.dma(SP, gbi[:], gate_b[0:4].rearrange("(p o) -> p o", o=1), writes=[Buf("c")], slow=True)
        sc.dma(SP, gbf[:], gate_b[4:8].rearrange("(p o) -> p o", o=1), writes=[GBFB], slow=True)
        sc.op(DVE, lambda e: e.memset(onesf[:], 1.0), writes=[CONST])
        sc.op(DVE, lambda e: e.tensor_scalar(out=ngbf[:], in0=gbf[:], scalar1=-1.0, scalar2=None, op0=ALU.mult), reads=[GBFB], writes=[CONST])

        xnT = sb("xnT", (128, 8, S), BF16)
        xnT_b = [Buf("xnT%d" % t) for t in range(NT)]
        catT = sb("catT", (128, 8, S), BF16)
        catT_b = [Buf("catT%d" % c) for c in range(8)]

        w_in_v = w_in.rearrange("(kc p) n -> p kc n", p=128)

        def norm_T(name, srcs, gi, dstT, dst_bufs, nt, scr):
            for _ in norm_gen(name, srcs, gi, dstT, dst_bufs, nt, scr):
                pass

        def norm_gen(name, srcs, gi, dstT, dst_bufs, nt, scr):
            _j, ss, sq, rstd, _x, xnb_b = scr
            uniq[0] += 1
            esN = contextlib.ExitStack()
            gb1 = esN.enter_context(nc.sbuf_tensor("gb1_%d" % uniq[0], [128, D], F32))
            junk = esN.enter_context(nc.sbuf_tensor("junk_%d" % uniq[0], [128, D], BF16))
            xnb = [esN.enter_context(nc.sbuf_tensor("xnb%d_%d" % (i, uniq[0]), [128, D], BF16)) for i in range(2)]
            sc.dma(SP, gb1[:], gi.partition_broadcast(128), writes=[gb_b])
            def stage1(t):
                src, sbufs = srcs(t)
                sc.op(ACT, lambda e: e.activation(out=junk[:], in_=src, func=AF.Square, accum_out=ss[:, t:t + 1]),
                      reads=sbufs, writes=[scr_b])
                sc.op(ACT, lambda e: e.activation(out=sq[:, t:t + 1], in_=ss[:, t:t + 1], func=AF.Sqrt, scale=1.0 / D, bias=eps_t[:]),
                      reads=[scr_b, CONST], writes=[scr_b])
                sc.op(DVE, lambda e: e.reciprocal(out=rstd[:, t:t + 1], in_=sq[:, t:t + 1]), reads=[scr_b], writes=[scr_b2])
                k = t % 2
                sc.op(DVE, lambda e: e.scalar_tensor_tensor(out=xnb[k][:], in0=src, scalar=rstd[:, t:t + 1], in1=gb1[:], op0=ALU.mult, op1=ALU.mult),
                      reads=list(sbufs) + [scr_b2, gb_b], writes=[xnb_b[k]])

            def stage2(t):
                k = t % 2
                bk, bb = bank()
                bkv = bk[:].bitcast(BF16)
                for kc in range(8):
                    sc.op(PE, lambda e, kc=kc: e.transpose(out=bkv[:, kc * 128:(kc + 1) * 128], in_=xnb[k][:, kc * 128:(kc + 1) * 128], identity=ident[:]),
                          reads=[xnb_b[k], CONST], writes=[bb], inc=(kc == 7))
                sc.op(ACT, lambda e: e.activation(out=dstT[:, :, t * 128:(t + 1) * 128], in_=bkv[:, 0:1024].rearrange("p (k c) -> p k c", k=8), func=AF.Copy),
                      reads=[bb], writes=[dst_bufs[t]])

            for i in range(nt + 1):
                if i < nt:
                    stage1(i)
                if i >= 1:
                    stage2(i - 1)
                if i < nt:
                    yield i
            esN.close()

        eps_t = sb("eps_t", (128, 1), F32)
        zero_t = sb("zero_t", (128, 1), F32)
        sc.op(DVE, lambda e: e.memset(zero_t[:], 0.0), writes=[CONST])
        sc.op(DVE, lambda e: e.memset(eps_t[:], EPS), writes=[CONST])
        ss_t = sb("ss_t", (128, NT), F32)
        sq_t = sb("sq_t", (128, NT), F32)
        rstd_t = sb("rstd_t", (128, NT), F32)
        xnb_b = [Buf("xnb%d" % i) for i in range(2)]
        scr_b = Buf("scr")
        scr_b2 = Buf("scr2")
        nscr = (None, ss_t, sq_t, rstd_t, None, xnb_b)

        outs_final = []

        for si in range(nseq):
            with contextlib.ExitStack() as esB:
                sc.barrier()
                def sbB(name, shape, dt):
                    return esB.enter_context(nc.sbuf_tensor("%s_s%d" % (name, si), list(shape), dt))
                wA = [sbB("wA%d" % i, (128, 8, 384), BF16) for i in range(2)]
                wA_b = [[Buf("wA%d_%d" % (i, o)) for o in range(3)] for i in range(2)]
                Ehp = [sbB("Ehp%d" % i, (128, 6 * 256), BF16) for i in range(2)]
                Ehp_b = [Buf("Ehp%d" % i) for i in range(2)]
                natT = sbB("natT", (128, 4, S), BF16)
                nat_b = [Buf("nat%d" % o) for o in range(4)]
                permTs = [sbB("permT%d" % i, (128, 4, S), BF16) for i in range(2)]
                perm_bs = [[Buf("perm%d_%d" % (i, o)) for o in range(4)] for i in range(2)]
                sc.op(DVE, lambda e: e.memset(natT[64:128, 0, :], 0.0), writes=[nat_b[0]])
                sc.op(DVE, lambda e: e.memset(natT[0:64, 1, :], 0.0), writes=[nat_b[1]])
                for i in range(2):
                    sc.op(DVE, lambda e, i=i: e.memset(permTs[i][64:128, 0, :], 0.0), writes=[perm_bs[i][0]])
                    sc.op(DVE, lambda e, i=i: e.memset(permTs[i][0:64, 1, :], 0.0), writes=[perm_bs[i][1]])
                Vtok = [sbB("Vtok%d" % i, (128, 16, 192), BF16) for i in range(2)]
                Vtok_b = [Buf("Vtok%d" % i) for i in range(2)]
                acc2 = sbB("acc2", (128, 2, S), F32)
                acc = [acc2[:, 0, :], acc2[:, 1, :]]
                accB = Buf("acc2")
                acc_b = [accB, accB]
                accP = [Buf("accP%d" % i) for i in range(3)]
                pT = [sbB("pT%d" % i, (128, 512), BF16) for i in range(6)]
                pT_b = [Buf("pT%d" % i) for i in range(6)]
                rec = [sbB("rec%d" % i, (128, 512), F32) for i in range(2)]
                rec_b = [Buf("rec%d" % i) for i in range(2)]
                for i in range(2):
                    sc.op(DVE, lambda e, i=i: e.memset(Vtok[i][:], 0.0), writes=[Vtok_b[i]])
                    sc.op(DVE, lambda e, i=i: e.memset(Vtok[i][:, :, 64:65], 1.0), writes=[Vtok_b[i]])
                pi = 0
                vi = 0
                xin = [sbB("xin%d" % i, (128, D), F32) for i in range(3)]
                xin_b = [Buf("xin%d" % i) for i in range(3)]

                def srcA(t):
                    k = t % 3
                    sc.dma(SP, xin[k][:], x[si, t * 128:(t + 1) * 128, :], writes=[xin_b[k]])
                    return xin[k][:], [xin_b[k]]
                norm_T("mix", srcA, g_mix, xnT, xnT_b, NT, nscr)
                for hp in range(4):
                    wk = hp % 2
                    for o in range(3):
                        sc.dma(POOL, wA[wk][:, :, o * 128:(o + 1) * 128], w_in_v[:, :, o * A_W + hp * 128:o * A_W + (hp + 1) * 128], writes=[wA_b[wk][o]])
                    sc.dma(POOL, Ehp[wk][:], c_E[:, hp * 1536:(hp + 1) * 1536], writes=[Ehp_b[wk]])
                    for o in range(3):
                        for cb in range(4):
                            bk, bb = bank()
                            for kc in range(8):
                                sc.op(PE, lambda e, bk=bk, kc=kc, o=o, cb=cb, wk=wk: e.matmul(bk[:, :], lhsT=wA[wk][:, kc, o * 128:(o + 1) * 128], rhs=xnT[:, kc, cb * 512:(cb + 1) * 512], start=(kc == 0), stop=(kc == 7)),
                                      reads=[wA_b[wk][o]] + xnT_b[cb * 4:cb * 4 + 4], writes=[bb], inc=(kc == 7))
                            cs = slice(cb * 512, (cb + 1) * 512)
                            targets = [(natT, nat_b, None)] + [(permTs[i], perm_bs[i], DIL[i + 1]) for i in range(2)]
                            for ti, (dstT_, dstb_, dd) in enumerate(targets):
                                def views(rows, slot, dstT_=dstT_, dd=dd, bk=bk, cb=cb, cs=cs):
                                    if dd is None:
                                        return dstT_[rows, slot, cs], bk[rows, :]
                                    w = 512 // dd
                                    o_ap = dstT_[rows, slot, :].rearrange("p (r u) -> p r u", r=dd)[:, :, cb * w:(cb + 1) * w]
                                    i_ap = bk[rows, :].rearrange("p (u r) -> p r u", r=dd)
                                    return o_ap, i_ap
                                use_act = ((o * 4 + cb) % 2 == 0)
                                parts = [(slice(0, 64), 0), (slice(64, 128), 1)] if o == 0 else [(slice(0, 128), o + 1)]
                                for rows, slot in parts:
                                    o_ap, i_ap = views(rows, slot)
                                    if use_act:
                                        sc.op(ACT, lambda e, o_ap=o_ap, i_ap=i_ap: e.activation(out=o_ap, in_=i_ap, func=AF.Copy), reads=[bb], writes=[dstb_[slot]])
                                    else:
                                        sc.op(DVE, lambda e, o_ap=o_ap, i_ap=i_ap: e.tensor_copy(out=o_ap, in_=i_ap), reads=[bb], writes=[dstb_[slot]])
                    for di, d in enumerate(DIL):
                        U = S // d
                        nb = U // 128
                        if d == 1:
                            srcT, src_b = natT, nat_b
                        else:
                            permT, perm_b = permTs[di - 1], perm_bs[di - 1]
                            srcT, src_b = permT, perm_b
                        vk = vi % 2
                        vi += 1
                        for half in range(2):
                            bk, bb = bank()
                            bkv = bk[:].bitcast(BF16)
                            for j in range(8):
                                blk_i = half * 8 + j
                                sc.op(PE, lambda e, bkv=bkv, j=j, blk_i=blk_i, srcT=srcT: e.transpose(out=bkv[:, j * 128:(j + 1) * 128], in_=srcT[:, 3, blk_i * 128:(blk_i + 1) * 128], identity=ident[:]),
                                      reads=[src_b[3], CONST], writes=[bb], inc=(j == 7))
                            sc.op(DVE, lambda e, bkv=bkv, half=half, vk=vk: e.tensor_copy(
                                out=Vtok[vk][:, half * 8:(half + 1) * 8, :].rearrange("p b (s c) -> p b s c", s=3)[:, :, 0:3:2, :],
                                in_=bkv[:, 0:1024].rearrange("p (b s c) -> p b s c", b=8, s=2)),
                                reads=[bb], writes=[Vtok_b[vk]])
                        steps = [(r, n) for r in range(d) for n in range(nb)]
                        st = {}

                        def stageA(step, d=d, U=U, di=di, srcT=srcT, src_b=src_b, wk=wk):
                            nonlocal pi
                            r, n = step
                            Eoff = di * 512
                            c0 = r * U + n * 128
                            lo = 0 if n > 0 else 256
                            bk, bb = bank()
                            if n > 0:
                                sc.op(PE, lambda e: e.matmul(bk[:, 0:256].rearrange("p (s c) -> p s c", s=2), lhsT=srcT[:, 2, c0 - 128:c0], rhs=srcT[:, 0:2, c0:c0 + 128], start=True, stop=True),
                                      reads=[src_b[0], src_b[1], src_b[2]], writes=[bb], inc=False)
                            sc.op(PE, lambda e: e.matmul(bk[:, 256:512].rearrange("p (s c) -> p s c", s=2), lhsT=srcT[:, 2, c0:c0 + 128], rhs=srcT[:, 0:2, c0:c0 + 128], start=True, stop=True),
                                  reads=[src_b[0], src_b[1], src_b[2]], writes=[bb])
                            pk = pi % len(pT)
                            pi += 1
                            sc.op(ACT, lambda e: e.activation(out=pT[pk][:, lo:512], in_=bk[:, lo:512], func=AF.Exp, scale=0.125),
                                  reads=[bb], writes=[pT_b[pk]])
                            sc.op(DVE, lambda e: e.tensor_tensor(out=pT[pk][:, lo:512], in0=pT[pk][:, lo:512], in1=Ehp[wk][:, Eoff + lo:Eoff + 512], op=ALU.mult),
                                  reads=[pT_b[pk], Ehp_b[wk]], writes=[pT_b[pk]])
                            st[step] = pk

                        def stageB(step, d=d, U=U, vk=vk):
                            r, n = step
                            pk = st.pop(step)
                            c0 = r * U + n * 128
                            vb = c0 // 128
                            bk2, bb2 = bank()
                            for h in range(2):
                                vsl = slice(0, 65) if h == 0 else slice(64, 192)
                                mrows = 65 if h == 0 else 128
                                oc = slice(h * 128, (h + 1) * 128)
                                if n > 0:
                                    sc.op(PE, lambda e, vsl=vsl, mrows=mrows, h=h, oc=oc: e.matmul(bk2[0:mrows, oc], lhsT=Vtok[vk][:, vb - 1, vsl], rhs=pT[pk][:, h * 128:(h + 1) * 128], start=True, stop=False),
                                          reads=[Vtok_b[vk], pT_b[pk]], writes=[bb2], inc=False)
                                sc.op(PE, lambda e, vsl=vsl, mrows=mrows, h=h, oc=oc: e.matmul(bk2[0:mrows, oc], lhsT=Vtok[vk][:, vb, vsl], rhs=pT[pk][:, 256 + h * 128:256 + (h + 1) * 128], start=(n == 0), stop=True),
                                      reads=[Vtok_b[vk], pT_b[pk]], writes=[bb2], inc=(h == 1))
                            src2 = bk2[:, 0:256].rearrange("p (h c) -> p h c", h=2)
                            if d == 1:
                                dst = acc2[:, :, n * 128:(n + 1) * 128]
                                sc.op(ACT, lambda e: e.activation(out=dst, in_=src2, func=AF.Copy), reads=[bb2, accB], writes=[accP[0]])
                            else:
                                dst = acc2[:, :, :].rearrange("p h (u r) -> p h r u", r=d)[:, :, r, n * 128:(n + 1) * 128]
                                sc.op(DVE, lambda e, di=di: e.tensor_tensor(out=dst, in0=src2, in1=dst, op=ALU.add), reads=[bb2, accB] + accP[0:di], writes=[accP[di]])

                        SKEW = 3
                        for i in range(len(steps) + SKEW):
                            if i < len(steps):
                                stageA(steps[i])
                            if i >= SKEW:
                                stageB(steps[i - SKEW])
                    sc.op(ACT, lambda e: e.activation(out=acc2[64:65, 0, :], in_=acc2[64:65, 0, :], func=AF.Ln), reads=[accB] + accP, writes=[accB])
                    sc.op(ACT, lambda e: e.activation(out=acc2[0:1, 1, :], in_=acc2[0:1, 1, :], func=AF.Ln), reads=[accB] + accP, writes=[accB])
                    for h in range(2):
                        for cb in range(4):
                            bk, bb = bank()
                            cs = slice(cb * 512, (cb + 1) * 512)
                            if h == 0:
                                sc.op(PE, lambda e, bk=bk, cs=cs: e.matmul(bk[0:64, :], lhsT=onesf[64:65, 0:64], rhs=acc2[64:65, 0, cs], start=True, stop=True), reads=[accB, CONST], writes=[bb])
                                rows = slice(0, 64)
                            else:
                                sc.op(PE, lambda e, bk=bk, cs=cs: e.matmul(bk[:, :], lhsT=onesf[0:1, 0:128], rhs=acc2[0:1, 1, cs], start=True, stop=True), reads=[accB, CONST], writes=[bb])
                                rows = slice(64, 128)
                            rk = (h * 4 + cb) % 2
                            sc.op(ACT, lambda e, bk=bk, rk=rk, rows=rows: e.activation(out=rec[rk][rows, :], in_=bk[rows, :], func=AF.Exp, scale=-1.0), reads=[bb], writes=[rec_b[rk]])
                            sc.op(DVE, lambda e, rk=rk, rows=rows, cs=cs, h=h, hp=hp: e.tensor_tensor(out=catT[rows, hp, cs], in0=acc2[rows, h, cs], in1=rec[rk][rows, :], op=ALU.mult),
                                  reads=[rec_b[rk], accB] + accP, writes=[catT_b[hp]])
                    if stop_after == "B0":
                        stgB = sbB("stgB", (128, S), F32)
                        stgB_b = Buf("stgB")
                        for c, (src, srcb) in enumerate([(acc[0], acc_b[0]), (acc[1], acc_b[1]), (natT[:, 0, :], nat_b[0]), (natT[:, 2, :], nat_b[2]), (natT[:, 3, :], nat_b[3])]):
                            sc.op(DVE, lambda e, src=src: e.tensor_copy(out=stgB[:], in_=src), reads=[srcb], writes=[stgB_b])
                            db = Buf("dbgB%d" % c)
                            sc.dma(SP, dbg_t[c * 128:(c + 1) * 128, :], stgB[:], reads=[stgB_b], writes=[db])
                            outs_final.append(db)
                        break
            if stop_after in ("B", "B0"):
                break

            with contextlib.ExitStack() as esC:
                sc.barrier()

                def sbC(name, shape, dt):
                    return esC.enter_context(nc.sbuf_tensor("%s_s%d" % (name, si), list(shape), dt))
                wg = sbC("wg", (128, 8, 8), BF16)
                wg_b = Buf("wg")
                wv = sbC("wv", (128, 8, 512), BF16)
                wv_b = Buf("wv")
                wqk = [sbC("wqk%d" % i, (128, 8, 384), BF16) for i in range(2)]
                wqk_b = [[Buf("wqk%d_%d" % (i, o)) for o in range(3)] for i in range(2)]
                rmask = sbC("rmask", (4, S), BF16)
                rmask_b = Buf("rmask")
                sc.dma(POOL, rmask[:], c_rmask, writes=[rmask_b])
                X1 = sbC("X1", (4, S), F32)
                X2 = sbC("X2", (4, S), F32)
                X3 = sbC("X3", (4, S), F32)
                X4 = sbC("X4", (4, S), F32)
                GB = Buf("gates")
                gsm = sbC("gsm", (4, 4, 32), F32)
                cpre = [sbC("cpre%d" % i, (128, S + 3), BF16) for i in range(2)]
                cpre_b = [Buf("cpre%d" % i) for i in range(2)]
                cacc = sbC("cacc", (128, S), BF16)
                cacc_b = Buf("cacc")
                dg = [sbC("dg%d" % i, (128, 4, 128), BF16) for i in range(2)]
                dg_b = [Buf("dg%d" % i) for i in range(2)]
                qTe2 = [sbC("qTe%d" % i, (128, S), BF16) for i in range(2)]
                qTo2 = [sbC("qTo%d" % i, (128, S), BF16) for i in range(2)]
                q_b2 = [Buf("qT%d" % i) for i in range(2)]
                ktT2 = [sbC("ktT%d" % i, (128, S), BF16) for i in range(2)]
                ktT_b2 = [Buf("ktT%d" % i) for i in range(2)]
                kttok2 = [sbC("kttok%d" % i, (128, NT, 128), BF16) for i in range(2)]
                kttok_b2 = [Buf("kttok%d" % i) for i in range(2)]
                vaug = sbC("vaug", (128, NT, 4, 129), BF16)
                vaug_b = Buf("vaug")
                Cst = sbC("Cst", (128, 4, 129), F32)
                Cst_r = [Buf("Cst%d" % i) for i in range(4)]
                Cst_b = Buf("Cst")
                Chat = sbC("Chat", (128, 32, 129), BF16)
                Chat_b = Buf("Chat")
                gamB = sbC("gamB", (128, 4, 32), F32)
                gamB_b = Buf("gamB")
                fltok = sbC("fltok", (128, NT, 4), F32)
                fltok_b = Buf("fltok")
                hmb = sbC("hmb", (128, NT, 128), BF16)
                hmb_b = Buf("hmb")
                hmb_t = [Buf("hmb%d" % t) for t in range(NT)]
                ssm = sbC("ssm", (128, NT), F32)
                sqm = sbC("sqm", (128, NT), F32)
                rsm = sbC("rsm", (128, NT), F32)
                ssm_b = Buf("ssm")
                dn = sbC("dn", (128, NT, 2), F32)
                dn_b = [Buf("dn%d" % t) for t in range(NT)]
                Pm = [sbC("Pm%d" % i, (128, 128), BF16) for i in range(3)]
                Pm_b = [Buf("Pm%d" % i) for i in range(3)]
                sg = [sbC("sg%d" % i, (128, 128), BF16) for i in range(3)]
                sg_b = [Buf("sg%d" % i) for i in range(3)]
                t1 = [sbC("t1_%d" % i, (128, 128), BF16) for i in range(3)]
                t1_b = [Buf("t1_%d" % i) for i in range(3)]
                t2 = [sbC("t2_%d" % i, (128, 128), BF16) for i in range(3)]
                t2_b = [Buf("t2_%d" % i) for i in range(3)]
                junkm = sbC("junkm", (128, 128), BF16)

                MQ0 = 3 * A_W
                sc.dma(POOL, wg[:], w_in_v[:, :, 3584:3592], writes=[wg_b])
                sc.dma(POOL, wv[:], w_in_v[:, :, MQ0 + 1024:MQ0 + 1536], writes=[wv_b])

                def load_wqk(j):
                    k = j % 2
                    sc.dma(POOL, wqk[k][:, :, 0:128], w_in_v[:, :, MQ0 + j * 128:MQ0 + (j + 1) * 128], writes=[wqk_b[k][0]])
                    sc.dma(POOL, wqk[k][:, :, 128:256], w_in_v[:, :, MQ0 + 512 + j * 128:MQ0 + 512 + (j + 1) * 128], writes=[wqk_b[k][1]])
                    sc.dma(POOL, wqk[k][:, :, 256:384], w_in_v[:, :, MQ0 + 1536 + j * 128:MQ0 + 1536 + (j + 1) * 128], writes=[wqk_b[k][2]])
                load_wqk(0)
                sc.op(POOL, lambda e: e.memset(vaug[:, :, :, 128:129], 1.0), writes=[vaug_b])
                for i in range(2):
                    sc.op(POOL, lambda e, i=i: e.memset(cpre[i][:, 0:3], 0.0), writes=[cpre_b[i]])
                for i in range(2):
                    sc.op(POOL, lambda e, i=i: e.memset(qTe2[i][:], 0.0), writes=[q_b2[i]])
                    sc.op(POOL, lambda e, i=i: e.memset(qTo2[i][:], 0.0), writes=[q_b2[i]])

                for cb in range(4):
                    cs = slice(cb * 512, (cb + 1) * 512)
                    bk, bb = bank()
                    for kc in range(8):
                        sc.op(PE, lambda e, bk=bk, kc=kc, cs=cs: e.matmul(bk[0:4, :], lhsT=wg[:, kc, 0:4], rhs=xnT[:, kc, cs], start=(kc == 0), stop=(kc == 7)),
                              reads=[wg_b] + xnT_b[cb * 4:cb * 4 + 4], writes=[bb], inc=(kc == 7))
                    sc.op(ACT, lambda e, bk=bk, cs=cs: e.activation(out=X1[:, cs], in_=bk[0:4, :], func=AF.Identity, bias=gbi[:], scale=1.0), reads=[bb, CONST], writes=[GB])
                    bk, bb = bank()
                    for kc in range(8):
                        sc.op(PE, lambda e, bk=bk, kc=kc, cs=cs: e.matmul(bk[0:4, :], lhsT=wg[:, kc, 4:8], rhs=xnT[:, kc, cs], start=(kc == 0), stop=(kc == 7)),
                              reads=[wg_b] + xnT_b[cb * 4:cb * 4 + 4], writes=[bb], inc=(kc == 7))
                    sc.op(ACT, lambda e, bk=bk, cs=cs: e.activation(out=X2[:, cs], in_=bk[0:4, :], func=AF.Exp, bias=ngbf[:], scale=-1.0), reads=[bb, CONST], writes=[GB])
                def v_tile(T):
                    bk, bb = bank()
                    for kc in range(8):
                        sc.op(PE, lambda e, kc=kc: e.matmul(bk[:, :], lhsT=xnT[:, kc, T * 128:(T + 1) * 128], rhs=wv[:, kc, :], start=(kc == 0), stop=(kc == 7)),
                              reads=[wv_b, xnT_b[T]], writes=[bb], inc=(kc == 7))
                    sc.op(ACT, lambda e: e.activation(out=vaug[:, T, :, 0:128], in_=bk[:, :].rearrange("p (j c) -> p j c", j=4), func=AF.Copy), reads=[bb], writes=[vaug_b])
                v_todo = list(range(NT))

                def v_some(n):
                    for _ in range(n):
                        if v_todo:
                            v_tile(v_todo.pop(0))
                sc.op(ACT, lambda e: e.activation(out=X2[:], in_=X2[:], func=AF.Ln, bias=1.0, scale=1.0), reads=[GB], writes=[GB])
                v_some(1)
                sc.op(DVE, lambda e: e.tensor_scalar(out=X2[:], in0=X2[:], scalar1=-1.0, scalar2=None, op0=ALU.mult), reads=[GB], writes=[GB])
                v_some(1)
                sc.op(DVE, lambda e: e.tensor_tensor_scan(out=X3[:], data0=rmask[:], data1=X2[:], initial=0.0, op0=ALU.mult, op1=ALU.add), reads=[GB, rmask_b], writes=[GB])
                v_some(1)
                sc.op(DVE, lambda e: e.tensor_tensor(out=X1[:], in0=X1[:], in1=X3[:], op=ALU.subtract), reads=[GB], writes=[GB])
                v_some(1)
                sc.op(DVE, lambda e: e.memset(X2[:], 0.0), reads=[GB], writes=[GB])
                v_some(1)
                X3v = X3[:].rearrange("p (c l) -> p c l", l=64)
                X2v = X2[:].rearrange("p (c l) -> p c l", l=64)
                X1v = X1[:].rearrange("p (c l) -> p c l", l=64)
                X4v = X4[:].rearrange("p (c l) -> p c l", l=64)
                sc.op(DVE, lambda e: e.tensor_copy(out=X2v[:, 1:32, 0:1], in_=X3v[:, 0:31, 63:64]), reads=[GB], writes=[GB])
                v_some(1)
                sc.op(DVE, lambda e: e.tensor_tensor_scan(out=X4[:], data0=X2[:], data1=X1[:], initial=0.0, op0=ALU.add, op1=ALU.max), reads=[GB], writes=[GB])
                v_some(1)
                sc.op(DVE, lambda e: e.tensor_copy(out=gsm[:, 0, :], in_=X3v[:, :, 63]), reads=[GB], writes=[GB])
                v_some(1)
                sc.op(DVE, lambda e: e.tensor_copy(out=gsm[:, 1, :], in_=X4v[:, :, 63]), reads=[GB], writes=[GB])
                v_some(1)
                sc.op(DVE, lambda e: e.memset(gsm[:, 2, :], 0.0), reads=[GB], writes=[GB])
                v_some(1)
                sc.op(DVE, lambda e: e.tensor_tensor(out=gsm[:, 3, 1:32], in0=gsm[:, 0, 0:31], in1=gsm[:, 1, 0:31], op=ALU.add), reads=[GB], writes=[GB])
                v_some(1)
                sc.op(DVE, lambda e: e.tensor_tensor(out=gsm[:, 3, 1:32], in0=gsm[:, 3, 1:32], in1=gsm[:, 1, 1:32], op=ALU.subtract), reads=[GB], writes=[GB])
                v_some(1)
                sc.op(ACT, lambda e: e.activation(out=gsm[:, 2, 1:32], in_=gsm[:, 3, 1:32], func=AF.Exp), reads=[GB], writes=[GB])
                v_some(1)
                Mend_bc = gsm[:, 1, :].unsqueeze(2).to_broadcast([4, 32, 64])
                sc.op(DVE, lambda e: e.tensor_tensor(out=X1v, in0=X1v, in1=Mend_bc, op=ALU.subtract), reads=[GB], writes=[GB])
                v_some(1)
                sc.op(DVE, lambda e: e.tensor_scalar(out=X1[:], in0=X1[:], scalar1=-0.5 * math.log(128.0), scalar2=None, op0=ALU.add), reads=[GB], writes=[GB])
                v_some(1)
                sc.op(ACT, lambda e: e.activation(out=X1[:], in_=X1[:], func=AF.Exp), reads=[GB], writes=[GB])
                v_some(1)
                sc.op(DVE, lambda e: e.tensor_tensor(out=X3v, in0=X3v, in1=Mend_bc, op=ALU.add), reads=[GB], writes=[GB])
                v_some(1)
                sc.op(ACT, lambda e: e.activation(out=X3[:], in_=X3[:], func=AF.Exp, scale=-1.0), reads=[GB], writes=[GB])
                v_some(1)
                if stop_after == "C1":
                    for c, (src, n) in enumerate([(X1[:], S), (X3[:], S), (X4[:], S), (gsm[:].rearrange("p a b -> p (a b)"), 128)]):
                        db = Buf("dbgC%d" % c)
                        sc.dma(SP, dbg_t[c * 4:(c + 1) * 4, 0:n], src, reads=[GB], writes=[db])
                        outs_final.append(db)
                    break
                bk, bb = bank()
                for j in range(4):
                    sc.op(PE, lambda e, bk=bk, j=j: e.matmul(bk[:, j * 32:(j + 1) * 32], lhsT=onehot[:, j * 128:(j + 1) * 128], rhs=gsm[:, 2, :], start=True, stop=True), reads=[GB, CONST], writes=[bb], inc=(j == 3))
                sc.op(DVE, lambda e, bk=bk: e.tensor_copy(out=gamB[:].rearrange("p j c -> p (j c)"), in_=bk[:, 0:128]), reads=[bb], writes=[gamB_b])
                bk, bb = bank()
                for T in range(NT):
                    sc.op(PE, lambda e, bk=bk, T=T: e.transpose(out=bk[:, T * 4:(T + 1) * 4], in_=X3[:, T * 128:(T + 1) * 128], identity=identf[0:4, 0:4]), reads=[GB, CONST], writes=[bb], inc=(T == NT - 1))
                sc.op(DVE, lambda e, bk=bk: e.tensor_copy(out=fltok[:].rearrange("p t j -> p (t j)"), in_=bk[:, 0:64]), reads=[bb], writes=[fltok_b])
                v_some(NT)
                def head_gen(j):
                    wk = j % 2
                    qTe, qTo, ktT, kttok = qTe2[j % 2], qTo2[j % 2], ktT2[j % 2], kttok2[j % 2]
                    q_b, ktT_b, kttok_b = q_b2[j % 2], ktT_b2[j % 2], kttok_b2[j % 2]
                    if j > 0:
                        load_wqk(j)
                    for qk in range(2):
                        chb = qk * 4 + j
                        cp = cpre[qk]
                        cpb = cpre_b[qk]
                        for tap in range(4):
                            sc.op(DVE, lambda e, tap=tap, chb=chb, qk=qk: e.tensor_scalar(out=dg[qk][:, tap, :], in0=ident[:], scalar1=cwm[:, chb, tap:tap + 1], scalar2=None, op0=ALU.mult), reads=[CONST], writes=[dg_b[qk]])
                        for cb in range(4):
                            cs = slice(cb * 512, (cb + 1) * 512)
                            bk, bb = bank()
                            for kc in range(8):
                                sc.op(PE, lambda e, bk=bk, kc=kc, cs=cs, wk=wk, qk=qk: e.matmul(bk[:, :], lhsT=wqk[wk][:, kc, qk * 128:(qk + 1) * 128], rhs=xnT[:, kc, cs], start=(kc == 0), stop=(kc == 7)),
                                      reads=[wqk_b[wk][qk]] + xnT_b[cb * 4:cb * 4 + 4], writes=[bb], inc=(kc == 7))
                            if cb % 2 == 0:
                                sc.op(ACT, lambda e, bk=bk, cb=cb, cp=cp: e.activation(out=cp[:, 3 + cb * 512:3 + (cb + 1) * 512], in_=bk[:, :], func=AF.Copy), reads=[bb], writes=[cpb])
                            else:
                                sc.op(DVE, lambda e, bk=bk, cb=cb, cp=cp: e.tensor_copy(out=cp[:, 3 + cb * 512:3 + (cb + 1) * 512], in_=bk[:, :]), reads=[bb], writes=[cpb])
                            yield None
                    for qk in range(2):
                        chb = qk * 4 + j
                        cp = cpre[qk]
                        cpb = cpre_b[qk]
                        for cb in range(4):
                            cs = slice(cb * 512, (cb + 1) * 512)
                            bk, bb = bank()
                            for tap in range(4):
                                sc.op(PE, lambda e, bk=bk, tap=tap, cb=cb, cp=cp, qk=qk: e.matmul(bk[:, :], lhsT=dg[qk][:, tap, :], rhs=cp[:, 3 + cb * 512 - tap:3 + (cb + 1) * 512 - tap], start=(tap == 0), stop=(tap == 3)),
                                      reads=[dg_b[qk], cpb], writes=[bb], inc=(tap == 3))
                            if qk == 0:
                                bv = bk[:, :].rearrange("p (t two l) -> p t two l", two=2, l=64)
                                sc.op(ACT, lambda e, bv=bv, cs=cs, chb=chb: e.activation(out=qTe[:, cs].rearrange("p (t two l) -> p t two l", two=2, l=64)[:, :, 0, :], in_=bv[:, :, 0, :], func=AF.Silu, bias=cbm[:, chb:chb + 1], scale=1.0), reads=[bb, CONST], writes=[q_b])
                                sc.op(ACT, lambda e, bv=bv, cs=cs, chb=chb: e.activation(out=qTo[:, cs].rearrange("p (t two l) -> p t two l", two=2, l=64)[:, :, 1, :], in_=bv[:, :, 1, :], func=AF.Silu, bias=cbm[:, chb:chb + 1], scale=1.0), reads=[bb, CONST], writes=[q_b])
                            else:
                                sc.op(ACT, lambda e, bk=bk, cs=cs, chb=chb: e.activation(out=cacc[:, cs], in_=bk[:, :], func=AF.Silu, bias=cbm[:, chb:chb + 1], scale=1.0), reads=[bb, CONST], writes=[cacc_b])
                                bk2, bb2 = bank()
                                sc.op(PE, lambda e, bk2=bk2, cs=cs, j=j: e.matmul(bk2[:, :], lhsT=onehot[:, j * 128:(j + 1) * 128], rhs=X1[:, cs], start=True, stop=True), reads=[GB, CONST], writes=[bb2])
                                sc.op(DVE, lambda e, bk2=bk2, cs=cs: e.tensor_tensor(out=ktT[:, cs], in0=bk2[:, :], in1=cacc[:, cs], op=ALU.mult), reads=[bb2, cacc_b], writes=[ktT_b])
                            yield None
                    for half in range(2):
                        bk, bb = bank()
                        bkv = bk[:].bitcast(BF16)
                        for i in range(8):
                            T = half * 8 + i
                            sc.op(PE, lambda e, bkv=bkv, i=i, T=T: e.transpose(out=bkv[:, i * 128:(i + 1) * 128], in_=ktT[:, T * 128:(T + 1) * 128], identity=ident[:]), reads=[ktT_b, CONST], writes=[bb], inc=(i == 7))
                        sc.op(ACT, lambda e, bkv=bkv, half=half: e.activation(out=kttok[:, half * 8:(half + 1) * 8, :], in_=bkv[:, 0:1024].rearrange("p (t c) -> p t c", t=8), func=AF.Copy), reads=[bb], writes=[kttok_b])
                    yield "EARLY_DONE"
                    sc.op(POOL, lambda e: e.memset(Cst[:, 0, :], 0.0), writes=[Cst_r[0]])
                    sc.op(POOL, lambda e: e.memset(Chat[:, 0, :], 0.0), reads=[], writes=[Chat_b])
                    for c in range(31):
                        T, par = c // 2, c % 2
                        pr = slice(par * 64, (par + 1) * 64)
                        bk, bb = bank()
                        sc.op(PE, lambda e, bk=bk, T=T, pr=pr, j=j: e.matmul(bk[:, 0:129], lhsT=kttok[pr, T, :], rhs=vaug[pr, T, j, :], start=True, stop=True), reads=[kttok_b, vaug_b], writes=[bb])
                        sc.op(DVE, lambda e, bk=bk, c=c, j=j: e.scalar_tensor_tensor(out=Cst[:, (c + 1) % 4, :], in0=Cst[:, c % 4, :], scalar=gamB[:, j, c:c + 1], in1=bk[:, 0:129], op0=ALU.mult, op1=ALU.add), reads=[bb, Cst_r[c % 4], gamB_b], writes=[Cst_r[(c + 1) % 4]])
                        sc.op(ACT, lambda e, c=c, j=j: e.activation(out=Chat[:, c + 1, :], in_=Cst[:, (c + 1) % 4, :], func=AF.Copy, scale=gamB[:, j, c + 1:c + 2]), reads=[Cst_r[(c + 1) % 4], gamB_b], writes=[Chat_b])
                        yield None
                    def tile1(T):
                        ts_ = slice(T * 128, (T + 1) * 128)
                        bk, bb = bank()
                        sc.op(PE, lambda e: e.matmul(bk[:, 0:128], lhsT=ktT[:, ts_], rhs=qTe[:, ts_], start=True, stop=False), reads=[ktT_b, q_b], writes=[bb], inc=False)
                        sc.op(PE, lambda e: e.matmul(bk[:, 0:128], lhsT=ktT[:, ts_], rhs=qTo[:, ts_], start=False, stop=True), reads=[ktT_b, q_b], writes=[bb])
                        pk = T % 3
                        sc.op(DVE, lambda e: e.tensor_tensor(out=Pm[pk][:], in0=bk[:, 0:128], in1=cmask[:], op=ALU.mult), reads=[bb, CONST], writes=[Pm_b[pk]])

                    def tile2(T, j=j):
                        ts_ = slice(T * 128, (T + 1) * 128)
                        pk = T % 3
                        bk2, bb2 = bank()
                        sc.op(PE, lambda e: e.matmul(bk2[:, 0:129], lhsT=qTe[:, ts_], rhs=Chat[:, 2 * T, :], start=True, stop=False), reads=[q_b, Chat_b], writes=[bb2], inc=False)
                        sc.op(PE, lambda e: e.matmul(bk2[:, 0:129], lhsT=qTo[:, ts_], rhs=Chat[:, 2 * T + 1, :], start=False, stop=False), reads=[q_b, Chat_b], writes=[bb2], inc=False)
                        sc.op(PE, lambda e: e.matmul(bk2[:, 0:129], lhsT=Pm[pk][:], rhs=vaug[:, T, j, :], start=False, stop=True), reads=[Pm_b[pk], vaug_b], writes=[bb2])
                        sc.op(ACT, lambda e: e.activation(out=dn[:, T, 0:1], in_=bk2[:, 128:129], func=AF.Abs), reads=[bb2], writes=[dn_b[T]])
                        sc.op(DVE, lambda e: e.tensor_tensor(out=dn[:, T, 0:1], in0=dn[:, T, 0:1], in1=fltok[:, T, j:j + 1], op=ALU.max), reads=[dn_b[T], fltok_b], writes=[dn_b[T]])
                        sc.op(DVE, lambda e: e.reciprocal(out=dn[:, T, 1:2], in_=dn[:, T, 0:1]), reads=[dn_b[T]], writes=[dn_b[T]])
                        sc.op(DVE, lambda e: e.tensor_scalar(out=hmb[:, T, :], in0=bk2[:, 0:128], scalar1=dn[:, T, 1:2], scalar2=None, op0=ALU.mult), reads=[bb2, dn_b[T]], writes=[hmb_t[T]])
                        sc.op(ACT, lambda e: e.activation(out=junkm[:], in_=hmb[:, T, :], func=AF.Square, accum_out=ssm[:, T:T + 1]), reads=[hmb_t[T]], writes=[ssm_b])
                    for i in range(NT + 2):
                        if i < NT:
                            tile1(i)
                        if i >= 2:
                            tile2(i - 2)
                        yield None
                    sc.op(ACT, lambda e: e.activation(out=sqm[:], in_=ssm[:], func=AF.Sqrt, scale=1.0 / 128.0, bias=eps_t[:]), reads=[ssm_b, CONST], writes=[ssm_b])
                    sc.op(DVE, lambda e: e.reciprocal(out=rsm[:], in_=sqm[:]), reads=[ssm_b], writes=[ssm_b])
                    bkT, bbT = bank_long()
                    bkTv = bkT[:].bitcast(BF16)

                    def og1(T, j=j, wk=wk):
                        k2 = T % 3
                        bk, bb = bank()
                        for kc in range(8):
                            sc.op(PE, lambda e, kc=kc: e.matmul(bk[:, 0:128], lhsT=xnT[:, kc, T * 128:(T + 1) * 128], rhs=wqk[wk][:, kc, 256:384], start=(kc == 0), stop=(kc == 7)),
                                  reads=[wqk_b[wk][2], xnT_b[T]], writes=[bb], inc=(kc == 7))
                        sc.op(ACT, lambda e: e.activation(out=sg[k2][:], in_=bk[:, 0:128], func=AF.Sigmoid), reads=[bb], writes=[sg_b[k2]])
                        sc.op(DVE, lambda e: e.scalar_tensor_tensor(out=t1[k2][:], in0=hmb[:, T, :], scalar=rsm[:, T:T + 1], in1=hgbc[:, j * 128:(j + 1) * 128], op0=ALU.mult, op1=ALU.mult), reads=[hmb_t[T], ssm_b, CONST], writes=[t1_b[k2]])
                        sc.op(DVE, lambda e: e.tensor_tensor(out=t2[k2][:], in0=t1[k2][:], in1=sg[k2][:], op=ALU.mult), reads=[t1_b[k2], sg_b[k2]], writes=[t2_b[k2]])

                    def og2(T, j=j):
                        k2 = T % 3
                        i = T % 8
                        half = T // 8
                        sc.op(PE, lambda e: e.transpose(out=bkTv[:, i * 128:(i + 1) * 128], in_=t2[k2][:], identity=ident[:]), reads=[t2_b[k2], CONST], writes=[bbT])
                        if i == 7:
                            sc.op(ACT, lambda e: e.activation(out=catT[:, 4 + j, half * 1024:(half + 1) * 1024], in_=bkTv[:, 0:1024], func=AF.Copy), reads=[bbT], writes=[catT_b[4 + j]])
                    for i in range(NT + 2):
                        if i < NT:
                            og1(i)
                        if i >= 2:
                            og2(i - 2)
                        yield None


                gens = [head_gen(j) for j in range(4)]

                def run_early(g):
                    for v in g:
                        if v == "EARLY_DONE":
                            return
                run_early(gens[0])
                for j in range(4):
                    nxt = gens[j + 1] if j + 1 < 4 else None
                    nxt_done = nxt is None
                    cur_done = False
                    while not (cur_done and nxt_done):
                        if not cur_done:
                            try:
                                next(gens[j])
                            except StopIteration:
                                cur_done = True
                        if not nxt_done:
                            try:
                                v = next(nxt)
                                if v == "EARLY_DONE":
                                    nxt_done = True
                            except StopIteration:
                                nxt_done = True

            if stop_after in ("C", "C1", "C2", "C3"):
                break

            with contextlib.ExitStack() as esD:
                sc.barrier()

                def sbD(name, shape, dt):
                    return esD.enter_context(nc.sbuf_tensor("%s_s%d" % (name, si), list(shape), dt))
                h_sb = sbD("h_sb", (128, NT, D), F32)
                h_b = [[Buf("h%d_%d" % (t, nb)) for nb in range(2)] for t in range(NT)]
                wbig = sbD("wbig", (128, 8, 1024), BF16)
                wbig_h = [Buf("wbig_h0"), Buf("wbig_h1")]
                wbig_b = wbig_h

                def load_wbig(src_v):
                    n = src_v.shape[1]
                    for hf in range(2):
                        sc.dma(POOL, wbig[:, 0:n, hf * 512:(hf + 1) * 512], src_v[:, :, hf * 512:(hf + 1) * 512], writes=[wbig_h[hf]])

                def out_proj(nk, in_bufs, fused=None):
                    for T in range(NT):
                        for nb in range(2):
                            bk, bb = bank()
                            for kc in range(nk):
                                sc.op(PE, lambda e, bk=bk, kc=kc, T=T, nb=nb: e.matmul(bk[:, :], lhsT=catT[:, kc, T * 128:(T + 1) * 128], rhs=wbig[:, kc, nb * 512:(nb + 1) * 512], start=(kc == 0), stop=(kc == nk - 1)),
                                      reads=[wbig_h[nb]] + in_bufs, writes=[bb], inc=(kc == nk - 1))
                            sc.op(DVE, lambda e, bk=bk, T=T, nb=nb: e.tensor_tensor(out=h_sb[:, T, nb * 512:(nb + 1) * 512], in0=bk[:, :], in1=h_sb[:, T, nb * 512:(nb + 1) * 512], op=ALU.add),
                                  reads=[bb, h_b[T][nb]], writes=[h_b[T][nb]])
                        if fused is not None:
                            next(fused, None)
                    if fused is not None:
                        for _ in fused:
                            pass

                def srcH(t):
                    return h_sb[:, t, :], list(h_b[t])

                load_wbig(w_mo.rearrange("(kc p) n -> p kc n", p=128))
                for T in range(NT):
                    sc.dma(SP, h_sb[:, T, :], x[si, T * 128:(T + 1) * 128, :], writes=h_b[T])
                if stop_after == "D":
                    out_proj(8, catT_b)
                else:
                    out_proj(8, catT_b, fused=norm_gen("xa", srcH, g_xa, xnT, xnT_b, NT, nscr))

                stopped = False
                if stop_after != "D":
                    with contextlib.ExitStack() as esE:
                        def sbE(name, shape, dt):
                            return esE.enter_context(nc.sbuf_tensor("%s_s%d" % (name, si), list(shape), dt))
                        xm = [sbE("xm%d" % i, (128, D), F32) for i in range(2)]
                        xm_b = [Buf("xm%d" % i) for i in range(2)]
                        memnT = sbE("memnT", (128, 8, NMEM), BF16)
                        memn_b = [Buf("memn%d" % i) for i in range(2)]
                        kTx = sbE("kTx", (128, 8, NMEM), BF16)
                        kTx_b = Buf("kTx")
                        vtokx = sbE("vtokx", (128, 2, D), BF16)
                        vtokx_b = Buf("vtokx")
                        wq = [sbE("wq%d" % i, (128, 8, 256), BF16) for i in range(2)]
                        wq_b = [Buf("wq%d" % i) for i in range(2)]
                        qTx = sbE("qTx", (128, 2, S), BF16)
                        qTx_b = Buf("qTx")
                        qTx_c = [[Buf("qTx%d_%d" % (a_, b_)) for b_ in range(4)] for a_ in range(2)]
                        pTx = [sbE("pTx%d" % i, (128, 512), BF16) for i in range(4)]
                        pTx_b = [Buf("pTx%d" % i) for i in range(4)]
                        recx = [sbE("recx%d" % i, (128, 512), F32) for i in range(2)]
                        recx_b = [Buf("recx%d" % i) for i in range(2)]
                        ones_bf = sbE("ones_bf", (128, 128), BF16)
                        ones_bf_b = Buf("ones_bf")
                        sc.op(POOL, lambda e: e.memset(ones_bf[:], 1.0), writes=[ones_bf_b])

                        def srcM(t):
                            sc.dma(SP, xm[t][:], mem[si, t * 128:(t + 1) * 128, :], writes=[xm_b[t]])
                            return xm[t][:], [xm_b[t]]
                        norm_T("mem", srcM, g_mem, memnT, memn_b, 2, nscr)
                        w_xkv_v = w_xkv.rearrange("(kc p) n -> p kc n", p=128)
                        load_wbig(w_xkv_v[:, :, 0:1024])
                        for oc in range(8):
                            bk, bb = bank()
                            for kc in range(8):
                                sc.op(PE, lambda e, bk=bk, kc=kc, oc=oc: e.matmul(bk[:, 0:NMEM], lhsT=wbig[:, kc, oc * 128:(oc + 1) * 128], rhs=memnT[:, kc, :], start=(kc == 0), stop=(kc == 7)),
                                      reads=[wbig_h[oc // 4]] + memn_b, writes=[bb], inc=(kc == 7))
                            sc.op(ACT, lambda e, bk=bk, oc=oc: e.activation(out=kTx[:, oc, :], in_=bk[:, 0:NMEM], func=AF.Copy), reads=[bb], writes=[kTx_b])
                        load_wbig(w_xkv_v[:, :, 1024:2048])
                        for mt in range(2):
                            for nb in range(2):
                                bk, bb = bank()
                                for kc in range(8):
                                    sc.op(PE, lambda e, bk=bk, kc=kc, mt=mt, nb=nb: e.matmul(bk[:, :], lhsT=memnT[:, kc, mt * 128:(mt + 1) * 128], rhs=wbig[:, kc, nb * 512:(nb + 1) * 512], start=(kc == 0), stop=(kc == 7)),
                                          reads=[wbig_h[nb], memn_b[mt]], writes=[bb], inc=(kc == 7))
                                sc.op(DVE, lambda e, bk=bk, mt=mt, nb=nb: e.tensor_copy(out=vtokx[:, mt, nb * 512:(nb + 1) * 512], in_=bk[:, :]), reads=[bb], writes=[vtokx_b])
                        w_xq_v = w_xq.rearrange("(kc p) n -> p kc n", p=128)
                        pxi = 0
                        for hh in range(4):
                            wk = hh % 2
                            sc.dma(POOL, wq[wk][:], w_xq_v[:, :, hh * 256:(hh + 1) * 256], writes=[wq_b[wk]])
                            for c2 in range(2):
                                for cb in range(4):
                                    cs = slice(cb * 512, (cb + 1) * 512)
                                    bk, bb = bank()
                                    for kc in range(8):
                                        sc.op(PE, lambda e, bk=bk, kc=kc, c2=c2, cs=cs, wk=wk: e.matmul(bk[:, :], lhsT=wq[wk][:, kc, c2 * 128:(c2 + 1) * 128], rhs=xnT[:, kc, cs], start=(kc == 0), stop=(kc == 7)),
                                              reads=[wq_b[wk]] + xnT_b[cb * 4:cb * 4 + 4], writes=[bb], inc=(kc == 7))
                                    if cb % 2 == 0:
                                        sc.op(ACT, lambda e, bk=bk, c2=c2, cs=cs: e.activation(out=qTx[:, c2, cs], in_=bk[:, :], func=AF.Copy), reads=[bb], writes=[qTx_c[c2][cb]])
                                    else:
                                        sc.op(DVE, lambda e, bk=bk, c2=c2, cs=cs: e.tensor_copy(out=qTx[:, c2, cs], in_=bk[:, :]), reads=[bb], writes=[qTx_c[c2][cb]])
                            for cb in range(4):
                                cs = slice(cb * 512, (cb + 1) * 512)
                                pks = []
                                for mt in range(2):
                                    bk, bb = bank()
                                    for c2 in range(2):
                                        sc.op(PE, lambda e, bk=bk, c2=c2, mt=mt, cs=cs, hh=hh: e.matmul(bk[:, :], lhsT=kTx[:, hh * 2 + c2, mt * 128:(mt + 1) * 128], rhs=qTx[:, c2, cs], start=(c2 == 0), stop=(c2 == 1)),
                                              reads=[kTx_b, qTx_c[c2][cb]], writes=[bb], inc=(c2 == 1))
                                    pk = pxi % 4
                                    pxi += 1
                                    pks.append(pk)
                                    sc.op(ACT, lambda e, bk=bk, pk=pk: e.activation(out=pTx[pk][:], in_=bk[:, :], func=AF.Exp, scale=1.0 / 16.0), reads=[bb], writes=[pTx_b[pk]])
                                bk, bb = bank()
                                for mt in range(2):
                                    sc.op(PE, lambda e, bk=bk, mt=mt, pk=pks[mt]: e.matmul(bk[:, :], lhsT=ones_bf[:], rhs=pTx[pk][:], start=(mt == 0), stop=(mt == 1)),
                                          reads=[ones_bf_b, pTx_b[pks[mt]]], writes=[bb], inc=(mt == 1))
                                rk = cb % 2
                                sc.op(ACT, lambda e, bk=bk, rk=rk: e.activation(out=recx[rk][:], in_=bk[:, :], func=AF.Ln), reads=[bb], writes=[recx_b[rk]])
                                sc.op(ACT, lambda e, rk=rk: e.activation(out=recx[rk][:], in_=recx[rk][:], func=AF.Exp, scale=-1.0), reads=[recx_b[rk]], writes=[recx_b[rk]])
                                for c2 in range(2):
                                    bk, bb = bank()
                                    for mt in range(2):
                                        sc.op(PE, lambda e, bk=bk, mt=mt, c2=c2, hh=hh, pk=pks[mt]: e.matmul(bk[:, :], lhsT=vtokx[:, mt, hh * 256 + c2 * 128:hh * 256 + (c2 + 1) * 128], rhs=pTx[pk][:], start=(mt == 0), stop=(mt == 1)),
                                              reads=[vtokx_b, pTx_b[pks[mt]]], writes=[bb], inc=(mt == 1))
                                    sc.op(DVE, lambda e, bk=bk, rk=rk, c2=c2, hh=hh, cs=cs: e.tensor_tensor(out=catT[:, hh * 2 + c2, cs], in0=bk[:, :], in1=recx[rk][:], op=ALU.mult),
                                          reads=[bb, recx_b[rk]], writes=[catT_b[hh * 2 + c2]])
                        load_wbig(w_xo.rearrange("(kc p) n -> p kc n", p=128))
                        if stop_after == "E":
                            out_proj(8, catT_b)
                        else:
                            out_proj(8, catT_b, fused=norm_gen("ffn", srcH, g_ffn, xnT, xnT_b, NT, nscr))

                if stop_after not in ("D", "E"):
                    with contextlib.ExitStack() as esF:
                        sc.barrier()
                        def sbF(name, shape, dt):
                            return esF.enter_context(nc.sbuf_tensor("%s_s%d" % (name, si), list(shape), dt))
                        wup = [sbF("wup%d" % i, (128, 8, 256), BF16) for i in range(3)]
                        wupg_b = [Buf("wupg%d" % i) for i in range(3)]
                        wupu_b = [Buf("wupu%d" % i) for i in range(3)]
                        gpre = [sbF("gpre%d" % i, (128, S + 2), BF16) for i in range(2)]
                        gpre_b = [Buf("gpre%d" % i) for i in range(2)]
                        gact = [sbF("gact%d" % i, (128, S), BF16) for i in range(2)]
                        gact_b = [Buf("gact%d" % i) for i in range(2)]
                        dgf = [sbF("dgf%d" % i, (128, 3, 128), BF16) for i in range(2)]
                        dgf_b = [Buf("dgf%d" % i) for i in range(2)]
                        for i in range(2):
                            sc.op(POOL, lambda e, i=i: e.memset(gpre[i][:, 0:2], 0.0), writes=[gpre_b[i]])
                        otF = [sbF("otF%d" % i, (128, D), F32) for i in range(2)]
                        otF_b = [Buf("otF%d" % i) for i in range(2)]
                        gbF = sbF("gbF", (128, D), F32)
                        junkF = sbF("junkF", (128, D), BF16)

                        def final_gen():
                            sc.dma(SP, gbF[:], g_fin.partition_broadcast(128), writes=[gb_b])
                            for T in range(NT):
                                k = T % 2
                                yb = Buf("y%d_%d" % (si, T))
                                sc.op(ACT, lambda e, T=T: e.activation(out=junkF[:], in_=h_sb[:, T, :], func=AF.Square, accum_out=ss_t[:, T:T + 1]), reads=h_b[T], writes=[scr_b])
                                sc.op(ACT, lambda e, T=T: e.activation(out=sq_t[:, T:T + 1], in_=ss_t[:, T:T + 1], func=AF.Sqrt, scale=1.0 / D, bias=eps_t[:]), reads=[scr_b, CONST], writes=[scr_b])
                                sc.op(DVE, lambda e, T=T: e.reciprocal(out=rstd_t[:, T:T + 1], in_=sq_t[:, T:T + 1]), reads=[scr_b], writes=[scr_b2])
                                sc.op(DVE, lambda e, T=T, k=k: e.scalar_tensor_tensor(out=otF[k][:], in0=h_sb[:, T, :], scalar=rstd_t[:, T:T + 1], in1=gbF[:], op0=ALU.mult, op1=ALU.mult), reads=h_b[T] + [scr_b2, gb_b], writes=[otF_b[k]])
                                sc.dma(SP, y[si, T * 128:(T + 1) * 128, :], otF[k][:], reads=[otF_b[k]], writes=[yb])
                                outs_final.append(yb)
                                yield T
                        w_up_v = w_up.rearrange("(kc p) n -> p kc n", p=128)

                        def f_gate(fc):
                            k = fc % 2
                            w3 = fc % 3
                            sc.dma(POOL, wup[w3][:, :, 0:128], w_up_v[:, :, fc * 128:(fc + 1) * 128], writes=[wupg_b[w3]])
                            sc.dma(POOL, wup[w3][:, :, 128:256], w_up_v[:, :, DFF + fc * 128:DFF + (fc + 1) * 128], writes=[wupu_b[w3]])
                            for tap in range(3):
                                sc.op(DVE, lambda e, tap=tap: e.tensor_scalar(out=dgf[k][:, tap, :], in0=ident[:], scalar1=cwf[:, fc, tap:tap + 1], scalar2=None, op0=ALU.mult), reads=[CONST], writes=[dgf_b[k]])
                            for cb in range(4):
                                cs = slice(cb * 512, (cb + 1) * 512)
                                bk, bb = bank()
                                for kc in range(8):
                                    sc.op(PE, lambda e, bk=bk, kc=kc, cs=cs: e.matmul(bk[:, :], lhsT=wup[w3][:, kc, 0:128], rhs=xnT[:, kc, cs], start=(kc == 0), stop=(kc == 7)),
                                          reads=[wupg_b[w3]] + xnT_b[cb * 4:cb * 4 + 4], writes=[bb], inc=(kc == 7))
                                sc.op(ACT, lambda e, bk=bk, cb=cb: e.activation(out=gpre[k][:, 2 + cb * 512:2 + (cb + 1) * 512], in_=bk[:, :], func=AF.Copy), reads=[bb], writes=[gpre_b[k]])

                        def f_rest(fc, f0):
                            k = fc % 2
                            w3 = fc % 3
                            for cb in range(4):
                                cs = slice(cb * 512, (cb + 1) * 512)
                                bk, bb = bank()
                                for tap in range(3):
                                    sc.op(PE, lambda e, bk=bk, tap=tap, cb=cb: e.matmul(bk[:, :], lhsT=dgf[k][:, tap, :], rhs=gpre[k][:, 2 + cb * 512 - tap:2 + (cb + 1) * 512 - tap], start=(tap == 0), stop=(tap == 2)),
                                          reads=[dgf_b[k], gpre_b[k]], writes=[bb], inc=(tap == 2))
                                sc.op(ACT, lambda e, bk=bk, cs=cs: e.activation(out=gact[k][:, cs], in_=bk[:, :], func=AF.Silu, bias=cbf[:, fc:fc + 1], scale=1.0), reads=[bb, CONST], writes=[gact_b[k]])
                            for cb in range(4):
                                cs = slice(cb * 512, (cb + 1) * 512)
                                bk, bb = bank()
                                for kc in range(8):
                                    sc.op(PE, lambda e, bk=bk, kc=kc, cs=cs: e.matmul(bk[:, :], lhsT=wup[w3][:, kc, 128:256], rhs=xnT[:, kc, cs], start=(kc == 0), stop=(kc == 7)),
                                          reads=[wupu_b[w3]] + xnT_b[cb * 4:cb * 4 + 4], writes=[bb], inc=(kc == 7))
                                sc.op(DVE, lambda e, bk=bk, cs=cs: e.tensor_tensor(out=catT[:, fc - f0, cs], in0=bk[:, :], in1=gact[k][:, cs], op=ALU.mult), reads=[bb, gact_b[k]], writes=[catT_b[fc - f0]])

                        f_gate(0)
                        for (f0, f1) in ((0, 8), (8, 16), (16, 22)):
                            ng = f1 - f0
                            load_wbig(w_dn[f0 * 128:f1 * 128, :].rearrange("(fc p) n -> p fc n", p=128))
                            for fc in range(f0, f1):
                                if fc + 1 < NFF:
                                    f_gate(fc + 1)
                                f_rest(fc, f0)
                            if f1 == NFF and stop_after != "F":
                                out_proj(ng, catT_b[0:ng], fused=final_gen())
                            else:
                                out_proj(ng, catT_b[0:ng])

                if stop_after in ("D", "E", "F"):
                  with contextlib.ExitStack() as esG:
                    sc.barrier()
                    ot = [esG.enter_context(nc.sbuf_tensor("ot%d_s%d" % (i, si), [128, D], F32)) for i in range(2)]
                    ot_b = [Buf("ot%d" % i) for i in range(2)]
                    raw = stop_after in ("D", "E", "F")
                    gb1 = esG.enter_context(nc.sbuf_tensor("gb1G_s%d" % si, [128, D], F32))
                    junk = esG.enter_context(nc.sbuf_tensor("junkG_s%d" % si, [128, D], BF16))
                    if not raw:
                        sc.dma(SP, gb1[:], g_fin.partition_broadcast(128), writes=[gb_b])
                    for T in range(NT):
                        yb = Buf("y%d_%d" % (si, T))
                        if raw:
                            sc.dma(SP, y[si, T * 128:(T + 1) * 128, :], h_sb[:, T, :], reads=h_b[T], writes=[yb])
                        else:
                            k = T % 2
                            sc.op(ACT, lambda e, T=T: e.activation(out=junk[:], in_=h_sb[:, T, :], func=AF.Square, accum_out=ss_t[:, T:T + 1]), reads=h_b[T], writes=[scr_b])
                            sc.op(ACT, lambda e, T=T: e.activation(out=sq_t[:, T:T + 1], in_=ss_t[:, T:T + 1], func=AF.Sqrt, scale=1.0 / D, bias=eps_t[:]), reads=[scr_b, CONST], writes=[scr_b])
                            sc.op(DVE, lambda e, T=T: e.reciprocal(out=rstd_t[:, T:T + 1], in_=sq_t[:, T:T + 1]), reads=[scr_b], writes=[scr_b2])
                            sc.op(DVE, lambda e, T=T, k=k: e.scalar_tensor_tensor(out=ot[k][:], in0=h_sb[:, T, :], scalar=rstd_t[:, T:T + 1], in1=gb1[:], op0=ALU.mult, op1=ALU.mult), reads=h_b[T] + [scr_b2, gb_b], writes=[ot_b[k]])
                            sc.dma(SP, y[si, T * 128:(T + 1) * 128, :], ot[k][:], reads=[ot_b[k]], writes=[yb])
                        outs_final.append(yb)
            if stop_after in ("D", "E", "F"):
                break

        fin = []
        if dbg and stop_after in ("B", "C"):
            stg = sb("stg", (128, S), F32)
            stg_b = Buf("stg")
            for c in range(8):
                sc.op(DVE, lambda e, c=c: e.tensor_copy(out=stg[:], in_=catT[:, c, :]), reads=[catT_b[c]] + xnT_b, writes=[stg_b])
                db = Buf("dbgout%d" % c)
                sc.dma(SP, dbg_t[c * 128:(c + 1) * 128, :], stg[:], reads=[stg_b], writes=[db])
                fin.append(db)
        sc.final_wait(SP, fin + outs_final)
        stuck = sc.check_deadlock()
        if stuck:
            raise RuntimeError("semaphore deadlock detected at build time: %r" % (stuck,))
        for sem in sc.all_sems():
            nc.sync.sem_clear(sem)
        nc.all_engine_barrier()
        blk = es.enter_context(nc.Block())
        sc.emit(blk)
    return nc


def _consts():
    ident = np.eye(128, dtype=np.float32)
    slopes = np.exp2(-8.0 * np.arange(1, 9, dtype=np.float64) / 8.0)
    kj = np.arange(128)[:, None]
    qi = np.arange(128)[None, :]
    E = np.zeros((128, 4, 3, 512), np.float32)
    for h in range(8):
        for di, d in enumerate(DIL):
            relp = qi - kj + 128
            prev = np.where(relp <= 128, np.exp(np.minimum(-slopes[h] * d * relp, 0.0)), 0.0)
            relc = qi - kj
            cur = np.where(relc >= 0, np.exp(np.minimum(-slopes[h] * d * relc, 0.0)), 0.0)
            hp, hh = h // 2, h % 2
            E[:, hp, di, hh * 128:(hh + 1) * 128] = prev
            E[:, hp, di, 256 + hh * 128:256 + (hh + 1) * 128] = cur
    E = E.reshape(128, 8 * 3 * 256)
    s_ = np.arange(128)[:, None]
    t_ = np.arange(128)[None, :]
    cm = ((s_ // 64 == t_ // 64) & (s_ <= t_)).astype(np.float32)
    rm = np.ones((4, S), np.float32)
    rm[:, 0::64] = 0.0
    oh = np.zeros((4, 4, 128), np.float32)
    for j in range(4):
        oh[j, j, :] = 1.0
    return dict(c_ident=ident, c_E=E, c_cmask=cm, c_rmask=rm, c_onehot=oh.reshape(4, 512))


_W_NAMES = ["norm_mix_g", "w_in", "mlstm_conv_w", "mlstm_conv_b", "mlstm_gate_b", "mlstm_head_g", "w_mix_out",
            "norm_xattn_g", "norm_mem_g", "w_xq", "w_xkv", "w_xo", "norm_ffn_g", "w_ffn_up", "ffn_conv_w",
            "ffn_conv_b", "w_ffn_down"]


def make_in_maps(inputs, n_cores, nseq):
    consts = _consts()
    shared = {}
    for k in _W_NAMES:
        shared[k] = np.ascontiguousarray(np.asarray(inputs[k], dtype=np.float32)[0])
    shared["norm_final_g"] = np.ascontiguousarray(np.asarray(inputs["norm_final_g"], dtype=np.float32))
    shared.update(consts)
    xs = np.asarray(inputs["x"], dtype=np.float32)
    ms = np.asarray(inputs["mem"], dtype=np.float32)
    maps = []
    for c in range(n_cores):
        m = dict(shared)
        m["x"] = np.ascontiguousarray(xs[c * nseq:(c + 1) * nseq])
        m["mem"] = np.ascontiguousarray(ms[c * nseq:(c + 1) * nseq])
        maps.append(m)
    return maps


def kernel(**inputs):
    nseq = inputs["x"].shape[0] // N_CORES
    nc = build(nseq)
    maps = make_in_maps(inputs, N_CORES, nseq)
    res = run_bass_kernel_spmd(nc, maps, core_ids=list(range(N_CORES)))
    return np.concatenate([r["y"] for r in res.results], axis=0)
```
